# Optimizing a Trainium2 kernel written in Bass

```python
import math
import jax
import jax.numpy as jnp
from jax import lax
import numpy as np

D_MODEL = 4096
BATCH = 4
SEQ = 4096
DEPTH = 4

GRID_W = 64
CTX_LEN = 256
N_MIXERS = 4
W_GROUP = D_MODEL // N_MIXERS
MIX_W = N_MIXERS * W_GROUP
HEAD_DIM = 128
N_HEADS = W_GROUP // HEAD_DIM
GDN_CHUNK = 64
CONV_W = 4
NA_WIN_R = 8
NA_WIN_C = 16
DIFF_DH = HEAD_DIM // 2
ATTN_QBLOCK = 128
LRU_C = 8.0
ROPE_BASE = 10000.0
EPS = 1e-6
IN_SPLITS = (
    W_GROUP, W_GROUP, W_GROUP, W_GROUP,
    N_HEADS, N_HEADS, N_HEADS, N_HEADS,
    W_GROUP, W_GROUP, W_GROUP, W_GROUP,
    W_GROUP, W_GROUP, W_GROUP, W_GROUP,
    W_GROUP, W_GROUP,
)
IN_W = sum(IN_SPLITS)

kernel_name = "hybrid_dit_gdn_natten_diff_rglru"


def rms_norm(x, g):
    xf = x.astype(jnp.float32)
    y = xf * lax.rsqrt(jnp.mean(xf * xf, axis=-1, keepdims=True) + EPS)
    return (y * g.astype(jnp.float32)).astype(x.dtype)


def l2_norm(x):
    xf = x.astype(jnp.float32)
    return (xf * lax.rsqrt(jnp.sum(xf * xf, axis=-1, keepdims=True) + EPS)).astype(x.dtype)


def split_heads(x, d):
    return x.reshape(x.shape[:-1] + (x.shape[-1] // d, d))


def flip_t(a):
    return jnp.flip(a, axis=1)


def split_proj(p):
    idx = [int(i) for i in np.cumsum(IN_SPLITS)[:-1]]
    return jnp.split(p, idx, axis=-1)


def centred_dwconv(x, w):
    k = w.shape[0]
    left = k // 2
    return lax.conv_general_dilated(
        x, w[:, None, :].astype(x.dtype), window_strides=(1,),
        padding=[(left, k - 1 - left)], dimension_numbers=("NWC", "WIO", "NWC"),
        feature_group_count=x.shape[-1])


def axial_rope(n_tokens, dim):
    n_freq = dim // 4
    inv = ROPE_BASE ** (-jnp.arange(n_freq, dtype=jnp.float32) / n_freq)
    t = jnp.arange(n_tokens, dtype=jnp.int32)
    pos = jnp.stack([t // GRID_W, t % GRID_W], axis=-1).astype(jnp.float32)
    ang = pos[:, :, None] * inv
    return jnp.cos(ang), jnp.sin(ang)


def apply_rope(x, cos, sin):
    shp = x.shape
    f = shp[-1] // 4
    xr = x.reshape(shp[:-1] + (2, 2, f))
    bshape = (1, shp[1]) + (1,) * (len(shp) - 3) + (2, f)
    cs = cos.reshape(bshape).astype(x.dtype)
    sn = sin.reshape(bshape).astype(x.dtype)
    x1, x2 = xr[..., 0, :], xr[..., 1, :]
    out = jnp.stack([x1 * cs - x2 * sn, x2 * cs + x1 * sn], axis=-2)
    return out.reshape(shp)


def gated_delta_chunked(q, k, v, g, beta, s0):
    f32 = jnp.float32
    b, t, h, dk = q.shape
    dv = v.shape[-1]
    c = GDN_CHUNK
    n = t // c

    def to_chunks(a):
        return a.astype(f32).reshape(b, n, c, h, a.shape[-1]).transpose(1, 0, 3, 2, 4)

    qc = to_chunks(q) * dk ** -0.5
    kc = to_chunks(k)
    vc = to_chunks(v)
    gcum = jnp.cumsum(to_chunks(g[..., None])[..., 0], axis=-1)
    bc = to_chunks(beta[..., None])[..., 0]
    tril = jnp.tril(jnp.ones((c, c), dtype=bool))
    strict = jnp.tril(jnp.ones((c, c), dtype=bool), -1)
    gam = jnp.exp(jnp.where(tril, gcum[..., :, None] - gcum[..., None, :], -jnp.inf))
    kb = kc * bc[..., None]
    lower = jnp.where(strict, jnp.einsum("nbhid,nbhjd->nbhij", kb, kc) * gam, 0.0)
    a_mat = lower + jnp.eye(c, dtype=f32)
    rhs = jnp.concatenate([vc * bc[..., None], kb * jnp.exp(gcum)[..., None]], axis=-1)
    sol = lax.linalg.triangular_solve(a_mat, rhs, left_side=True, lower=True, unit_diagonal=True)
    u, w = sol[..., :dv], sol[..., dv:]
    intra = jnp.where(tril, jnp.einsum("nbhid,nbhjd->nbhij", qc, kc) * gam, 0.0)

    def step(s, xs):
        q_i, k_i, u_i, w_i, g_i, a_i = xs
        v_new = u_i - jnp.einsum("bhcd,bhde->bhce", w_i, s)
        o_i = (jnp.einsum("bhcd,bhde->bhce", q_i * jnp.exp(g_i)[..., None], s)
               + jnp.einsum("bhij,bhje->bhie", a_i, v_new))
        g_last = g_i[..., -1]
        s = (s * jnp.exp(g_last)[..., None, None]
             + jnp.einsum("bhcd,bhce->bhde", k_i * jnp.exp(g_last[..., None] - g_i)[..., None], v_new))
        return s, o_i

    s_fin, o = lax.scan(step, s0.astype(f32), (qc, kc, u, w, gcum, intra))
    o = o.transpose(1, 0, 3, 2, 4).reshape(b, t, h, dv)
    return o, s_fin


def gdn_prep(p, conv_w, a_log, dt_bias):
    q, k, v, z, a_f, a_b, b_f, b_b = p
    qkv = jax.nn.silu(centred_dwconv(jnp.concatenate([q, k, v], axis=-1), conv_w))
    q, k, v = jnp.split(qkv, 3, axis=-1)
    q = l2_norm(split_heads(q, HEAD_DIM))
    k = l2_norm(split_heads(k, HEAD_DIM))
    v = split_heads(v, HEAD_DIM)
    a_log = a_log.astype(jnp.float32)
    dt_bias = dt_bias.astype(jnp.float32)
    g = tuple(-jnp.exp(a_log[d]) * jax.nn.softplus(a.astype(jnp.float32) + dt_bias[d])
              for d, a in enumerate((a_f, a_b)))
    beta = tuple(jax.nn.sigmoid(bb.astype(jnp.float32)) for bb in (b_f, b_b))
    return q, k, v, g, beta, z


def gdn_mixer(p_lat, p_ctx, conv_w, a_log, dt_bias, norm_g, need_ctx):
    lat = gdn_prep(p_lat, conv_w, a_log, dt_bias)
    ctx = gdn_prep(p_ctx, conv_w, a_log, dt_bias)
    s_zero = jnp.zeros((lat[0].shape[0], N_HEADS, HEAD_DIM, HEAD_DIM), jnp.float32)

    def run(seq, d, s0):
        q, k, v, g, beta, _ = seq
        if d == 0:
            return gated_delta_chunked(q, k, v, g[0], beta[0], s0)
        o, s = gated_delta_chunked(flip_t(q), flip_t(k), flip_t(v), flip_t(g[1]), flip_t(beta[1]), s0)
        return flip_t(o), s

    oc_f, sc_f = run(ctx, 0, s_zero)
    oc_b, sc_b = run(ctx, 1, s_zero)
    ol_f, _ = run(lat, 0, sc_f)
    ol_b, _ = run(lat, 1, sc_b)

    def finish(o, z):
        y = rms_norm(o, norm_g) * jax.nn.silu(split_heads(z, HEAD_DIM).astype(jnp.float32))
        return y.reshape(z.shape).astype(z.dtype)

    o_lat = finish(ol_f + ol_b, lat[5])
    o_ctx = finish(oc_f + oc_b, ctx[5]) if need_ctx else None
    return o_lat, o_ctx


def na_mixer(p_lat, p_ctx, q_norm, k_norm, rpb, rows, need_ctx):
    def prep(p):
        q, k, v, z = p
        q = rms_norm(split_heads(q, HEAD_DIM), q_norm) * HEAD_DIM ** -0.5
        k = rms_norm(split_heads(k, HEAD_DIM), k_norm)
        return q, k, split_heads(v, HEAD_DIM), z

    q, k, v, z = prep(p_lat)
    qc, kc, vc, zc = prep(p_ctx)
    b, t, h, d = q.shape
    wr = min(NA_WIN_R, rows)
    wc = NA_WIN_C
    cols = np.arange(GRID_W)
    col_idx = np.clip(cols - wc // 2, 0, GRID_W - wc)[:, None] + np.arange(wc)[None, :]
    bias_cols = rpb[:, :, col_idx - cols[:, None] + wc - 1].astype(jnp.float32)
    kg = k.reshape(b, rows, GRID_W, h, d)
    vg = v.reshape(b, rows, GRID_W, h, d)
    qg = jnp.moveaxis(q.reshape(b, rows, GRID_W, h, d), 1, 0)

    def row_block(args):
        r, q_r = args
        r0 = jnp.clip(r - wr // 2, 0, rows - wr)
        k_w = lax.dynamic_slice_in_dim(kg, r0, wr, axis=1)[:, :, col_idx]
        v_w = lax.dynamic_slice_in_dim(vg, r0, wr, axis=1)[:, :, col_idx]
        dr = r0 + jnp.arange(wr) - r + NA_WIN_R - 1
        bias = jnp.take(bias_cols, dr, axis=1).transpose(0, 2, 1, 3)
        s_loc = jnp.einsum("bqhd,brqwhd->bhqrw", q_r, k_w).astype(jnp.float32) + bias[None]
        s_ctx = jnp.einsum("bqhd,bkhd->bhqk", q_r, kc).astype(jnp.float32)
        s = jnp.concatenate([s_loc.reshape(b, h, GRID_W, wr * wc), s_ctx], axis=-1)
        p = jax.nn.softmax(s, axis=-1).astype(v.dtype)
        p_loc = p[..., :wr * wc].reshape(b, h, GRID_W, wr, wc)
        return (jnp.einsum("bhqrw,brqwhd->bqhd", p_loc, v_w)
                + jnp.einsum("bhqk,bkhd->bqhd", p[..., wr * wc:], vc))

    o = lax.map(row_block, (jnp.arange(rows, dtype=jnp.int32), qg))
    o = jnp.moveaxis(o, 0, 1).reshape(b, t, h, d)

    def finish(o, z):
        y = o * jax.nn.silu(split_heads(z, HEAD_DIM))
        return y.reshape(z.shape).astype(z.dtype)

    o_lat = finish(o, z)
    o_ctx = None
    if need_ctx:
        sc = jnp.einsum("bqhd,bkhd->bhqk", qc, kc).astype(jnp.float32)
        pc = jax.nn.softmax(sc, axis=-1).astype(vc.dtype)
        o_ctx = finish(jnp.einsum("bhqk,bkhd->bqhd", pc, vc), zc)
    return o_lat, o_ctx


def diff_mixer(p_lat, p_ctx, q_norm, k_norm, lam_vec, subln, lam_init, cos, sin, need_ctx):
    def prep(p, rope):
        q, k, v, z = p
        shp = q.shape[:-1] + (N_HEADS, 2, DIFF_DH)
        q = rms_norm(q.reshape(shp), q_norm)
        k = rms_norm(k.reshape(shp), k_norm)
        if rope:
            q = apply_rope(q, cos, sin)
            k = apply_rope(k, cos, sin)
        return q * DIFF_DH ** -0.5, k, split_heads(v, HEAD_DIM), z

    q, k, v, z = prep(p_lat, True)
    qc, kc, vc, zc = prep(p_ctx, False)
    lv = lam_vec.astype(jnp.float32)
    lam = jnp.exp(jnp.sum(lv[0] * lv[1])) - jnp.exp(jnp.sum(lv[2] * lv[3])) + lam_init

    def attend(q_b, k_all, v_all):
        s = jnp.einsum("bqhmd,bkhmd->bhmqk", q_b, k_all).astype(jnp.float32)
        p = jax.nn.softmax(s, axis=-1)
        w = (p[:, :, 0] - lam * p[:, :, 1]).astype(v_all.dtype)
        return jnp.einsum("bhqk,bkhd->bqhd", w, v_all)

    def finish(o, z):
        y = (rms_norm(o, subln).astype(jnp.float32) * (1.0 - lam_init)
             * jax.nn.silu(split_heads(z, HEAD_DIM).astype(jnp.float32)))
        return y.reshape(z.shape).astype(z.dtype)

    b, t = q.shape[:2]
    k_all = jnp.concatenate([k, kc], axis=1)
    v_all = jnp.concatenate([v, vc], axis=1)
    q_blocks = jnp.moveaxis(q.reshape((b, t // ATTN_QBLOCK, ATTN_QBLOCK) + q.shape[2:]), 1, 0)
    o = lax.map(lambda q_b: attend(q_b, k_all, v_all), q_blocks)
    o = jnp.moveaxis(o, 0, 1).reshape(b, t, N_HEADS, HEAD_DIM)
    o_lat = finish(o, z)
    o_ctx = finish(attend(qc, kc, vc), zc) if need_ctx else None
    return o_lat, o_ctx


def rglru_coeffs(u, w_gate, b_gate, lam):
    uh = split_heads(u, W_GROUP // N_HEADS)
    gates = jnp.einsum("bthi,ghij->gbthj", uh, w_gate).reshape((2,) + u.shape) + b_gate[:, None, None, :]
    r = jax.nn.sigmoid(gates[0].astype(jnp.float32))
    i = jax.nn.sigmoid(gates[1].astype(jnp.float32))
    log_a = -LRU_C * r * jax.nn.softplus(-lam.astype(jnp.float32))
    a = jnp.exp(log_a)
    bx = jnp.sqrt(-jnp.expm1(2.0 * log_a)) * (i * u.astype(jnp.float32))
    return a, bx


def linear_scan(a, bx, h0):
    bx = bx.at[:, 0].add(a[:, 0] * h0)

    def comb(left, right):
        return right[0] * left[0], right[0] * left[1] + right[1]

    return lax.associative_scan(comb, (a, bx), axis=1)[1]


def directional_scan(a, bx, h0, reverse):
    if reverse:
        return flip_t(linear_scan(flip_t(a), flip_t(bx), h0))
    return linear_scan(a, bx, h0)


def lru_mixer(p_lat, p_ctx, conv_w, conv_b, w_gate, b_gate, lam, need_ctx):
    (xl, zl), (xc, zc) = p_lat, p_ctx
    ul = centred_dwconv(xl, conv_w) + conv_b
    uc = centred_dwconv(xc, conv_w) + conv_b
    h_zero = jnp.zeros((xl.shape[0], W_GROUP), jnp.float32)
    h_lat = jnp.zeros(ul.shape, jnp.float32)
    h_ctx = jnp.zeros(uc.shape, jnp.float32)
    for d in range(2):
        rev = d == 1
        a_c, b_c = rglru_coeffs(uc, w_gate[d], b_gate[d], lam[d])
        hc = directional_scan(a_c, b_c, h_zero, rev)
        h_end = hc[:, 0] if rev else hc[:, -1]
        a_l, b_l = rglru_coeffs(ul, w_gate[d], b_gate[d], lam[d])
        h_lat = h_lat + directional_scan(a_l, b_l, h_end, rev)
        h_ctx = h_ctx + hc
    o_lat = (h_lat * jax.nn.silu(zl.astype(jnp.float32))).astype(zl.dtype)
    o_ctx = (h_ctx * jax.nn.silu(zc.astype(jnp.float32))).astype(zc.dtype) if need_ctx else None
    return o_lat, o_ctx


def setup_inputs(seed: int = 0) -> dict:
    key = jax.random.key(seed)
    ks = jax.random.split(key, 26)
    f32 = jnp.float32
    bw = W_GROUP // N_HEADS

    def nrm(k, shape, s):
        return jax.random.normal(k, shape, f32) * s

    dt = jnp.exp(jax.random.uniform(ks[11], (DEPTH, 2, N_HEADS), f32, math.log(1e-3), math.log(1e-1)))
    u = jax.random.uniform(ks[24], (DEPTH, 2, W_GROUP), f32, 0.9, 0.999) ** (1.0 / LRU_C)
    return {
        "x": nrm(ks[0], (BATCH, SEQ, D_MODEL), 1.0),
        "c": nrm(ks[1], (BATCH, D_MODEL), 1.0),
        "ctx": nrm(ks[2], (BATCH, CTX_LEN, D_MODEL), 1.0),
        "c_ctx": nrm(ks[3], (D_MODEL,), 1.0),
        "w_mod": nrm(ks[4], (DEPTH, D_MODEL, 3 * D_MODEL), 0.5 * D_MODEL ** -0.5),
        "b_mod": nrm(ks[5], (DEPTH, 3 * D_MODEL), 0.02),
        "norm_g": 1.0 + nrm(ks[6], (DEPTH, D_MODEL), 0.02),
        "w_in": nrm(ks[7], (DEPTH, D_MODEL, IN_W), D_MODEL ** -0.5),
        "w_out": nrm(ks[8], (DEPTH, MIX_W, D_MODEL), MIX_W ** -0.5),
        "gdn_conv": nrm(ks[9], (DEPTH, CONV_W, 3 * W_GROUP), CONV_W ** -0.5),
        "gdn_a_log": jnp.log(jax.random.uniform(ks[10], (DEPTH, 2, N_HEADS), f32, 1.0, 16.0)),
        "gdn_dt_bias": dt + jnp.log(-jnp.expm1(-dt)),
        "gdn_norm_g": 1.0 + nrm(ks[12], (DEPTH, HEAD_DIM), 0.02),
        "na_q_norm": 1.0 + nrm(ks[13], (DEPTH, HEAD_DIM), 0.02),
        "na_k_norm": 1.0 + nrm(ks[14], (DEPTH, HEAD_DIM), 0.02),
        "na_rpb": nrm(ks[15], (DEPTH, N_HEADS, 2 * NA_WIN_R - 1, 2 * NA_WIN_C - 1), 0.1),
        "diff_q_norm": 1.0 + nrm(ks[16], (DEPTH, DIFF_DH), 0.02),
        "diff_k_norm": 1.0 + nrm(ks[17], (DEPTH, DIFF_DH), 0.02),
        "diff_lambda": nrm(ks[18], (DEPTH, 4, DIFF_DH), 0.1),
        "diff_subln": 1.0 + nrm(ks[19], (DEPTH, HEAD_DIM), 0.02),
        "lru_conv": nrm(ks[20], (DEPTH, CONV_W, W_GROUP), CONV_W ** -0.5),
        "lru_conv_b": nrm(ks[21], (DEPTH, W_GROUP), 0.02),
        "lru_w_gate": nrm(ks[22], (DEPTH, 2, 2, N_HEADS, bw, bw), bw ** -0.5),
        "lru_b_gate": nrm(ks[23], (DEPTH, 2, 2, W_GROUP), 0.02),
        "lru_lambda": jnp.log(u) - jnp.log1p(-u),
    }


def reference(x, c, ctx, c_ctx, w_mod, b_mod, norm_g, w_in, w_out, gdn_conv, gdn_a_log,
              gdn_dt_bias, gdn_norm_g, na_q_norm, na_k_norm, na_rpb, diff_q_norm, diff_k_norm,
              diff_lambda, diff_subln, lru_conv, lru_conv_b, lru_w_gate, lru_b_gate, lru_lambda):
    n_tok = x.shape[1]
    rows = n_tok // GRID_W
    cos, sin = axial_rope(n_tok, DIFF_DH)
    s_lat = jax.nn.silu(c)
    s_ctx = jax.nn.silu(c_ctx)
    cx = ctx
    for l in range(DEPTH):
        need_ctx = l < DEPTH - 1
        sh, sc, gt = jnp.split(s_lat @ w_mod[l] + b_mod[l], 3, axis=-1)
        shc, scc, gtc = jnp.split(s_ctx @ w_mod[l] + b_mod[l], 3, axis=-1)
        h = rms_norm(x, norm_g[l]) * (1.0 + sc[:, None]) + sh[:, None]
        hc = rms_norm(cx, norm_g[l]) * (1.0 + scc) + shc
        pl = split_proj(jnp.einsum("btd,de->bte", h, w_in[l]))
        pc = split_proj(jnp.einsum("btd,de->bte", hc, w_in[l]))
        oa, oa_c = gdn_mixer(pl[0:8], pc[0:8], gdn_conv[l], gdn_a_log[l], gdn_dt_bias[l],
                             gdn_norm_g[l], need_ctx)
        ob, ob_c = na_mixer(pl[8:12], pc[8:12], na_q_norm[l], na_k_norm[l], na_rpb[l], rows, need_ctx)
        lam_init = 0.8 - 0.6 * math.exp(-0.3 * l)
        oc, oc_c = diff_mixer(pl[12:16], pc[12:16], diff_q_norm[l], diff_k_norm[l], diff_lambda[l],
                              diff_subln[l], lam_init, cos, sin, need_ctx)
        od, od_c = lru_mixer(pl[16:18], pc[16:18], lru_conv[l], lru_conv_b[l], lru_w_gate[l],
                             lru_b_gate[l], lru_lambda[l], need_ctx)
        y = jnp.einsum("bte,ed->btd", jnp.concatenate([oa, ob, oc, od], axis=-1), w_out[l])
        x = x + gt[:, None] * y
        if need_ctx:
            yc = jnp.einsum("bte,ed->btd", jnp.concatenate([oa_c, ob_c, oc_c, od_c], axis=-1), w_out[l])
            cx = cx + gtc * yc
    return x
```

```python
import contextlib
import numpy as np
import concourse.bass as bass
import concourse.mybir as mybir
from concourse.bass_utils import run_bass_kernel_spmd

F32 = mybir.dt.float32
BF16 = mybir.dt.bfloat16
AF = mybir.ActivationFunctionType
ALU = mybir.AluOpType
AX = mybir.AxisListType

SELF_SYNC = ("dve", "act", "pool")


class Res:
    __slots__ = ("name", "w", "r", "dsem", "excl")

    def __init__(self, name):
        self.name = name
        self.excl = False
        self.w = {}
        self.r = {}
        self.dsem = None


class Tile:
    __slots__ = ("t", "res", "shape")

    def __init__(self, t, res, shape):
        self.t = t
        self.res = res
        self.shape = shape

    def __getitem__(self, idx):
        return self.t[idx]


class Prog:
    def __init__(self):
        self.nc = bass.Bass("TRN2", target_bir_lowering=False)
        nc = self.nc
        self.eng = dict(pe=nc.tensor, dve=nc.vector, act=nc.scalar, pool=nc.gpsimd, sp=nc.sync)
        self.st = contextlib.ExitStack()
        self.pst = None
        self.sem = {}
        self.cnt = {}
        self.seen = {k: {} for k in self.eng}
        for k in self.eng:
            self.sem[k] = self.st.enter_context(nc.semaphore("s_" + k))
            self.cnt[k] = 0
        self.ndsem = 0
        self.free_dsem = {}
        self.phase_dsem = []
        self.n_ins = 0
        self.n_wait = 0

    def dram(self, name, shape, dtype, kind="Internal"):
        return self.nc.dram_tensor(name, list(shape), dtype, kind=kind).ap()

    def tile(self, name, shape, dtype):
        st = self.pst if self.pst is not None else self.st
        self.n_tiles = getattr(self, "n_tiles", 0) + 1
        name = "%s_%d" % (name, self.n_tiles)
        t = st.enter_context(self.nc.sbuf_tensor(name, list(shape), dtype))
        return Tile(t, Res(name), shape)

    def psum(self, name, shape, dtype=F32):
        t = self.st.enter_context(self.nc.psum_tensor(name, list(shape), dtype))
        r = Res(name)
        r.excl = True
        return Tile(t, r, shape)

    def res(self, name):
        return Res(name)

    @contextlib.contextmanager
    def phase(self):
        assert self.pst is None
        self.pst = contextlib.ExitStack()
        self.phase_dsem = []
        try:
            yield
        finally:
            self.barrier()
            self.pst.close()
            self.pst = None
            for q_, k_ in self.phase_dsem:
                self.free_dsem.setdefault(q_, []).append(k_)
            self.phase_dsem = []

    def barrier(self):
        for e in self.eng:
            deps = {}
            for k, c in self.cnt.items():
                if c > 0 and k != e:
                    deps[k] = c
            self._wait(e, deps)

    @staticmethod
    def _r(x):
        return x.res if isinstance(x, Tile) else x

    def _deps(self, reads, writes, skip_waw_key=None, partial=False):
        deps = {}

        def add(k, c):
            if deps.get(k, 0) < c:
                deps[k] = c
        for r in reads:
            r = self._r(r)
            for k, c in r.w.items():
                add(k, c)
            if r.excl:
                for k, c in r.r.items():
                    add(k, c)
        for w in writes:
            w = self._r(w)
            if not partial:
                for k, c in w.w.items():
                    if skip_waw_key is not None and k == skip_waw_key and not w.r:
                        continue
                    add(k, c)
            for k, c in w.r.items():
                add(k, c)
        return deps

    def _wait(self, e, deps):
        seen = self.seen[e]
        for k, c in deps.items():
            if k == e and e not in SELF_SYNC:
                continue
            if seen.get(k, 0) >= c:
                continue
            self.eng[e].wait_ge(self.sem[k], c)
            self.n_wait += 1
            seen[k] = c

    def _mark(self, tok, reads, writes, partial=False):
        k, c = tok
        for r in reads:
            r = self._r(r)
            if r.r.get(k, 0) < c:
                r.r[k] = c
        for w in writes:
            w = self._r(w)
            if partial:
                w.w[k] = c
            else:
                w.w = {k: c}
                w.r = {}

    def op(self, e, fn, reads=(), writes=()):
        self._wait(e, self._deps(reads, writes))
        ins = fn(self.eng[e])
        self.cnt[e] += 1
        ins.then_inc(self.sem[e], 1)
        self._mark((e, self.cnt[e]), reads, writes)
        self.n_ins += 1
        return ins

    def dma(self, q, out, in_, reads=(), writes=(), owner=None, partial=False, **kw):
        owner = self._r(owner)
        if owner.dsem is None:
            owner.dsem = {}
        if q not in owner.dsem:
            fl = self.free_dsem.setdefault(q, [])
            if fl:
                key = fl.pop()
            else:
                key = "d%d" % self.ndsem
                self.ndsem += 1
                self.sem[key] = self.st.enter_context(self.nc.semaphore("s_" + key))
                self.cnt[key] = 0
            owner.dsem[q] = key
            if self.pst is not None:
                self.phase_dsem.append((q, key))
        key = owner.dsem[q]
        self._wait(q, self._deps(reads, writes, skip_waw_key=key, partial=partial))
        ins = self.eng[q].dma_start(out=out, in_=in_, **kw)
        self.cnt[key] += 16
        ins.then_inc(self.sem[key], 16)
        self._mark((key, self.cnt[key]), reads, writes, partial=partial)
        self.n_ins += 1
        return ins

    def mm(self, out, lhsT, rhs, start, stop, R, W):
        return self.op("pe", lambda e: e.matmul(out, lhsT=lhsT, rhs=rhs, start=start, stop=stop), reads=R, writes=W)

    def act(self, out, in_, func, R, W, **kw):
        return self.op("act", lambda e: e.activation(out=out, in_=in_, func=func, **kw), reads=R, writes=W)

    def tt(self, eng, out, in0, in1, op, R, W):
        return self.op(eng, lambda e: e.tensor_tensor(out=out, in0=in0, in1=in1, op=op), reads=R, writes=W)

    def ts(self, eng, out, in0, s1, s2, op0, op1, R, W):
        if s2 is None:
            return self.op(eng, lambda e: e.tensor_scalar(out=out, in0=in0, scalar1=s1, scalar2=None, op0=op0), reads=R, writes=W)
        return self.op(eng, lambda e: e.tensor_scalar(out=out, in0=in0, scalar1=s1, scalar2=s2, op0=op0, op1=op1), reads=R, writes=W)

    def stt(self, eng, out, in0, scalar, in1, op0, op1, R, W):
        return self.op(eng, lambda e: e.scalar_tensor_tensor(out=out, in0=in0, scalar=scalar, in1=in1, op0=op0, op1=op1), reads=R, writes=W)

    def copy(self, eng, out, in_, R, W):
        if eng == "act":
            return self.op("act", lambda e: e.activation(out=out, in_=in_, func=AF.Identity), reads=R, writes=W)
        return self.op(eng, lambda e: e.tensor_copy(out=out, in_=in_), reads=R, writes=W)

    def finish(self):
        deps = {k: c for k, c in self.cnt.items() if c > 0 and k != "sp"}
        self._wait("sp", deps)
        self.st.close()
        return self.nc


import ml_dtypes

NT = 4352
NCTX = 256
NLAT = 4096
EPS = 1e-6
TT = [(0, 256)] + [(256 + 512 * i, 512) for i in range(8)]
COMPS = ["Aq", "Ak", "Av", "Az", "As", "Bq", "Bk", "Bv", "Bz", "Cq", "Ck", "Cv", "Cz", "Dx", "Dz"]
CW = {c: (16 if c == "As" else 512) for c in COMPS}
COFF = {}
_o = 0
for _c in COMPS:
    COFF[_c] = _o
    _o += CW[_c]
NCOLS = _o
FM = ["Aq", "Ak", "Av", "Az", "Bq", "Bk", "Bz", "Cq", "Ck", "Cz", "Dx", "Dz"]
FMI = {c: i for i, c in enumerate(FM)}
TM = ["Bv", "Cv"]
TMI = {c: i for i, c in enumerate(TM)}


def alloc_psum(P):
    return [P.psum("psb%d" % i, [128, 512]) for i in range(8)]


def phase_norm(P, PS, xin, normg, modv, ident_d, hT, hT_res):
    with P.phase():
        ident = P.tile("ident", [128, 128], F32)
        P.dma("sp", ident[:, :], ident_d, writes=[ident], owner=ident)
        ng = P.tile("ng", [128, 32], F32)
        mv = P.tile("mv", [128, 6 * 32], F32)
        P.dma("sp", ng[:, :], normg, writes=[ng], owner=ng)
        P.dma("sp", mv[:, :], modv, writes=[mv], owner=mv)
        AB = P.tile("AB", [128, 4 * 32], F32)
        for j, (sci, shi) in enumerate(((4, 3), (1, 0))):
            P.ts("dve", AB[:, (2 * j) * 32:(2 * j + 1) * 32], mv[:, sci * 32:(sci + 1) * 32], 1.0, None, ALU.add, None, [mv], [AB])
            P.tt("dve", AB[:, (2 * j) * 32:(2 * j + 1) * 32], AB[:, (2 * j) * 32:(2 * j + 1) * 32], ng[:, :], ALU.mult, [AB, ng], [AB])
            P.copy("dve", AB[:, (2 * j + 1) * 32:(2 * j + 2) * 32], mv[:, shi * 32:(shi + 1) * 32], [mv], [AB])
        xt = [P.tile("xt%d" % i, [128, 4096], F32) for i in range(2)]
        sq = P.tile("sq", [128, 4096], BF16)
        ss = [P.tile("ss%d" % i, [128, 2], F32) for i in range(2)]
        dg = [P.tile("dg%d" % i, [128, 128], F32) for i in range(2)]
        ht = [P.tile("ht%d" % i, [128, 32, 128], BF16) for i in range(2)]
        for i in range(34):
            x = xt[i % 2]
            s = ss[i % 2]
            d = dg[i % 2]
            h = ht[i % 2]
            P.dma("sp" if i % 2 == 0 else "pool", x[:, :], xin[i * 128:(i + 1) * 128, :], writes=[x], owner=x)
            P.act(sq[:, :], x[:, :], AF.Square, [x], [sq])
            P.op("dve", lambda e: e.reduce_sum(out=s[:, 0:1], in_=sq[:, :], axis=AX.X), reads=[sq], writes=[s])
            P.act(s[:, 1:2], s[:, 0:1], AF.Sqrt, [s], [s], scale=1.0 / 4096, bias=EPS)
            P.op("dve", lambda e: e.reciprocal(out=s[:, 1:2], in_=s[:, 1:2]), reads=[s], writes=[s])
            P.ts("dve", d[:, :], ident[:, :], s[:, 1:2], None, ALU.mult, None, [ident, s], [d])
            j = 0 if i < 2 else 1
            for cg in range(8):
                ps = PS[cg % 8]
                for cc in range(4):
                    c = cg * 4 + cc
                    P.mm(ps[:, cc * 128:(cc + 1) * 128], x[:, c * 128:(c + 1) * 128], d[:, :], True, True, [x, d], [ps])
                for cc in range(4):
                    c = cg * 4 + cc
                    A = AB[:, (2 * j) * 32 + c:(2 * j) * 32 + c + 1]
                    Bv = AB[:, (2 * j + 1) * 32 + c:(2 * j + 1) * 32 + c + 1]
                    if cg % 2 == 0:
                        P.act(h[:, c, :], ps[:, cc * 128:(cc + 1) * 128], AF.Identity, [ps, AB], [h], scale=A, bias=Bv)
                    else:
                        P.ts("dve", h[:, c, :], ps[:, cc * 128:(cc + 1) * 128], A, Bv, ALU.mult, ALU.add, [ps, AB], [h])
            P.dma("pool" if i % 2 == 0 else "sp", hT[:, :, i * 128:(i + 1) * 128].rearrange("c p t -> p c t"), h[:, :, :],
                  reads=[h], writes=[hT_res[i]], owner=h)


def phase_inproj(P, PS, w_in, hT, hT_res, pT, pT_res, pv, pv_res, psm, psm_res, comps=None):
    wv = w_in.rearrange("(c p) n -> p c n", p=128)
    with P.phase():
        stg = [P.tile("stg%d" % i, [128, 8, 512], F32) for i in range(2)]
        wb = [P.tile("wb%d" % i, [128, 32, 512], BF16) for i in range(2)]
        hts = [P.tile("hts%d" % i, [128, 32, 512], BF16) for i in range(2)]
        ot = [P.tile("ot%d" % i, [128, 512], F32) for i in range(4)]
        ob = [P.tile("ob%d" % i, [128, 512], BF16) for i in range(2)]
        nst = 0
        nht = 0
        no = 0
        nob = 0
        nps = 0
        for ci, comp in enumerate(comps or COMPS):
            w = wb[ci % 2]
            cw = CW[comp]
            for g in range(4):
                s = stg[nst % 2]
                nst += 1
                P.dma("sp", s[:, :, 0:cw], wv[:, g * 8:(g + 1) * 8, COFF[comp]:COFF[comp] + cw], writes=[s], owner=s)
                P.copy("pool" if g % 2 == 0 else "dve", w[:, g * 8:(g + 1) * 8, 0:cw], s[:, :, 0:cw], [s], [w])
            for ti, (t0, nt) in enumerate(TT):
                h = hts[nht % 2]
                nht += 1
                P.dma("pool", h[:, :, 0:nt], hT[:, :, t0:t0 + nt].rearrange("c p t -> p c t"),
                      reads=[hT_res[t0 // 128 + k] for k in range(nt // 128)], writes=[h], owner=h)
                if comp in FMI:
                    for hd in range(4):
                        ps = PS[nps % 8]
                        nps += 1
                        for c in range(32):
                            P.mm(ps[:, 0:nt], w[:, c, hd * 128:(hd + 1) * 128], h[:, c, 0:nt], c == 0, c == 31, [w, h], [ps])
                        o = ot[no % 4]
                        no += 1
                        if comp.endswith("z"):
                            P.act(o[:, 0:nt], ps[:, 0:nt], AF.Silu, [ps], [o])
                        elif no % 2 == 0:
                            P.copy("act", o[:, 0:nt], ps[:, 0:nt], [ps], [o])
                        else:
                            P.copy("dve", o[:, 0:nt], ps[:, 0:nt], [ps], [o])
                        P.dma("sp", pT[FMI[comp], hd, :, t0:t0 + nt], o[:, 0:nt], reads=[o], writes=[pT_res[FMI[comp]][hd]],
                              owner=o, partial=True)
                else:
                    for sidx in range(nt // 128):
                        ps = PS[nps % 8]
                        nps += 1
                        for c in range(32):
                            P.mm(ps[:, 0:cw], h[:, c, sidx * 128:(sidx + 1) * 128], w[:, c, 0:cw], c == 0, c == 31, [w, h], [ps])
                        r0 = t0 + sidx * 128
                        if comp == "As":
                            o = ot[no % 4]
                            no += 1
                            P.copy("dve", o[:, 0:16], ps[:, 0:16], [ps], [o])
                            P.dma("sp", psm[r0:r0 + 128, :], o[:, 0:16], reads=[o], writes=[psm_res], owner=o, partial=True)
                        else:
                            o = ob[nob % 2]
                            nob += 1
                            P.copy("act" if nob % 2 else "dve", o[:, :], ps[:, :], [ps], [o])
                            P.dma("sp", pv[TMI[comp], r0:r0 + 128, :], o[:, :], reads=[o], writes=[pv_res[TMI[comp]]], owner=o, partial=True)


def phase_lru(P, PS, pT, pT_res, oT, oT_res, lru_conv, lru_conv_b, lru_wg, lru_bg, lru_lam):
    with P.phase():
        cw = P.tile("cw", [128, 16], F32)
        cb = P.tile("cb", [128, 4], F32)
        bg = P.tile("bg", [128, 16], F32)
        lam = P.tile("lam", [128, 8], F32)
        c1 = P.tile("c1", [128, 8], F32)
        P.dma("sp", cw[:, :], lru_conv, writes=[cw], owner=cw)
        P.dma("sp", cb[:, :], lru_conv_b, writes=[cb], owner=cb)
        P.dma("sp", bg[:, :], lru_bg, writes=[bg], owner=bg)
        P.dma("sp", lam[:, :], lru_lam, writes=[lam], owner=lam)
        P.act(c1[:, :], lam[:, :], AF.Exp, [lam], [c1], scale=-1.0)
        P.act(c1[:, :], c1[:, :], AF.Ln, [c1], [c1], bias=1.0)
        P.ts("dve", c1[:, :], c1[:, :], -8.0, None, ALU.mult, None, [c1], [c1])
        wgs = P.tile("wgs", [128, 128], F32)
        wgb = [P.tile("wgb%d" % i, [128, 128], BF16) for i in range(4)]
        x = P.tile("x", [128, NT], F32)
        z = P.tile("z", [128, NT], F32)
        u = P.tile("u", [128, NT], F32)
        ub = P.tile("ub", [128, NT], BF16)
        r = P.tile("r", [128, NT], F32)
        ig = P.tile("ig", [128, NT], F32)
        hs = P.tile("hs", [128, NT], F32)
        hsum = P.tile("hsum", [128, NT], F32)
        ob = P.tile("ob", [128, NT], BF16)
        segs = [(0, NCTX), (NCTX, NT)]
        nps = 0
        for hd in range(4):
            P.dma("sp", x[:, :], pT[FMI["Dx"], hd], reads=[pT_res[FMI["Dx"]][hd]], writes=[x], owner=x)
            P.dma("pool", z[:, :], pT[FMI["Dz"], hd], reads=[pT_res[FMI["Dz"]][hd]], writes=[z], owner=z)
            P.act(u[:, :], x[:, :], AF.Identity, [x, cw, cb], [u], scale=cw[:, hd * 4 + 2:hd * 4 + 3], bias=cb[:, hd:hd + 1])
            for (s, e) in segs:
                P.stt("dve", u[:, s + 2:e], x[:, s:e - 2], cw[:, hd * 4 + 0:hd * 4 + 1], u[:, s + 2:e], ALU.mult, ALU.add, [x, cw, u], [u])
                P.stt("dve", u[:, s + 1:e], x[:, s:e - 1], cw[:, hd * 4 + 1:hd * 4 + 2], u[:, s + 1:e], ALU.mult, ALU.add, [x, cw, u], [u])
                P.stt("dve", u[:, s:e - 1], x[:, s + 1:e], cw[:, hd * 4 + 3:hd * 4 + 4], u[:, s:e - 1], ALU.mult, ALU.add, [x, cw, u], [u])
            P.copy("pool", ub[:, :], u[:, :], [u], [ub])
            for d in range(2):
                for g in range(2):
                    P.dma("sp", wgs[:, :], lru_wg[d, g, hd], writes=[wgs], owner=wgs)
                    P.copy("dve", wgb[d * 2 + g][:, :], wgs[:, :], [wgs], [wgb[d * 2 + g]])
            for d in range(2):
                col = d * 4 + hd
                for g in range(2):
                    dst = r if g == 0 else ig
                    bcol = (d * 2 + g) * 4 + hd
                    for (t0, nt) in TT:
                        ps = PS[nps % 8]
                        nps += 1
                        P.mm(ps[:, 0:nt], wgb[d * 2 + g][:, :], ub[:, t0:t0 + nt], True, True, [wgb[d * 2 + g], ub], [ps])
                        P.act(dst[:, t0:t0 + nt], ps[:, 0:nt], AF.Sigmoid, [ps, bg], [dst], bias=bg[:, bcol:bcol + 1])
                P.act(r[:, :], r[:, :], AF.Exp, [r, c1], [r], scale=c1[:, col:col + 1])
                P.tt("dve", ig[:, :], ig[:, :], u[:, :], ALU.mult, [ig, u], [ig])
                P.tt("pool", hs[:, :], r[:, :], r[:, :], ALU.mult, [r], [hs])
                P.ts("dve", hs[:, :], hs[:, :], 1.0, None, ALU.min, None, [hs], [hs])
                P.act(hs[:, :], hs[:, :], AF.Sqrt, [hs], [hs], scale=-1.0, bias=1.0)
                P.tt("dve", ig[:, :], ig[:, :], hs[:, :], ALU.mult, [ig, hs], [ig])
                if d == 0:
                    P.op("dve", lambda e: e.tensor_tensor_scan(out=hsum[:, :], data0=r[:, :], data1=ig[:, :], initial=0.0, op0=ALU.mult, op1=ALU.add),
                         reads=[r, ig], writes=[hsum])
                else:
                    P.op("dve", lambda e: e.tensor_tensor_scan(out=hs[:, 0:NCTX][:, ::-1], data0=r[:, 0:NCTX][:, ::-1], data1=ig[:, 0:NCTX][:, ::-1],
                                                               initial=0.0, op0=ALU.mult, op1=ALU.add), reads=[r, ig], writes=[hs])
                    P.op("dve", lambda e: e.tensor_tensor_scan(out=hs[:, NCTX:NT][:, ::-1], data0=r[:, NCTX:NT][:, ::-1], data1=ig[:, NCTX:NT][:, ::-1],
                                                               initial=hs[:, 0:1], op0=ALU.mult, op1=ALU.add), reads=[r, ig, hs], writes=[hs])
                    P.tt("dve", hsum[:, :], hsum[:, :], hs[:, :], ALU.add, [hsum, hs], [hsum])
            P.tt("dve", ob[:, :], hsum[:, :], z[:, :], ALU.mult, [hsum, z], [ob])
            P.dma("sp", oT[12 + hd], ob[:, :], reads=[ob], writes=[oT_res[12 + hd]], owner=ob)


def phase_outproj(P, PS, oTf, oTf_res, w_out, xh, gtl, gtc, xo, nctx_tiles, xo_map=None):
    ntok = xh.shape[0]
    ntiles = ntok // 128
    wv = w_out.rearrange("(c p) n -> p c n", p=128)
    oreads = list(oTf_res) if isinstance(oTf_res, (list, tuple)) else [oTf_res]
    with P.phase():
        gt = [P.tile("gt%d" % i, [128, 4096], F32) for i in range(2)]
        P.dma("sp", gt[0][:, :], gtc.broadcast_to([128, 4096]), writes=[gt[0]], owner=gt[0])
        P.dma("sp", gt[1][:, :], gtl.broadcast_to([128, 4096]), writes=[gt[1]], owner=gt[1])
        stg = [P.tile("stg%d" % i, [128, 8, 512], F32) for i in range(2)]
        wb = [P.tile("wb%d" % i, [128, 32, 512], BF16) for i in range(2)]
        ot = [P.tile("ot%d" % i, [128, 32, 128], BF16) for i in range(3)]
        xt = [P.tile("xt%d" % i, [128, 512], F32) for i in range(3)]
        yt = [P.tile("yt%d" % i, [128, 512], F32) for i in range(3)]
        nst = 0
        n = 0
        for cb in range(8):
            w = wb[cb % 2]
            for g in range(4):
                s = stg[nst % 2]
                nst += 1
                P.dma("sp", s[:, :, :], wv[:, g * 8:(g + 1) * 8, cb * 512:(cb + 1) * 512], writes=[s], owner=s)
                P.copy("pool" if g % 2 == 0 else "act", w[:, g * 8:(g + 1) * 8, :], s[:, :, :], [s], [w])
            for ti in range(ntiles):
                dst = xo_map(ti) if xo_map is not None else xo[ti * 128:(ti + 1) * 128, :]
                if dst is None:
                    continue
                o = ot[n % 3]
                x = xt[n % 3]
                y = yt[n % 3]
                ps = PS[n % 8]
                n += 1
                P.dma("pool", o[:, :, :], oTf[:, :, ti * 128:(ti + 1) * 128].rearrange("c p t -> p c t"), reads=oreads, writes=[o], owner=o)
                P.dma("pool", x[:, :], xh[ti * 128:(ti + 1) * 128, cb * 512:(cb + 1) * 512], writes=[x], owner=x)
                for c in range(32):
                    P.mm(ps[:, :], o[:, c, :], w[:, c, :], c == 0, c == 31, [o, w], [ps])
                g_ = gt[0] if ti < nctx_tiles else gt[1]
                P.tt("dve", y[:, :], ps[:, :], g_[:, cb * 512:(cb + 1) * 512], ALU.mult, [ps, g_], [y])
                P.tt("dve", y[:, :], y[:, :], x[:, :], ALU.add, [y, x], [y])
                P.dma("sp", dst[:, cb * 512:(cb + 1) * 512], y[:, :], reads=[y], owner=y)


NEG = -30000.0
GRID_W = 64


def rmsnorm_fm(P, PS, nps, src, dst_f32, ones_t, gcol, ndim, tmp_sq, tmp_r, t0, nt, gt=()):
    ps = PS[nps % 8]
    P.act(tmp_sq[:, 0:nt], src[:, t0:t0 + nt], AF.Square, [src], [tmp_sq])
    P.mm(ps[:, 0:nt], ones_t[:, :], tmp_sq[:, 0:nt], True, True, [ones_t, tmp_sq], [ps])
    P.act(tmp_r[:, 0:nt], ps[:, 0:nt], AF.Sqrt, [ps], [tmp_r], scale=1.0 / ndim, bias=EPS)
    P.op("dve", lambda e: e.reciprocal(out=tmp_r[:, 0:nt], in_=tmp_r[:, 0:nt]), reads=[tmp_r], writes=[tmp_r])
    P.stt("dve", dst_f32[:, t0:t0 + nt], src[:, t0:t0 + nt], gcol, tmp_r[:, 0:nt], ALU.mult, ALU.mult, [src, tmp_r] + list(gt), [dst_f32])


def phase_diff(P, PS, pT, pT_res, pv, pv_res, oT, oT_res, C, dq_g, dk_g, dlam, dsub, laminit):
    with P.phase():
        blk = P.tile("blk", [128, 128], F32)
        onesf = P.tile("onesf", [128, 128], F32)
        onesb = P.tile("onesb", [128, 128], BF16)
        rot = P.tile("rot", [128, 128], F32)
        cos = P.tile("cos", [128, NT], F32)
        sin = P.tile("sin", [128, NT], F32)
        for t, nm in ((blk, "blk64"), (onesf, "ones"), (rot, "rot"), (cos, "cos"), (sin, "sin")):
            P.dma("sp", t[:, :], C[nm], writes=[t], owner=t)
        P.copy("dve", onesb[:, :], onesf[:, :], [onesf], [onesb])
        gq = P.tile("gq", [128, 1], F32)
        gk = P.tile("gk", [128, 1], F32)
        sub = P.tile("sub", [128, 1], F32)
        P.dma("sp", gq[:, :], dq_g, writes=[gq], owner=gq)
        P.dma("sp", gk[:, :], dk_g, writes=[gk], owner=gk)
        P.dma("sp", sub[:, :], dsub, writes=[sub], owner=sub)
        P.ts("dve", gq[:, :], gq[:, :], 0.125, None, ALU.mult, None, [gq], [gq])
        li = P.tile("li", [128, 2], F32)
        P.dma("sp", li[:, :], laminit, writes=[li], owner=li)
        P.ts("dve", sub[:, :], sub[:, :], li[:, 1:2], None, ALU.mult, None, [sub, li], [sub])
        lv = P.tile("lv", [128, 256], F32)
        lt = P.tile("lt", [128, 4], F32)
        P.dma("sp", lv[:, :], dlam.broadcast_to([128, 256]), writes=[lv], owner=lv)
        P.tt("dve", lv[:, 0:64], lv[:, 0:64], lv[:, 64:128], ALU.mult, [lv], [lv])
        P.tt("dve", lv[:, 128:192], lv[:, 128:192], lv[:, 192:256], ALU.mult, [lv], [lv])
        P.op("dve", lambda e: e.reduce_sum(out=lt[:, 0:1], in_=lv[:, 0:64], axis=AX.X), reads=[lv], writes=[lt])
        P.op("dve", lambda e: e.reduce_sum(out=lt[:, 1:2], in_=lv[:, 128:192], axis=AX.X), reads=[lv], writes=[lt])
        P.act(lt[:, 0:2], lt[:, 0:2], AF.Exp, [lt], [lt])
        P.tt("dve", lt[:, 2:3], lt[:, 1:2], lt[:, 0:1], ALU.subtract, [lt], [lt])
        P.ts("dve", lt[:, 3:4], lt[:, 2:3], li[:, 0:1], None, ALU.add, None, [lt, li], [lt])
        lamneg = lt[:, 3:4]

        xq = P.tile("xq", [128, NT], F32)
        xk = P.tile("xk", [128, NT], F32)
        z = P.tile("z", [128, NT], F32)
        qb = P.tile("qb", [128, NT], BF16)
        kb = P.tile("kb", [128, NT], BF16)
        V = P.tile("V", [128, 34, 128], BF16)
        tsq = P.tile("tsq", [128, 512], F32)
        tr = P.tile("tr", [128, 512], F32)
        t1 = P.tile("t1", [128, 512], F32)
        t2 = P.tile("t2", [128, 512], F32)
        pt = [P.tile("pt%d" % i, [128, 2, 512], BF16) for i in range(3)]
        r0 = P.tile("r0", [128, 512], F32)
        r1 = P.tile("r1", [128, 512], F32)
        o0 = P.tile("o0", [128, 512], F32)
        o1 = P.tile("o1", [128, 512], F32)
        obt = [P.tile("obt%d" % i, [128, 512], BF16) for i in range(2)]
        nps = 0
        npt = 0
        nob = 0
        for hd in range(4):
            P.dma("sp", xq[:, :], pT[FMI["Cq"], hd], reads=[pT_res[FMI["Cq"]][hd]], writes=[xq], owner=xq)
            P.dma("pool", xk[:, :], pT[FMI["Ck"], hd], reads=[pT_res[FMI["Ck"]][hd]], writes=[xk], owner=xk)
            P.dma("sp", z[:, :], pT[FMI["Cz"], hd], reads=[pT_res[FMI["Cz"]][hd]], writes=[z], owner=z)
            P.dma("pool", V[:, :, :], pv[TMI["Cv"], :, hd * 128:(hd + 1) * 128].rearrange("(c p) d -> p c d", p=128),
                  reads=[pv_res[TMI["Cv"]]], writes=[V], owner=V)
            for (src, g, dst) in ((xq, gq, qb), (xk, gk, kb)):
                for (t0, nt) in TT:
                    rmsnorm_fm(P, PS, nps, src, src, blk, g[:, 0:1], 64, tsq, tr, t0, nt, gt=[g])
                    nps += 1
                    ps = PS[nps % 8]
                    nps += 1
                    P.mm(ps[:, 0:nt], rot[:, :], src[:, t0:t0 + nt], True, True, [rot, src], [ps])
                    P.tt("pool", t1[:, 0:nt], src[:, t0:t0 + nt], cos[:, t0:t0 + nt], ALU.mult, [src, cos], [t1])
                    P.tt("dve", t2[:, 0:nt], ps[:, 0:nt], sin[:, t0:t0 + nt], ALU.mult, [ps, sin], [t2])
                    P.tt("dve", dst[:, t0:t0 + nt], t1[:, 0:nt], t2[:, 0:nt], ALU.add, [t1, t2], [dst])
            for (q0, nq) in TT:
                chunks = [0, 1] if q0 == 0 else list(range(34))
                O0, O1, S0, S1 = PS[4], PS[5], PS[6], PS[7]
                for ci, kc in enumerate(chunks):
                    a = (ci % 2) * 2
                    p_ = pt[npt % 3]
                    npt += 1
                    for m in range(2):
                        P.mm(PS[a + m][:, 0:nq], kb[64 * m:64 * m + 64, kc * 128:(kc + 1) * 128], qb[64 * m:64 * m + 64, q0:q0 + nq],
                             True, True, [kb, qb], [PS[a + m]])
                    for m in range(2):
                        P.act(p_[:, m, 0:nq], PS[a + m][:, 0:nq], AF.Exp, [PS[a + m]], [p_])
                    st = ci == 0
                    sp_ = ci == len(chunks) - 1
                    P.mm(O0[:, 0:nq], V[:, kc, :], p_[:, 0, 0:nq], st, sp_, [V, p_], [O0])
                    P.mm(O1[:, 0:nq], V[:, kc, :], p_[:, 1, 0:nq], st, sp_, [V, p_], [O1])
                    P.mm(S0[:, 0:nq], onesb[:, :], p_[:, 0, 0:nq], st, sp_, [onesb, p_], [S0])
                    P.mm(S1[:, 0:nq], onesb[:, :], p_[:, 1, 0:nq], st, sp_, [onesb, p_], [S1])
                P.op("dve", lambda e: e.reciprocal(out=r0[:, 0:nq], in_=S0[:, 0:nq]), reads=[S0], writes=[r0])
                P.op("dve", lambda e: e.reciprocal(out=r1[:, 0:nq], in_=S1[:, 0:nq]), reads=[S1], writes=[r1])
                P.tt("dve", o0[:, 0:nq], O0[:, 0:nq], r0[:, 0:nq], ALU.mult, [O0, r0], [o0])
                P.stt("dve", o1[:, 0:nq], O1[:, 0:nq], lamneg, r1[:, 0:nq], ALU.mult, ALU.mult, [O1, r1, lt], [o1])
                P.tt("dve", o0[:, 0:nq], o0[:, 0:nq], o1[:, 0:nq], ALU.add, [o0, o1], [o0])
                rmsnorm_fm(P, PS, 0, o0, o0, onesf, sub[:, 0:1], 128, tsq, tr, 0, nq, gt=[sub])
                ob = obt[nob % 2]
                nob += 1
                P.tt("dve", ob[:, 0:nq], o0[:, 0:nq], z[:, q0:q0 + nq], ALU.mult, [o0, z], [ob])
                P.dma("sp", oT[8 + hd][:, q0:q0 + nq], ob[:, 0:nq], reads=[ob], writes=[oT_res[8 + hd]], owner=ob, partial=True)


def na_tables():
    rows = 64
    def r0(r):
        return min(max(r - 4, 0), rows - 8)
    def c0(c):
        return min(max(c - 8, 0), GRID_W - 16)
    cfgs = {}
    pairs = []
    idx_dr, idx_dc, masks = [], [], []
    for pr in range(32):
        lo = min(r0(2 * pr), r0(2 * pr + 1))
        hi = max(r0(2 * pr), r0(2 * pr + 1)) + 8
        chunks = list(range(lo // 2, (hi + 1) // 2))
        assert len(chunks) <= 5
        DR = np.zeros((128, 640), np.int64)
        DC = np.zeros((128, 640), np.int64)
        M = np.full((128, 640), NEG, np.float32)
        for j, cj in enumerate(chunks):
            for krl in range(2):
                for qrl in range(2):
                    kr = 2 * cj + krl
                    qr = 2 * pr + qrl
                    rv = r0(qr) <= kr < r0(qr) + 8
                    for kc in range(64):
                        for qc in range(64):
                            cv = c0(qc) <= kc < c0(qc) + 16
                            p = krl * 64 + kc
                            q = j * 128 + qrl * 64 + qc
                            if rv and cv:
                                DR[p, q] = kr - qr + 7
                                DC[p, q] = kc - qc + 15
                                M[p, q] = 0.0
        key = (tuple(c - pr for c in chunks), DR.tobytes(), M.tobytes())
        if key not in cfgs:
            cfgs[key] = len(idx_dr)
            idx_dr.append(DR)
            idx_dc.append(DC)
            masks.append(M)
        pairs.append((cfgs[key], chunks))
    return pairs, np.stack(idx_dr), np.stack(idx_dc), np.stack(masks)


def phase_na(P, PS, pT, pT_res, pv, pv_res, oT, oT_res, C, nq_g, nk_g, nab, pairs, ncfg):
    with P.phase():
        onesf = P.tile("onesf", [128, 128], F32)
        onesb = P.tile("onesb", [128, 128], BF16)
        P.dma("sp", onesf[:, :], C["ones"], writes=[onesf], owner=onesf)
        P.copy("dve", onesb[:, :], onesf[:, :], [onesf], [onesb])
        gq = P.tile("gq", [128, 1], F32)
        gk = P.tile("gk", [128, 1], F32)
        P.dma("sp", gq[:, :], nq_g, writes=[gq], owner=gq)
        P.dma("sp", gk[:, :], nk_g, writes=[gk], owner=gk)
        P.ts("dve", gq[:, :], gq[:, :], 128.0 ** -0.5, None, ALU.mult, None, [gq], [gq])
        msk = P.tile("msk", [128, ncfg, 640], F32)
        for c in range(ncfg):
            P.dma("sp", msk[:, c, :], C["namask"][c], writes=[msk], owner=msk)
        bias = P.tile("bias", [128, ncfg, 640], F32)
        xq = P.tile("xq", [128, NT], F32)
        xk = P.tile("xk", [128, NT], F32)
        z = P.tile("z", [128, NT], F32)
        qb = P.tile("qb", [128, NT], BF16)
        kb = P.tile("kb", [128, NT], BF16)
        V = P.tile("V", [128, 34, 128], BF16)
        tsq = P.tile("tsq", [128, 512], F32)
        tr = P.tile("tr", [128, 512], F32)
        sb = [P.tile("sb%d" % i, [128, 640], F32) for i in range(2)]
        pt = [P.tile("pt%d" % i, [128, 896], BF16) for i in range(2)]
        rr = [P.tile("rr%d" % i, [128, 256], F32) for i in range(2)]
        oo = [P.tile("oo%d" % i, [128, 256], F32) for i in range(2)]
        obuf = [P.tile("obuf%d" % i, [128, NT], BF16) for i in range(2)]
        nps = 0
        it = 0
        for hd in range(4):
            P.dma("sp", xq[:, :], pT[FMI["Bq"], hd], reads=[pT_res[FMI["Bq"]][hd]], writes=[xq], owner=xq)
            P.dma("pool", xk[:, :], pT[FMI["Bk"], hd], reads=[pT_res[FMI["Bk"]][hd]], writes=[xk], owner=xk)
            P.dma("sp", z[:, :], pT[FMI["Bz"], hd], reads=[pT_res[FMI["Bz"]][hd]], writes=[z], owner=z)
            P.dma("pool", V[:, :, :], pv[TMI["Bv"], :, hd * 128:(hd + 1) * 128].rearrange("(c p) d -> p c d", p=128),
                  reads=[pv_res[TMI["Bv"]]], writes=[V], owner=V)
            for c in range(ncfg):
                P.dma("pool", bias[:, c, :], nab[hd, c], writes=[bias], owner=bias)
            P.tt("dve", bias[:, :, :], bias[:, :, :], msk[:, :, :], ALU.add, [bias, msk], [bias])
            for (src, g, dst) in ((xq, gq, qb), (xk, gk, kb)):
                for (t0, nt) in TT:
                    rmsnorm_fm(P, PS, nps, src, src, onesf, g[:, 0:1], 128, tsq, tr, t0, nt, gt=[g])
                    nps += 1
                    P.copy("pool", dst[:, t0:t0 + nt], src[:, t0:t0 + nt], [src], [dst])
            ob = obuf[hd % 2]
            A, B_, Cb = PS[0], PS[1], PS[2]
            p_ = pt[it % 2]; r_ = rr[it % 2]; o_ = oo[it % 2]
            it += 1
            for j in range(2):
                P.mm(A[:, j * 256:(j + 1) * 256], kb[:, j * 128:(j + 1) * 128], qb[:, 0:256], True, True, [kb, qb], [A])
            P.act(p_[:, 0:512], A[:, 0:512], AF.Exp, [A], [p_])
            for j in range(2):
                P.mm(Cb[:, 0:256], V[:, j, :], p_[:, j * 256:(j + 1) * 256], j == 0, j == 1, [V, p_], [Cb])
            for j in range(2):
                P.mm(Cb[:, 256:512], onesb[:, :], p_[:, j * 256:(j + 1) * 256], j == 0, j == 1, [onesb, p_], [Cb])
            P.op("dve", lambda e: e.reciprocal(out=r_[:, 0:256], in_=Cb[:, 256:512]), reads=[Cb], writes=[r_])
            P.tt("dve", o_[:, 0:256], Cb[:, 0:256], r_[:, 0:256], ALU.mult, [Cb, r_], [o_])
            P.tt("dve", ob[:, 0:256], o_[:, 0:256], z[:, 0:256], ALU.mult, [o_, z], [ob])
            for pr, (cfg, chunks) in enumerate(pairs):
                s3 = (it % 2) * 3
                A, B_, Cb = PS[s3], PS[s3 + 1], PS[s3 + 2]
                p_ = pt[it % 2]; r_ = rr[it % 2]; o_ = oo[it % 2]; s_ = sb[it % 2]
                it += 1
                q0 = NCTX + pr * 128
                nl = len(chunks)
                for j, cj in enumerate(chunks):
                    k0 = NCTX + cj * 128
                    dstp, off = (A, j * 128) if j < 4 else (B_, 0)
                    P.mm(dstp[:, off:off + 128], kb[:, k0:k0 + 128], qb[:, q0:q0 + 128], True, True, [kb, qb], [dstp])
                for j in range(2):
                    P.mm(B_[:, 128 + j * 128:256 + j * 128], kb[:, j * 128:(j + 1) * 128], qb[:, q0:q0 + 128], True, True, [kb, qb], [B_])
                na = min(nl, 4) * 128
                P.tt("dve", s_[:, 0:na], A[:, 0:na], bias[:, cfg, 0:na], ALU.add, [A, bias], [s_])
                if nl == 5:
                    P.tt("dve", s_[:, 512:640], B_[:, 0:128], bias[:, cfg, 512:640], ALU.add, [B_, bias], [s_])
                P.act(p_[:, 0:nl * 128], s_[:, 0:nl * 128], AF.Exp, [s_], [p_])
                P.act(p_[:, 640:896], B_[:, 128:384], AF.Exp, [B_], [p_])
                srcs = [(2 + cj, j * 128) for j, cj in enumerate(chunks)] + [(0, 640), (1, 768)]
                for n_, (vc, off) in enumerate(srcs):
                    P.mm(Cb[:, 0:128], V[:, vc, :], p_[:, off:off + 128], n_ == 0, n_ == len(srcs) - 1, [V, p_], [Cb])
                for n_, (vc, off) in enumerate(srcs):
                    P.mm(Cb[:, 128:256], onesb[:, :], p_[:, off:off + 128], n_ == 0, n_ == len(srcs) - 1, [onesb, p_], [Cb])
                P.op("dve", lambda e: e.reciprocal(out=r_[:, 0:128], in_=Cb[:, 128:256]), reads=[Cb], writes=[r_])
                P.tt("dve", o_[:, 0:128], Cb[:, 0:128], r_[:, 0:128], ALU.mult, [Cb, r_], [o_])
                P.tt("pool", ob[:, q0:q0 + 128], o_[:, 0:128], z[:, q0:q0 + 128], ALU.mult, [o_, z], [ob])
            P.dma("sp", oT[4 + hd], ob[:, :], reads=[ob], writes=[oT_res[4 + hd]], owner=ob)


import os
STAGE = int(os.environ.get('GDN_STAGE', '99'))
LEVELS = int(os.environ.get('GDN_LEVELS', '6'))
DIRS = int(os.environ.get('GDN_DIRS', '2'))


def gdn_consts():
    c = {}
    t = np.arange(128)
    c["U_f"] = (t[:, None] <= t[None, :]).astype(np.float32)
    c["U_b"] = (t[:, None] >= t[None, :]).astype(np.float32)
    c["mT_f"] = np.where(t[:, None] <= t[None, :], 0.0, NEG).astype(np.float32)
    c["mT_b"] = np.where(t[:, None] >= t[None, :], 0.0, NEG).astype(np.float32)
    c["st_f"] = (t[:, None] < t[None, :]).astype(np.float32)
    c["st_b"] = (t[:, None] > t[None, :]).astype(np.float32)

    def bm(b):
        return ((t[:, None] // b) == (t[None, :] // b)).astype(np.float32)
    c["bm16"] = bm(16)
    c["e32"] = bm(32) - bm(16)
    c["e64"] = bm(64) - bm(32)
    c["e128"] = bm(128) - bm(64)
    return c


def phase_gdn(P, PS, pT, pT_res, psm, psm_res, oT, oT_res, C, g_conv, g_alog, g_dtb, g_norm, heads=range(4), nsteps=34):
    nps = [0]

    def bank():
        nps[0] += 1
        return PS[nps[0] % 8]

    with P.phase():
        cst = {}
        for nm in ("ident", "ones", "U_f", "U_b", "mT_f", "mT_b", "st_f", "st_b", "bm16", "e32", "e64", "e128"):
            cst[nm] = P.tile("c_" + nm, [128, 128], F32)
            P.dma("sp", cst[nm][:, :], C[nm], writes=[cst[nm]], owner=cst[nm])
        ident, ones = cst["ident"], cst["ones"]
        negones = P.tile("negones", [128, 128], F32)
        P.ts("dve", negones[:, :], ones[:, :], -1.0, None, ALU.mult, None, [ones], [negones])
        cw = P.tile("cw", [128, 48], F32)
        P.dma("sp", cw[:, :], g_conv, writes=[cw], owner=cw)
        gn = P.tile("gn", [128, 1], F32)
        P.dma("sp", gn[:, :], g_norm, writes=[gn], owner=gn)
        al = P.tile("al", [128, 8], F32)
        dtb = P.tile("dtb", [128, 8], F32)
        P.dma("sp", al[:, :], g_alog.broadcast_to([128, 8]), writes=[al], owner=al)
        P.dma("sp", dtb[:, :], g_dtb.broadcast_to([128, 8]), writes=[dtb], owner=dtb)
        P.act(al[:, :], al[:, :], AF.Exp, [al], [al])
        P.ts("dve", al[:, :], al[:, :], -1.0, None, ALU.mult, None, [al], [al])
        G = P.tile("G", [128, 34, 16], F32)
        P.dma("sp", G[:, :, :], psm.rearrange("(c p) k -> p c k", p=128), reads=[psm_res], writes=[G], owner=G)
        for j in range(8):
            P.ts("dve", G[:, :, j], G[:, :, j], dtb[:, j:j + 1], None, ALU.add, None, [G, dtb], [G])
        P.act(G[:, :, 0:8], G[:, :, 0:8], AF.Exp, [G], [G])
        P.act(G[:, :, 0:8], G[:, :, 0:8], AF.Ln, [G], [G], bias=1.0)
        for j in range(8):
            P.ts("dve", G[:, :, j], G[:, :, j], al[:, j:j + 1], None, ALU.mult, None, [G, al], [G])
        P.act(G[:, :, 8:16], G[:, :, 8:16], AF.Sigmoid, [G], [G])

        xr = P.tile("xr", [128, NT], F32)
        qn = P.tile("qn", [128, NT], F32)
        kn = P.tile("kn", [128, NT], F32)
        vn = P.tile("vn", [128, NT], F32)
        qnb = P.tile("qnb", [128, NT], BF16)
        knb = P.tile("knb", [128, NT], BF16)
        ktok = P.tile("ktok", [128, 34, 128], F32)
        vtok = P.tile("vtok", [128, 34, 128], F32)
        oacc = P.tile("oacc", [128, NT], F32)
        tsq = P.tile("tsq", [128, 512], F32)
        tr = P.tile("tr", [128, 512], F32)
        ob = P.tile("ob", [128, NT], BF16)
        segs = [(0, NCTX), (NCTX, NT)]
        W = []
        for d in range(2):
            w = {}
            for nm in ("gU", "Gt", "egb", "Gs", "Mt", "MtT", "Q", "QT", "Q2", "Q2T", "Pm", "PmT", "Y", "YT", "dg", "Xb", "kg", "u"):
                w[nm] = P.tile("%s%d" % (nm, d), [128, 128], F32)
            for nm in ("AT", "qg", "kd", "wT", "vnew", "Sb"):
                w[nm] = P.tile("%s%d" % (nm, d), [128, 128], BF16)
            w["S"] = P.tile("S%d" % d, [128, 128], F32)
            w["egc"] = P.tile("egc%d" % d, [128, 1], F32)
            if os.environ.get("GDN_MEMSET"):
                for nm_, t_ in w.items():
                    P.op("pool", lambda e: e.memset(t_[:, :], 0.0), writes=[t_])
            W.append(w)

        for hd in heads:
            for ci, (comp, dst) in enumerate((("Aq", qn), ("Ak", kn), ("Av", vn))):
                P.dma("sp" if ci % 2 == 0 else "pool", xr[:, :], pT[FMI[comp], hd], reads=[pT_res[FMI[comp]][hd]], writes=[xr], owner=xr)
                cb = (ci * 4 + hd) * 4
                P.act(dst[:, :], xr[:, :], AF.Identity, [xr, cw], [dst], scale=cw[:, cb + 2:cb + 3])
                for (s, e) in segs:
                    P.stt("dve", dst[:, s + 2:e], xr[:, s:e - 2], cw[:, cb + 0:cb + 1], dst[:, s + 2:e], ALU.mult, ALU.add, [xr, cw, dst], [dst])
                    P.stt("dve", dst[:, s + 1:e], xr[:, s:e - 1], cw[:, cb + 1:cb + 2], dst[:, s + 1:e], ALU.mult, ALU.add, [xr, cw, dst], [dst])
                    P.stt("dve", dst[:, s:e - 1], xr[:, s + 1:e], cw[:, cb + 3:cb + 4], dst[:, s:e - 1], ALU.mult, ALU.add, [xr, cw, dst], [dst])
                P.act(dst[:, :], dst[:, :], AF.Silu, [dst], [dst])
                if comp != "Av":
                    for (t0, nt) in TT:
                        rmsnorm_fm(P, PS, nps[0], dst, dst, ones, (128.0 ** -0.5) if comp == "Aq" else 1.0, 1, tsq, tr, t0, nt)
                        nps[0] += 1
            P.dma("pool", xr[:, :], pT[FMI["Az"], hd], reads=[pT_res[FMI["Az"]][hd]], writes=[xr], owner=xr)
            P.copy("pool", qnb[:, :], qn[:, :], [qn], [qnb])
            P.copy("pool", knb[:, :], kn[:, :], [kn], [knb])
            for (src, dstt) in ((kn, ktok), (vn, vtok)):
                for g4 in range(0, 34, 4):
                    ps = bank()
                    nn = min(4, 34 - g4)
                    for k in range(nn):
                        n = g4 + k
                        P.mm(ps[:, k * 128:(k + 1) * 128], src[:, n * 128:(n + 1) * 128], ident[:, :], True, True, [src, ident], [ps])
                    P.copy("act", dstt[:, g4:g4 + nn, :], ps[:, 0:nn * 128].rearrange("p (a b) -> p a b", b=128), [ps], [dstt])
            for d in range(2):
                P.op("pool", lambda e: e.memset(W[d]["S"][:, :], 0.0), writes=[W[d]["S"]])
                P.op("pool", lambda e: e.memset(W[d]["Sb"][:, :], 0.0), writes=[W[d]["Sb"]])
            order = {0: [0, 1] + list(range(2, 34)), 1: [1, 0] + list(range(33, 1, -1))}
            first_write = {}
            for step in range(nsteps):
                for d in range(DIRS):
                    n = order[d][step]
                    w = W[d]
                    sfx = "_f" if d == 0 else "_b"
                    last = 127 if d == 0 else 0
                    t0 = n * 128
                    gcol = G[:, n, d * 4 + hd:d * 4 + hd + 1]
                    bcol = G[:, n, 8 + d * 4 + hd:8 + d * 4 + hd + 1]
                    P.ts("dve", w["gU"][:, :], cst["U" + sfx][:, :], gcol, None, ALU.mult, None, [cst["U" + sfx], G], [w["gU"]])
                    pD = bank()
                    P.mm(pD[:, 0:128], ones[:, :], w["gU"][:, :], True, False, [ones, w["gU"]], [pD])
                    P.mm(pD[:, 0:128], w["gU"][:, :], negones[:, :], False, False, [w["gU"], negones], [pD])
                    P.mm(pD[:, 0:128], ident[:, :], cst["mT" + sfx][:, :], False, True, [ident, cst["mT" + sfx]], [pD])
                    P.mm(pD[:, 128:256], ones[:, :], w["gU"][:, :], True, True, [ones, w["gU"]], [pD])
                    P.mm(pD[:, 256:258], w["gU"][:, :], ones[:, 0:2], True, True, [w["gU"], ones], [pD])
                    if STAGE < 2:
                        continue
                    P.act(w["Gt"][:, :], pD[:, 0:128], AF.Exp, [pD], [w["Gt"]])
                    P.act(w["egb"][:, :], pD[:, 128:256], AF.Exp, [pD], [w["egb"]])
                    P.act(w["egc"][:, :], pD[:, 256:257], AF.Exp, [pD], [w["egc"]])
                    if STAGE < 3:
                        continue
                    pK = bank()
                    P.mm(pK[:, 0:128], knb[:, t0:t0 + 128], knb[:, t0:t0 + 128], True, True, [knb], [pK])
                    P.mm(pK[:, 128:256], knb[:, t0:t0 + 128], qnb[:, t0:t0 + 128], True, True, [knb, qnb], [pK])
                    P.tt("pool", w["Gs"][:, :], w["Gt"][:, :], cst["st" + sfx][:, :], ALU.mult, [w["Gt"], cst["st" + sfx]], [w["Gs"]])
                    P.stt("dve", w["Mt"][:, :], pK[:, 0:128], bcol, w["Gs"][:, :], ALU.mult, ALU.mult, [pK, G, w["Gs"]], [w["Mt"]])
                    P.tt("dve", w["AT"][:, :], pK[:, 128:256], w["Gt"][:, :], ALU.mult, [pK, w["Gt"]], [w["AT"]])
                    if STAGE < 4:
                        continue
                    pT_ = bank()
                    P.mm(pT_[:, 0:128], w["Mt"][:, :], ident[:, :], True, True, [w["Mt"], ident], [pT_])
                    P.copy("act", w["MtT"][:, :], pT_[:, 0:128], [pT_], [w["MtT"]])
                    Q, QT = w["Q"], w["QT"]
                    P.tt("pool", Q[:, :], w["Mt"][:, :], cst["bm16"][:, :], ALU.mult, [w["Mt"], cst["bm16"]], [Q])
                    P.tt("dve", QT[:, :], w["MtT"][:, :], cst["bm16"][:, :], ALU.mult, [w["MtT"], cst["bm16"]], [QT])
                    X, XT = w["Pm"], w["PmT"]
                    P.tt("pool", X[:, :], ident[:, :], Q[:, :], ALU.subtract, [ident, Q], [X])
                    P.tt("dve", XT[:, :], ident[:, :], QT[:, :], ALU.subtract, [ident, QT], [XT])
                    nxt = [(w["Q"], w["QT"]), (w["Q2"], w["Q2T"])]
                    for s in range(1, 4):
                        pq = bank()
                        P.mm(pq[:, 0:128], QT[:, :], Q[:, :], True, True, [QT, Q], [pq])
                        P.mm(pq[:, 128:256], Q[:, :], QT[:, :], True, True, [QT, Q], [pq])
                        Qn, QTn = nxt[s % 2]
                        P.copy("act", Qn[:, :], pq[:, 0:128], [pq], [Qn])
                        P.copy("act", QTn[:, :], pq[:, 128:256], [pq], [QTn])
                        Q, QT = Qn, QTn
                        pp = bank()
                        P.mm(pp[:, 0:128], QT[:, :], X[:, :], True, True, [QT, X], [pp])
                        P.mm(pp[:, 128:256], Q[:, :], XT[:, :], True, True, [Q, XT], [pp])
                        P.tt("dve", X[:, :], X[:, :], pp[:, 0:128], ALU.add, [X, pp], [X])
                        P.tt("dve", XT[:, :], XT[:, :], pp[:, 128:256], ALU.add, [XT, pp], [XT])
                    E, ET = w["Q"], w["QT"]
                    for li, em in enumerate(("e32", "e64", "e128")):
                        lastl = li == 2
                        P.tt("pool", E[:, :], w["Mt"][:, :], cst[em][:, :], ALU.mult, [w["Mt"], cst[em]], [E])
                        P.tt("pool", ET[:, :], w["MtT"][:, :], cst[em][:, :], ALU.mult, [w["MtT"], cst[em]], [ET])
                        py = bank()
                        P.mm(py[:, 0:128], ET[:, :], X[:, :], True, True, [ET, X], [py])
                        if not lastl:
                            P.mm(py[:, 128:256], X[:, :], ET[:, :], True, True, [X, ET], [py])
                        P.copy("act", w["Y"][:, :], py[:, 0:128], [py], [w["Y"]])
                        if not lastl:
                            P.copy("act", w["YT"][:, :], py[:, 128:256], [py], [w["YT"]])
                        pz = bank()
                        P.mm(pz[:, 0:128], XT[:, :], w["Y"][:, :], True, True, [XT, w["Y"]], [pz])
                        if not lastl:
                            P.mm(pz[:, 128:256], w["Y"][:, :], XT[:, :], True, True, [w["Y"], XT], [pz])
                        P.tt("dve", X[:, :], X[:, :], pz[:, 0:128], ALU.subtract, [X, pz], [X])
                        if not lastl:
                            P.tt("dve", XT[:, :], XT[:, :], pz[:, 128:256], ALU.subtract, [XT, pz], [XT])
                    if STAGE < 5:
                        continue
                    P.ts("dve", w["dg"][:, :], ident[:, :], bcol, None, ALU.mult, None, [ident, G], [w["dg"]])
                    pB = bank()
                    P.mm(pB[:, 0:128], ones[:, :], w["dg"][:, :], True, True, [ones, w["dg"]], [pB])
                    P.tt("dve", w["Xb"][:, :], w["Pm"][:, :], pB[:, 0:128], ALU.mult, [w["Pm"], pB], [w["Xb"]])
                    P.act(w["kg"][:, :], ktok[:, n, :], AF.Identity, [ktok, w["egc"]], [w["kg"]], scale=w["egc"][:, 0:1])
                    pU = bank()
                    P.mm(pU[:, 0:128], w["Xb"][:, :], vtok[:, n, :], True, True, [w["Xb"], vtok], [pU])
                    P.mm(pU[:, 128:256], w["kg"][:, :], w["Xb"][:, :], True, True, [w["kg"], w["Xb"]], [pU])
                    P.copy("act", w["u"][:, :], pU[:, 0:128], [pU], [w["u"]])
                    P.copy("act", w["wT"][:, :], pU[:, 128:256], [pU], [w["wT"]])
                    P.tt("pool", w["qg"][:, :], qn[:, t0:t0 + 128], w["egb"][:, :], ALU.mult, [qn, w["egb"]], [w["qg"]])
                    P.act(w["kd"][:, :], ktok[:, n, :], AF.Identity, [ktok, w["Gt"]], [w["kd"]], scale=w["Gt"][:, last:last + 1])
                    if STAGE < 6:
                        continue
                    pW = bank()
                    P.mm(pW[:, 0:128], w["wT"][:, :], w["Sb"][:, :], True, True, [w["wT"], w["Sb"]], [pW])
                    P.tt("dve", w["vnew"][:, :], w["u"][:, :], pW[:, 0:128], ALU.subtract, [w["u"], pW], [w["vnew"]])
                    P.mm(pW[:, 128:256], w["Sb"][:, :], w["qg"][:, :], True, False, [w["Sb"], w["qg"]], [pW])
                    P.mm(pW[:, 128:256], w["vnew"][:, :], w["AT"][:, :], False, True, [w["vnew"], w["AT"]], [pW])
                    if n not in first_write:
                        first_write[n] = True
                        P.copy("act", oacc[:, t0:t0 + 128], pW[:, 128:256], [pW], [oacc])
                    else:
                        P.tt("dve", oacc[:, t0:t0 + 128], oacc[:, t0:t0 + 128], pW[:, 128:256], ALU.add, [oacc, pW], [oacc])
                    P.mm(pW[:, 256:384], w["kd"][:, :], w["vnew"][:, :], True, True, [w["kd"], w["vnew"]], [pW])
                    P.stt("dve", w["S"][:, :], w["S"][:, :], w["egb"][:, last:last + 1], pW[:, 256:384], ALU.mult, ALU.add, [w["S"], w["egb"], pW], [w["S"]])
                    P.copy("act", w["Sb"][:, :], w["S"][:, :], [w["S"]], [w["Sb"]])
            for (t0, nt) in TT:
                rmsnorm_fm(P, PS, nps[0], oacc, oacc, ones, gn[:, 0:1], 128, tsq, tr, t0, nt, gt=[gn])
                nps[0] += 1
            P.tt("dve", ob[:, :], oacc[:, :], xr[:, :], ALU.mult, [oacc, xr], [ob])
            P.dma("sp", oT[hd], ob[:, :], reads=[ob], writes=[oT_res[hd]], owner=ob)


import math

_IN_SPLITS = (1024,) * 4 + (8,) * 4 + (1024,) * 10
_HC = {}


def host_consts():
    if _HC:
        return _HC
    c = {}
    c["ident"] = np.eye(128, dtype=np.float32)
    c["ones"] = np.ones((128, 128), np.float32)
    blk = np.zeros((128, 128), np.float32)
    blk[0:64, 0:64] = 1.0
    blk[64:128, 64:128] = 1.0
    c["blk64"] = blk
    rot = np.zeros((128, 128), np.float32)
    for d in range(128):
        f = d % 32
        if f < 16:
            rot[d + 16, d] = -1.0
        else:
            rot[d - 16, d] = 1.0
    c["rot"] = rot
    n_freq = 16
    inv = (np.float32(10000.0) ** (-np.arange(n_freq, dtype=np.float32) / np.float32(n_freq))).astype(np.float32)
    t = np.arange(NLAT, dtype=np.int32)
    pos = np.stack([t // 64, t % 64], axis=-1).astype(np.float32)
    ang = (pos[:, :, None] * inv[None, None, :]).astype(np.float32)
    cs, sn = np.cos(ang).astype(np.float32), np.sin(ang).astype(np.float32)
    cos = np.ones((128, NT), np.float32)
    sin = np.zeros((128, NT), np.float32)
    for d in range(128):
        f = d % 64
        a = f // 32
        fi = f % 16
        cos[d, NCTX:] = cs[:, a, fi]
        sin[d, NCTX:] = sn[:, a, fi]
    c["cos"] = cos
    c["sin"] = sin
    pairs, idr, idc, masks = na_tables()
    c["namask"] = masks
    c["_pairs"] = pairs
    c["_idr"] = idr
    c["_idc"] = idc
    c.update(gdn_consts())
    _HC.update(c)
    return _HC


CONST_NAMES = ["ident", "ones", "blk64", "rot", "cos", "sin", "namask", "U_f", "U_b", "mT_f", "mT_b", "st_f", "st_b", "bm16", "e32", "e64", "e128"]


def core_inputs(inp, l, hh, xfull, mod_lat, mod_ctx, light=False):
    hc = host_consts()
    d = {}
    if not light:
        d["xin"] = np.ascontiguousarray(xfull)
        d["normg"] = np.ascontiguousarray(inp["norm_g"][l].reshape(32, 128).T)
        mv = np.stack([mod_lat[0:4096], mod_lat[4096:8192], mod_lat[8192:], mod_ctx[0:4096], mod_ctx[4096:8192], mod_ctx[8192:]])
        d["modv"] = np.ascontiguousarray(mv.reshape(6, 32, 128).transpose(2, 0, 1).reshape(128, 192))
    w = inp["w_in"][l]
    offs = np.cumsum([0] + list(_IN_SPLITS))
    cols = []
    for i in range(4):
        cols += list(range(offs[i] + hh * 512, offs[i] + hh * 512 + 512))
    for i in range(4, 8):
        cols += list(range(offs[i] + hh * 4, offs[i] + hh * 4 + 4))
    for i in range(8, 18):
        cols += list(range(offs[i] + hh * 512, offs[i] + hh * 512 + 512))
    d["w_in"] = np.ascontiguousarray(w[:, cols])
    hs = slice(hh * 4, hh * 4 + 4)
    cs = slice(hh * 512, hh * 512 + 512)
    d["lru_conv"] = np.ascontiguousarray(inp["lru_conv"][l][:, cs].reshape(4, 4, 128).transpose(2, 1, 0).reshape(128, 16))
    d["lru_conv_b"] = np.ascontiguousarray(inp["lru_conv_b"][l][cs].reshape(4, 128).T)
    d["lru_wg"] = np.ascontiguousarray(inp["lru_w_gate"][l][:, :, hs])
    d["lru_bg"] = np.ascontiguousarray(inp["lru_b_gate"][l][:, :, cs].reshape(2, 2, 4, 128).transpose(3, 0, 1, 2).reshape(128, 16))
    d["lru_lam"] = np.ascontiguousarray(inp["lru_lambda"][l][:, cs].reshape(2, 4, 128).transpose(2, 0, 1).reshape(128, 8))
    d["dq_g"] = np.ascontiguousarray(np.tile(inp["diff_q_norm"][l], 2).reshape(128, 1))
    d["dk_g"] = np.ascontiguousarray(np.tile(inp["diff_k_norm"][l], 2).reshape(128, 1))
    d["dlam"] = np.ascontiguousarray(inp["diff_lambda"][l].reshape(1, 256))
    d["dsub"] = np.ascontiguousarray(inp["diff_subln"][l].reshape(128, 1))
    d["nq_g"] = np.ascontiguousarray(inp["na_q_norm"][l].reshape(128, 1))
    d["nk_g"] = np.ascontiguousarray(inp["na_k_norm"][l].reshape(128, 1))
    rpb = inp["na_rpb"][l][hs]
    d["nab"] = np.ascontiguousarray(rpb[:, hc["_idr"], hc["_idc"]])
    gc = inp["gdn_conv"][l]
    gcc = np.stack([gc[:, ci * 1024 + hh * 512: ci * 1024 + hh * 512 + 512] for ci in range(3)])
    d["g_conv"] = np.ascontiguousarray(gcc.reshape(3, 4, 4, 128).transpose(3, 0, 2, 1).reshape(128, 48))
    d["g_alog"] = np.ascontiguousarray(inp["gdn_a_log"][l][:, hs].reshape(1, 8))
    d["g_dtb"] = np.ascontiguousarray(inp["gdn_dt_bias"][l][:, hs].reshape(1, 8))
    d["g_norm"] = np.ascontiguousarray(inp["gdn_norm_g"][l].reshape(128, 1))
    lam_init = 0.8 - 0.6 * math.exp(-0.3 * l)
    d["laminit"] = np.tile(np.array([[-lam_init, 1.0 - lam_init]], np.float32), (128, 1))
    if not light:
        for k in CONST_NAMES:
            d["c_" + k] = hc[k]
    return d


def build_la(phases=("norm", "inproj", "gdn", "na", "diff", "lru"), dump=False, comps=None):
    hc = host_consts()
    ncfg = hc["namask"].shape[0]
    P = Prog()
    PS = alloc_psum(P)
    EI = "ExternalInput"
    xin = P.dram("xin", [NT, 4096], F32, kind=EI)
    normg = P.dram("normg", [128, 32], F32, kind=EI)
    modv = P.dram("modv", [128, 192], F32, kind=EI)
    w_in = P.dram("w_in", [4096, NCOLS], F32, kind=EI)
    C = {}
    for k in CONST_NAMES:
        C[k] = P.dram("c_" + k, list(hc[k].shape), F32, kind=EI)
    lru_conv = P.dram("lru_conv", [128, 16], F32, kind=EI)
    lru_conv_b = P.dram("lru_conv_b", [128, 4], F32, kind=EI)
    lru_wg = P.dram("lru_wg", [2, 2, 4, 128, 128], F32, kind=EI)
    lru_bg = P.dram("lru_bg", [128, 16], F32, kind=EI)
    lru_lam = P.dram("lru_lam", [128, 8], F32, kind=EI)
    dq_g = P.dram("dq_g", [128, 1], F32, kind=EI)
    dk_g = P.dram("dk_g", [128, 1], F32, kind=EI)
    dlam = P.dram("dlam", [1, 256], F32, kind=EI)
    dsub = P.dram("dsub", [128, 1], F32, kind=EI)
    nq_g = P.dram("nq_g", [128, 1], F32, kind=EI)
    nk_g = P.dram("nk_g", [128, 1], F32, kind=EI)
    nab = P.dram("nab", [4, ncfg, 128, 640], F32, kind=EI)
    g_conv = P.dram("g_conv", [128, 48], F32, kind=EI)
    g_alog = P.dram("g_alog", [1, 8], F32, kind=EI)
    g_dtb = P.dram("g_dtb", [1, 8], F32, kind=EI)
    g_norm = P.dram("g_norm", [128, 1], F32, kind=EI)
    kd = "ExternalOutput" if dump else "Internal"
    hT = P.dram("hT", [32, 128, NT], BF16, kind=kd)
    hT_res = [P.res("hT%d" % i) for i in range(34)]
    pT = P.dram("pT", [12, 4, 128, NT], F32, kind=kd)
    pT_res = [[P.res("pT%d_%d" % (i, j)) for j in range(4)] for i in range(12)]
    pv = P.dram("pv", [2, NT, 512], BF16, kind=kd)
    pv_res = [P.res("pv%d" % i) for i in range(2)]
    psm = P.dram("psm", [NT, 16], F32, kind=kd)
    psm_res = P.res("psm")
    oT = P.dram("oT", [16, 128, NT], BF16, kind="ExternalOutput")
    oT_res = [P.res("oT%d" % i) for i in range(16)]
    laminit = P.dram("laminit", [128, 2], F32, kind=EI)
    if "norm" in phases:
        phase_norm(P, PS, xin, normg, modv, C["ident"], hT, hT_res)
    if "inproj" in phases:
        phase_inproj(P, PS, w_in, hT, hT_res, pT, pT_res, pv, pv_res, psm, psm_res, comps=comps)
    if "lru" in phases:
        phase_lru(P, PS, pT, pT_res, oT, oT_res, lru_conv, lru_conv_b, lru_wg, lru_bg, lru_lam)
    if "diff" in phases:
        phase_diff(P, PS, pT, pT_res, pv, pv_res, oT, oT_res, C, dq_g, dk_g, dlam, dsub, laminit)
    if "na" in phases:
        phase_na(P, PS, pT, pT_res, pv, pv_res, oT, oT_res, C, nq_g, nk_g, nab, hc["_pairs"], ncfg)
    if "gdn" in phases:
        phase_gdn(P, PS, pT, pT_res, psm, psm_res, oT, oT_res, C, g_conv, g_alog, g_dtb, g_norm)
    print("LA instructions", P.n_ins, "waits", P.n_wait, "dsems", P.ndsem)
    return P.finish()


import os

NL = 4
NCORE = 4


def phase_mod(P, PS, cs, w_mod, bmT, modv_d, gtrow_d):
    with P.phase():
        sT = P.tile("sT", [128, 32, 2], F32)
        sg = P.tile("sg", [128, 32, 2], F32)
        bm = P.tile("bm", [128, NL * 96], F32)
        P.dma("sp", sT[:, :, :], cs.rearrange("p (c r) -> p c r", r=2), writes=[sT], owner=sT)
        P.dma("sp", bm[:, :], bmT, writes=[bm], owner=bm)
        P.act(sg[:, :, :], sT[:, :, :], AF.Sigmoid, [sT], [sg])
        P.tt("dve", sT[:, :, :], sT[:, :, :], sg[:, :, :], ALU.mult, [sT, sg], [sT])
        wt = [P.tile("wm%d" % i, [128, 32, 128], F32) for i in range(3)]
        mT = [P.tile("mT%d" % i, [128, 96, 2], F32) for i in range(2)]
        mv = [P.tile("mvv%d" % i, [128, 192], F32) for i in range(2)]
        n = 0
        for l in range(NL):
            m = mT[l % 2]
            v = mv[l % 2]
            wv = w_mod[l].rearrange("(c p) n -> p c n", p=128)
            for j in range(96):
                w = wt[n % 3]
                ps = PS[n % 8]
                q = "sp" if n % 2 == 0 else "pool"
                n += 1
                P.dma(q, w[:, :, :], wv[:, :, j * 128:(j + 1) * 128], writes=[w], owner=w)
                for c in range(32):
                    P.mm(ps[:, 0:2], w[:, c, :], sT[:, c, :], c == 0, c == 31, [w, sT], [ps])
                P.ts("dve", m[:, j, :], ps[:, 0:2], bm[:, l * 96 + j:l * 96 + j + 1], None, ALU.add, None, [ps, bm], [m])
            P.copy("dve", v[:, 0:96], m[:, :, 0], [m], [v])
            P.copy("dve", v[:, 96:192], m[:, :, 1], [m], [v])
            P.dma("sp", modv_d[l], v[:, :], reads=[v], owner=v)
            for r in range(2):
                for g in range(4):
                    P.dma("sp", gtrow_d[l, r, g * 1024:(g + 1) * 1024].rearrange("(c p) -> p c", p=128),
                          v[:, r * 96 + 64 + g * 8: r * 96 + 64 + (g + 1) * 8], reads=[v], owner=v, allow_slow_non_contiguous=True)


def build_fused():
    hc = host_consts()
    ncfg = hc["namask"].shape[0]
    P = Prog()
    PS = alloc_psum(P)
    EI = "ExternalInput"
    xin = P.dram("xin", [NT, 4096], F32, kind=EI)
    cs = P.dram("cs", [128, 64], F32, kind=EI)
    w_mod = P.dram("w_mod", [NL, 4096, 12288], F32, kind=EI)
    bmT = P.dram("bmT", [128, NL * 96], F32, kind=EI)
    normg = P.dram("normg", [NL, 128, 32], F32, kind=EI)
    w_in = P.dram("w_in", [NL, 4096, 2 * NCOLS], F32, kind=EI)
    w_out = P.dram("w_out", [NL, 4096, 4096], F32, kind=EI)
    C = {k: P.dram("c_" + k, list(hc[k].shape), F32, kind=EI) for k in CONST_NAMES}
    shp = dict(lru_conv=[128, 16], lru_conv_b=[128, 4], lru_wg=[2, 2, 4, 128, 128], lru_bg=[128, 16], lru_lam=[128, 8],
               dq_g=[128, 1], dk_g=[128, 1], dlam=[1, 256], dsub=[128, 1], nq_g=[128, 1], nk_g=[128, 1],
               nab=[4, ncfg, 128, 640], g_conv=[128, 48], g_alog=[1, 8], g_dtb=[1, 8], g_norm=[128, 1], laminit=[128, 2])
    prm = {k: P.dram(k, [NL, 2] + v, F32, kind=EI) for k, v in shp.items()}
    xout = P.dram("xout", [NLAT, 4096], F32, kind="ExternalOutput")
    modv_d = P.dram("modv_d", [NL, 128, 192], F32)
    gtrow_d = P.dram("gtrow_d", [NL, 2, 4096], F32)
    hT = P.dram("hT", [32, 128, NT], BF16)
    hT_res = [P.res("hT%d" % i) for i in range(34)]
    pT = P.dram("pT", [12, 4, 128, NT], F32)
    pT_res = [[P.res("pT%d_%d" % (i, j)) for j in range(4)] for i in range(12)]
    pv = P.dram("pv", [2, NT, 512], BF16)
    pv_res = [P.res("pv%d" % i) for i in range(2)]
    psm = P.dram("psm", [NT, 16], F32)
    psm_res = P.res("psm")
    oT = P.dram("oT", [32, 128, NT], BF16)
    oT_res = [P.res("oT%d" % i) for i in range(32)]
    xb = [P.dram("xb%d" % i, [NT, 4096], F32) for i in range(2)]
    phase_mod(P, PS, cs, w_mod, bmT, modv_d, gtrow_d)
    for l in range(NL):
        xcur = xin if l == 0 else xb[l % 2]
        xnext = xb[(l + 1) % 2]
        phase_norm(P, PS, xcur, normg[l], modv_d[l], C["ident"], hT, hT_res)
        for hh in range(2):
            g = lambda k: prm[k][l, hh]
            o_ = oT[hh * 16:(hh + 1) * 16]
            r_ = oT_res[hh * 16:(hh + 1) * 16]
            phase_inproj(P, PS, w_in[l][:, hh * NCOLS:(hh + 1) * NCOLS], hT, hT_res, pT, pT_res, pv, pv_res, psm, psm_res)
            phase_lru(P, PS, pT, pT_res, o_, r_, g("lru_conv"), g("lru_conv_b"), g("lru_wg"), g("lru_bg"), g("lru_lam"))
            phase_diff(P, PS, pT, pT_res, pv, pv_res, o_, r_, C, g("dq_g"), g("dk_g"), g("dlam"), g("dsub"), g("laminit"))
            phase_na(P, PS, pT, pT_res, pv, pv_res, o_, r_, C, g("nq_g"), g("nk_g"), g("nab"), hc["_pairs"], ncfg)
            phase_gdn(P, PS, pT, pT_res, psm, psm_res, o_, r_, C, g("g_conv"), g("g_alog"), g("g_dtb"), g("g_norm"))
        if l < NL - 1:
            phase_outproj(P, PS, oT, oT_res, w_out[l], xcur, gtrow_d[l, 0:1, :], gtrow_d[l, 1:2, :], xnext, 2)
        else:
            phase_outproj(P, PS, oT, oT_res, w_out[l], xcur, gtrow_d[l, 0:1, :], gtrow_d[l, 1:2, :], None, 2,
                          xo_map=lambda ti: None if ti < 2 else xout[(ti - 2) * 128:(ti - 1) * 128, :])
    print("FUSED instructions", P.n_ins, "waits", P.n_wait, "dsems", P.ndsem, {k: v for k, v in P.cnt.items() if k in P.eng}, flush=True)
    return P.finish()


def fused_inputs(inp, b):
    hc = host_consts()
    d = {}
    d["xin"] = np.ascontiguousarray(np.concatenate([inp["ctx"][b], inp["x"][b]], axis=0))
    cs2 = np.stack([inp["c"][b], inp["c_ctx"]], axis=-1)
    d["cs"] = np.ascontiguousarray(cs2.reshape(32, 128, 2).transpose(1, 0, 2).reshape(128, 64))
    d["w_mod"] = np.ascontiguousarray(inp["w_mod"])
    d["bmT"] = np.ascontiguousarray(inp["b_mod"].reshape(NL, 96, 128).transpose(2, 0, 1).reshape(128, NL * 96))
    d["normg"] = np.ascontiguousarray(inp["norm_g"].reshape(NL, 32, 128).transpose(0, 2, 1))
    per = [[core_inputs(inp, l, hh, None, None, None, light=True) for hh in range(2)] for l in range(NL)]
    d["w_in"] = np.stack([np.concatenate([per[l][0]["w_in"], per[l][1]["w_in"]], axis=1) for l in range(NL)])
    rows = []
    for hh in range(2):
        for grp in range(4):
            for hl in range(4):
                r0 = grp * 1024 + (hh * 4 + hl) * 128
                rows += list(range(r0, r0 + 128))
    d["w_out"] = np.ascontiguousarray(inp["w_out"][:, rows, :])
    for k in ("lru_conv", "lru_conv_b", "lru_wg", "lru_bg", "lru_lam", "dq_g", "dk_g", "dlam", "dsub", "nq_g", "nk_g", "nab",
              "g_conv", "g_alog", "g_dtb", "g_norm", "laminit"):
        d[k] = np.ascontiguousarray(np.stack([np.stack([per[l][hh][k] for hh in range(2)]) for l in range(NL)]))
    for k in CONST_NAMES:
        d["c_" + k] = hc[k]
    return d


def kernel(**inputs):
    inp = {k: np.asarray(v) for k, v in inputs.items()}
    nc = build_fused()
    in_maps = [fused_inputs(inp, b) for b in range(NCORE)]
    res = run_bass_kernel_spmd(nc, in_maps, core_ids=list(range(NCORE)))
    return np.stack([np.asarray(res.results[b]["xout"]) for b in range(NCORE)]).astype(np.float32)
```

```python
import contextlib
import numpy as np
import concourse.bass as bass
import concourse.mybir as mybir
from concourse.bass_utils import run_bass_kernel_spmd

F32 = mybir.dt.float32
BF16 = mybir.dt.bfloat16
AF = mybir.ActivationFunctionType
ALU = mybir.AluOpType
AX = mybir.AxisListType

SELF_SYNC = ("dve", "act", "pool")


class Res:
    __slots__ = ("name", "w", "r", "dsem", "excl")

    def __init__(self, name):
        self.name = name
        self.excl = False
        self.w = {}
        self.r = {}
        self.dsem = None


class Tile:
    __slots__ = ("t", "res", "shape")

    def __init__(self, t, res, shape):
        self.t = t
        self.res = res
        self.shape = shape

    def __getitem__(self, idx):
        return self.t[idx]


class Prog:
    def __init__(self):
        self.nc = bass.Bass("TRN2", target_bir_lowering=False)
        nc = self.nc
        self.eng = dict(pe=nc.tensor, dve=nc.vector, act=nc.scalar, pool=nc.gpsimd, sp=nc.sync)
        self.st = contextlib.ExitStack()
        self.pst = None
        self.sem = {}
        self.cnt = {}
        self.seen = {k: {} for k in self.eng}
        for k in self.eng:
            self.sem[k] = self.st.enter_context(nc.semaphore("s_" + k))
            self.cnt[k] = 0
        self.ndsem = 0
        self.free_dsem = {}
        self.phase_dsem = []
        self.n_ins = 0
        self.n_wait = 0

    def dram(self, name, shape, dtype, kind="Internal"):
        return self.nc.dram_tensor(name, list(shape), dtype, kind=kind).ap()

    def tile(self, name, shape, dtype):
        st = self.pst if self.pst is not None else self.st
        self.n_tiles = getattr(self, "n_tiles", 0) + 1
        name = "%s_%d" % (name, self.n_tiles)
        t = st.enter_context(self.nc.sbuf_tensor(name, list(shape), dtype))
        return Tile(t, Res(name), shape)

    def psum(self, name, shape, dtype=F32):
        t = self.st.enter_context(self.nc.psum_tensor(name, list(shape), dtype))
        r = Res(name)
        r.excl = True
        return Tile(t, r, shape)

    def res(self, name):
        return Res(name)

    @contextlib.contextmanager
    def phase(self):
        assert self.pst is None
        self.pst = contextlib.ExitStack()
        self.phase_dsem = []
        try:
            yield
        finally:
            self.barrier()
            self.pst.close()
            self.pst = None
            for q_, k_ in self.phase_dsem:
                self.free_dsem.setdefault(q_, []).append(k_)
            self.phase_dsem = []

    def barrier(self):
        for e in self.eng:
            deps = {}
            for k, c in self.cnt.items():
                if c > 0 and k != e:
                    deps[k] = c
            self._wait(e, deps)

    @staticmethod
    def _r(x):
        return x.res if isinstance(x, Tile) else x

    def _deps(self, reads, writes, skip_waw_key=None, partial=False):
        deps = {}

        def add(k, c):
            if deps.get(k, 0) < c:
                deps[k] = c
        for r in reads:
            r = self._r(r)
            for k, c in r.w.items():
                add(k, c)
            if r.excl:
                for k, c in r.r.items():
                    add(k, c)
        for w in writes:
            w = self._r(w)
            if not partial:
                for k, c in w.w.items():
                    if skip_waw_key is not None and k == skip_waw_key and not w.r:
                        continue
                    add(k, c)
            for k, c in w.r.items():
                add(k, c)
        return deps

    def _wait(self, e, deps):
        seen = self.seen[e]
        for k, c in deps.items():
            if k == e and e not in SELF_SYNC:
                continue
            if seen.get(k, 0) >= c:
                continue
            self.eng[e].wait_ge(self.sem[k], c)
            self.n_wait += 1
            seen[k] = c

    def _mark(self, tok, reads, writes, partial=False):
        k, c = tok
        for r in reads:
            r = self._r(r)
            if r.r.get(k, 0) < c:
                r.r[k] = c
        for w in writes:
            w = self._r(w)
            if partial:
                w.w[k] = c
            else:
                w.w = {k: c}
                w.r = {}

    def op(self, e, fn, reads=(), writes=()):
        self._wait(e, self._deps(reads, writes))
        ins = fn(self.eng[e])
        self.cnt[e] += 1
        ins.then_inc(self.sem[e], 1)
        self._mark((e, self.cnt[e]), reads, writes)
        self.n_ins += 1
        return ins

    def dma(self, q, out, in_, reads=(), writes=(), owner=None, partial=False, **kw):
        owner = self._r(owner)
        if owner.dsem is None:
            owner.dsem = {}
        if q not in owner.dsem:
            fl = self.free_dsem.setdefault(q, [])
            if fl:
                key = fl.pop()
            else:
                key = "d%d" % self.ndsem
                self.ndsem += 1
                self.sem[key] = self.st.enter_context(self.nc.semaphore("s_" + key))
                self.cnt[key] = 0
            owner.dsem[q] = key
            if self.pst is not None:
                self.phase_dsem.append((q, key))
        key = owner.dsem[q]
        self._wait(q, self._deps(reads, writes, skip_waw_key=key, partial=partial))
        ins = self.eng[q].dma_start(out=out, in_=in_, **kw)
        self.cnt[key] += 16
        ins.then_inc(self.sem[key], 16)
        self._mark((key, self.cnt[key]), reads, writes, partial=partial)
        self.n_ins += 1
        return ins

    def mm(self, out, lhsT, rhs, start, stop, R, W):
        return self.op("pe", lambda e: e.matmul(out, lhsT=lhsT, rhs=rhs, start=start, stop=stop), reads=R, writes=W)

    def act(self, out, in_, func, R, W, **kw):
        return self.op("act", lambda e: e.activation(out=out, in_=in_, func=func, **kw), reads=R, writes=W)

    def tt(self, eng, out, in0, in1, op, R, W):
        return self.op(eng, lambda e: e.tensor_tensor(out=out, in0=in0, in1=in1, op=op), reads=R, writes=W)

    def ts(self, eng, out, in0, s1, s2, op0, op1, R, W):
        if s2 is None:
            return self.op(eng, lambda e: e.tensor_scalar(out=out, in0=in0, scalar1=s1, scalar2=None, op0=op0), reads=R, writes=W)
        return self.op(eng, lambda e: e.tensor_scalar(out=out, in0=in0, scalar1=s1, scalar2=s2, op0=op0, op1=op1), reads=R, writes=W)

    def stt(self, eng, out, in0, scalar, in1, op0, op1, R, W):
        return self.op(eng, lambda e: e.scalar_tensor_tensor(out=out, in0=in0, scalar=scalar, in1=in1, op0=op0, op1=op1), reads=R, writes=W)

    def copy(self, eng, out, in_, R, W):
        if eng == "act":
            return self.op("act", lambda e: e.activation(out=out, in_=in_, func=AF.Identity), reads=R, writes=W)
        return self.op(eng, lambda e: e.tensor_copy(out=out, in_=in_), reads=R, writes=W)

    def finish(self):
        deps = {k: c for k, c in self.cnt.items() if c > 0 and k != "sp"}
        self._wait("sp", deps)
        self.st.close()
        return self.nc


import ml_dtypes

NT = 4352
NCTX = 256
NLAT = 4096
EPS = 1e-6
TT = [(0, 256)] + [(256 + 512 * i, 512) for i in range(8)]
COMPS = ["Aq", "Ak", "Av", "Az", "As", "Bq", "Bk", "Bv", "Bz", "Cq", "Ck", "Cv", "Cz", "Dx", "Dz"]
CW = {c: (16 if c == "As" else 512) for c in COMPS}
COFF = {}
_o = 0
for _c in COMPS:
    COFF[_c] = _o
    _o += CW[_c]
NCOLS = _o
FM = ["Aq", "Ak", "Av", "Az", "Bq", "Bk", "Bz", "Cq", "Ck", "Cz", "Dx", "Dz"]
FMI = {c: i for i, c in enumerate(FM)}
TM = ["Bv", "Cv"]
TMI = {c: i for i, c in enumerate(TM)}


def alloc_psum(P):
    return [P.psum("psb%d" % i, [128, 512]) for i in range(8)]


def phase_norm(P, PS, xin, normg, modv, ident_d, hT, hT_res):
    with P.phase():
        ident = P.tile("ident", [128, 128], F32)
        P.dma("sp", ident[:, :], ident_d, writes=[ident], owner=ident)
        ng = P.tile("ng", [128, 32], F32)
        mv = P.tile("mv", [128, 6 * 32], F32)
        P.dma("sp", ng[:, :], normg, writes=[ng], owner=ng)
        P.dma("sp", mv[:, :], modv, writes=[mv], owner=mv)
        AB = P.tile("AB", [128, 4 * 32], F32)
        for j, (sci, shi) in enumerate(((4, 3), (1, 0))):
            P.ts("dve", AB[:, (2 * j) * 32:(2 * j + 1) * 32], mv[:, sci * 32:(sci + 1) * 32], 1.0, None, ALU.add, None, [mv], [AB])
            P.tt("dve", AB[:, (2 * j) * 32:(2 * j + 1) * 32], AB[:, (2 * j) * 32:(2 * j + 1) * 32], ng[:, :], ALU.mult, [AB, ng], [AB])
            P.copy("dve", AB[:, (2 * j + 1) * 32:(2 * j + 2) * 32], mv[:, shi * 32:(shi + 1) * 32], [mv], [AB])
        xt = [P.tile("xt%d" % i, [128, 4096], F32) for i in range(2)]
        sq = P.tile("sq", [128, 4096], BF16)
        ss = [P.tile("ss%d" % i, [128, 2], F32) for i in range(2)]
        dg = [P.tile("dg%d" % i, [128, 128], F32) for i in range(2)]
        ht = [P.tile("ht%d" % i, [128, 32, 128], BF16) for i in range(2)]
        for i in range(34):
            x = xt[i % 2]
            s = ss[i % 2]
            d = dg[i % 2]
            h = ht[i % 2]
            P.dma("sp" if i % 2 == 0 else "pool", x[:, :], xin[i * 128:(i + 1) * 128, :], writes=[x], owner=x)
            P.act(sq[:, :], x[:, :], AF.Square, [x], [sq])
            P.op("dve", lambda e: e.reduce_sum(out=s[:, 0:1], in_=sq[:, :], axis=AX.X), reads=[sq], writes=[s])
            P.act(s[:, 1:2], s[:, 0:1], AF.Sqrt, [s], [s], scale=1.0 / 4096, bias=EPS)
            P.op("dve", lambda e: e.reciprocal(out=s[:, 1:2], in_=s[:, 1:2]), reads=[s], writes=[s])
            P.ts("dve", d[:, :], ident[:, :], s[:, 1:2], None, ALU.mult, None, [ident, s], [d])
            j = 0 if i < 2 else 1
            for cg in range(8):
                ps = PS[cg % 8]
                for cc in range(4):
                    c = cg * 4 + cc
                    P.mm(ps[:, cc * 128:(cc + 1) * 128], x[:, c * 128:(c + 1) * 128], d[:, :], True, True, [x, d], [ps])
                for cc in range(4):
                    c = cg * 4 + cc
                    A = AB[:, (2 * j) * 32 + c:(2 * j) * 32 + c + 1]
                    Bv = AB[:, (2 * j + 1) * 32 + c:(2 * j + 1) * 32 + c + 1]
                    if cg % 2 == 0:
                        P.act(h[:, c, :], ps[:, cc * 128:(cc + 1) * 128], AF.Identity, [ps, AB], [h], scale=A, bias=Bv)
                    else:
                        P.ts("dve", h[:, c, :], ps[:, cc * 128:(cc + 1) * 128], A, Bv, ALU.mult, ALU.add, [ps, AB], [h])
            P.dma("pool" if i % 2 == 0 else "sp", hT[:, :, i * 128:(i + 1) * 128].rearrange("c p t -> p c t"), h[:, :, :],
                  reads=[h], writes=[hT_res[i]], owner=h)


def phase_inproj(P, PS, w_in, hT, hT_res, pT, pT_res, pv, pv_res, psm, psm_res, comps=None):
    wv = w_in.rearrange("(c p) n -> p c n", p=128)
    with P.phase():
        stg = [P.tile("stg%d" % i, [128, 8, 512], F32) for i in range(2)]
        wb = [P.tile("wb%d" % i, [128, 32, 512], BF16) for i in range(2)]
        hts = [P.tile("hts%d" % i, [128, 32, 512], BF16) for i in range(2)]
        ot = [P.tile("ot%d" % i, [128, 512], F32) for i in range(4)]
        ob = [P.tile("ob%d" % i, [128, 512], BF16) for i in range(2)]
        nst = 0
        nht = 0
        no = 0
        nob = 0
        nps = 0
        for ci, comp in enumerate(comps or COMPS):
            w = wb[ci % 2]
            cw = CW[comp]
            for g in range(4):
                s = stg[nst % 2]
                nst += 1
                P.dma("sp", s[:, :, 0:cw], wv[:, g * 8:(g + 1) * 8, COFF[comp]:COFF[comp] + cw], writes=[s], owner=s)
                P.copy("pool" if g % 2 == 0 else "dve", w[:, g * 8:(g + 1) * 8, 0:cw], s[:, :, 0:cw], [s], [w])
            for ti, (t0, nt) in enumerate(TT):
                h = hts[nht % 2]
                nht += 1
                P.dma("pool", h[:, :, 0:nt], hT[:, :, t0:t0 + nt].rearrange("c p t -> p c t"),
                      reads=[hT_res[t0 // 128 + k] for k in range(nt // 128)], writes=[h], owner=h)
                if comp in FMI:
                    for hd in range(4):
                        ps = PS[nps % 8]
                        nps += 1
                        for c in range(32):
                            P.mm(ps[:, 0:nt], w[:, c, hd * 128:(hd + 1) * 128], h[:, c, 0:nt], c == 0, c == 31, [w, h], [ps])
                        o = ot[no % 4]
                        no += 1
                        if comp.endswith("z"):
                            P.act(o[:, 0:nt], ps[:, 0:nt], AF.Silu, [ps], [o])
                        elif no % 2 == 0:
                            P.copy("act", o[:, 0:nt], ps[:, 0:nt], [ps], [o])
                        else:
                            P.copy("dve", o[:, 0:nt], ps[:, 0:nt], [ps], [o])
                        P.dma("sp", pT[FMI[comp], hd, :, t0:t0 + nt], o[:, 0:nt], reads=[o], writes=[pT_res[FMI[comp]][hd]],
                              owner=o, partial=True)
                else:
                    for sidx in range(nt // 128):
                        ps = PS[nps % 8]
                        nps += 1
                        for c in range(32):
                            P.mm(ps[:, 0:cw], h[:, c, sidx * 128:(sidx + 1) * 128], w[:, c, 0:cw], c == 0, c == 31, [w, h], [ps])
                        r0 = t0 + sidx * 128
                        if comp == "As":
                            o = ot[no % 4]
                            no += 1
                            P.copy("dve", o[:, 0:16], ps[:, 0:16], [ps], [o])
                            P.dma("sp", psm[r0:r0 + 128, :], o[:, 0:16], reads=[o], writes=[psm_res], owner=o, partial=True)
                        else:
                            o = ob[nob % 2]
                            nob += 1
                            P.copy("act" if nob % 2 else "dve", o[:, :], ps[:, :], [ps], [o])
                            P.dma("sp", pv[TMI[comp], r0:r0 + 128, :], o[:, :], reads=[o], writes=[pv_res[TMI[comp]]], owner=o, partial=True)


def phase_lru(P, PS, pT, pT_res, oT, oT_res, lru_conv, lru_conv_b, lru_wg, lru_bg, lru_lam):
    with P.phase():
        cw = P.tile("cw", [128, 16], F32)
        cb = P.tile("cb", [128, 4], F32)
        bg = P.tile("bg", [128, 16], F32)
        lam = P.tile("lam", [128, 8], F32)
        c1 = P.tile("c1", [128, 8], F32)
        P.dma("sp", cw[:, :], lru_conv, writes=[cw], owner=cw)
        P.dma("sp", cb[:, :], lru_conv_b, writes=[cb], owner=cb)
        P.dma("sp", bg[:, :], lru_bg, writes=[bg], owner=bg)
        P.dma("sp", lam[:, :], lru_lam, writes=[lam], owner=lam)
        P.act(c1[:, :], lam[:, :], AF.Exp, [lam], [c1], scale=-1.0)
        P.act(c1[:, :], c1[:, :], AF.Ln, [c1], [c1], bias=1.0)
        P.ts("dve", c1[:, :], c1[:, :], -8.0, None, ALU.mult, None, [c1], [c1])
        wgs = P.tile("wgs", [128, 128], F32)
        wgb = [P.tile("wgb%d" % i, [128, 128], BF16) for i in range(4)]
        x = P.tile("x", [128, NT], F32)
        z = P.tile("z", [128, NT], F32)
        u = P.tile("u", [128, NT], F32)
        ub = P.tile("ub", [128, NT], BF16)
        r = P.tile("r", [128, NT], F32)
        ig = P.tile("ig", [128, NT], F32)
        hs = P.tile("hs", [128, NT], F32)
        hsum = P.tile("hsum", [128, NT], F32)
        ob = P.tile("ob", [128, NT], BF16)
        segs = [(0, NCTX), (NCTX, NT)]
        nps = 0
        for hd in range(4):
            P.dma("sp", x[:, :], pT[FMI["Dx"], hd], reads=[pT_res[FMI["Dx"]][hd]], writes=[x], owner=x)
            P.dma("pool", z[:, :], pT[FMI["Dz"], hd], reads=[pT_res[FMI["Dz"]][hd]], writes=[z], owner=z)
            P.act(u[:, :], x[:, :], AF.Identity, [x, cw, cb], [u], scale=cw[:, hd * 4 + 2:hd * 4 + 3], bias=cb[:, hd:hd + 1])
            for (s, e) in segs:
                P.stt("dve", u[:, s + 2:e], x[:, s:e - 2], cw[:, hd * 4 + 0:hd * 4 + 1], u[:, s + 2:e], ALU.mult, ALU.add, [x, cw, u], [u])
                P.stt("dve", u[:, s + 1:e], x[:, s:e - 1], cw[:, hd * 4 + 1:hd * 4 + 2], u[:, s + 1:e], ALU.mult, ALU.add, [x, cw, u], [u])
                P.stt("dve", u[:, s:e - 1], x[:, s + 1:e], cw[:, hd * 4 + 3:hd * 4 + 4], u[:, s:e - 1], ALU.mult, ALU.add, [x, cw, u], [u])
            P.copy("pool", ub[:, :], u[:, :], [u], [ub])
            for d in range(2):
                for g in range(2):
                    P.dma("sp", wgs[:, :], lru_wg[d, g, hd], writes=[wgs], owner=wgs)
                    P.copy("dve", wgb[d * 2 + g][:, :], wgs[:, :], [wgs], [wgb[d * 2 + g]])
            for d in range(2):
                col = d * 4 + hd
                for g in range(2):
                    dst = r if g == 0 else ig
                    bcol = (d * 2 + g) * 4 + hd
                    for (t0, nt) in TT:
                        ps = PS[nps % 8]
                        nps += 1
                        P.mm(ps[:, 0:nt], wgb[d * 2 + g][:, :], ub[:, t0:t0 + nt], True, True, [wgb[d * 2 + g], ub], [ps])
                        P.act(dst[:, t0:t0 + nt], ps[:, 0:nt], AF.Sigmoid, [ps, bg], [dst], bias=bg[:, bcol:bcol + 1])
                P.act(r[:, :], r[:, :], AF.Exp, [r, c1], [r], scale=c1[:, col:col + 1])
                P.tt("dve", ig[:, :], ig[:, :], u[:, :], ALU.mult, [ig, u], [ig])
                P.tt("pool", hs[:, :], r[:, :], r[:, :], ALU.mult, [r], [hs])
                P.ts("dve", hs[:, :], hs[:, :], 1.0, None, ALU.min, None, [hs], [hs])
                P.act(hs[:, :], hs[:, :], AF.Sqrt, [hs], [hs], scale=-1.0, bias=1.0)
                P.tt("dve", ig[:, :], ig[:, :], hs[:, :], ALU.mult, [ig, hs], [ig])
                if d == 0:
                    P.op("dve", lambda e: e.tensor_tensor_scan(out=hsum[:, :], data0=r[:, :], data1=ig[:, :], initial=0.0, op0=ALU.mult, op1=ALU.add),
                         reads=[r, ig], writes=[hsum])
                else:
                    P.op("dve", lambda e: e.tensor_tensor_scan(out=hs[:, 0:NCTX][:, ::-1], data0=r[:, 0:NCTX][:, ::-1], data1=ig[:, 0:NCTX][:, ::-1],
                                                               initial=0.0, op0=ALU.mult, op1=ALU.add), reads=[r, ig], writes=[hs])
                    P.op("dve", lambda e: e.tensor_tensor_scan(out=hs[:, NCTX:NT][:, ::-1], data0=r[:, NCTX:NT][:, ::-1], data1=ig[:, NCTX:NT][:, ::-1],
                                                               initial=hs[:, 0:1], op0=ALU.mult, op1=ALU.add), reads=[r, ig, hs], writes=[hs])
                    P.tt("dve", hsum[:, :], hsum[:, :], hs[:, :], ALU.add, [hsum, hs], [hsum])
            P.tt("dve", ob[:, :], hsum[:, :], z[:, :], ALU.mult, [hsum, z], [ob])
            P.dma("sp", oT[12 + hd], ob[:, :], reads=[ob], writes=[oT_res[12 + hd]], owner=ob)


def phase_outproj(P, PS, oTf, oTf_res, w_out, xh, gtl, gtc, xo, nctx_tiles, xo_map=None):
    ntok = xh.shape[0]
    ntiles = ntok // 128
    wv = w_out.rearrange("(c p) n -> p c n", p=128)
    oreads = list(oTf_res) if isinstance(oTf_res, (list, tuple)) else [oTf_res]
    with P.phase():
        gt = [P.tile("gt%d" % i, [128, 4096], F32) for i in range(2)]
        P.dma("sp", gt[0][:, :], gtc.broadcast_to([128, 4096]), writes=[gt[0]], owner=gt[0])
        P.dma("sp", gt[1][:, :], gtl.broadcast_to([128, 4096]), writes=[gt[1]], owner=gt[1])
        stg = [P.tile("stg%d" % i, [128, 8, 512], F32) for i in range(2)]
        wb = [P.tile("wb%d" % i, [128, 32, 512], BF16) for i in range(2)]
        ot = [P.tile("ot%d" % i, [128, 32, 128], BF16) for i in range(3)]
        xt = [P.tile("xt%d" % i, [128, 512], F32) for i in range(3)]
        yt = [P.tile("yt%d" % i, [128, 512], F32) for i in range(3)]
        nst = 0
        n = 0
        for cb in range(8):
            w = wb[cb % 2]
            for g in range(4):
                s = stg[nst % 2]
                nst += 1
                P.dma("sp", s[:, :, :], wv[:, g * 8:(g + 1) * 8, cb * 512:(cb + 1) * 512], writes=[s], owner=s)
                P.copy("pool" if g % 2 == 0 else "act", w[:, g * 8:(g + 1) * 8, :], s[:, :, :], [s], [w])
            for ti in range(ntiles):
                dst = xo_map(ti) if xo_map is not None else xo[ti * 128:(ti + 1) * 128, :]
                if dst is None:
                    continue
                o = ot[n % 3]
                x = xt[n % 3]
                y = yt[n % 3]
                ps = PS[n % 8]
                n += 1
                P.dma("pool", o[:, :, :], oTf[:, :, ti * 128:(ti + 1) * 128].rearrange("c p t -> p c t"), reads=oreads, writes=[o], owner=o)
                P.dma("pool", x[:, :], xh[ti * 128:(ti + 1) * 128, cb * 512:(cb + 1) * 512], writes=[x], owner=x)
                for c in range(32):
                    P.mm(ps[:, :], o[:, c, :], w[:, c, :], c == 0, c == 31, [o, w], [ps])
                g_ = gt[0] if ti < nctx_tiles else gt[1]
                P.tt("dve", y[:, :], ps[:, :], g_[:, cb * 512:(cb + 1) * 512], ALU.mult, [ps, g_], [y])
                P.tt("dve", y[:, :], y[:, :], x[:, :], ALU.add, [y, x], [y])
                P.dma("sp", dst[:, cb * 512:(cb + 1) * 512], y[:, :], reads=[y], owner=y)


NEG = -30000.0
GRID_W = 64


def rmsnorm_fm(P, PS, nps, src, dst_f32, ones_t, gcol, ndim, tmp_sq, tmp_r, t0, nt, gt=()):
    ps = PS[nps % 8]
    P.act(tmp_sq[:, 0:nt], src[:, t0:t0 + nt], AF.Square, [src], [tmp_sq])
    P.mm(ps[:, 0:nt], ones_t[:, :], tmp_sq[:, 0:nt], True, True, [ones_t, tmp_sq], [ps])
    P.act(tmp_r[:, 0:nt], ps[:, 0:nt], AF.Sqrt, [ps], [tmp_r], scale=1.0 / ndim, bias=EPS)
    P.op("dve", lambda e: e.reciprocal(out=tmp_r[:, 0:nt], in_=tmp_r[:, 0:nt]), reads=[tmp_r], writes=[tmp_r])
    P.stt("dve", dst_f32[:, t0:t0 + nt], src[:, t0:t0 + nt], gcol, tmp_r[:, 0:nt], ALU.mult, ALU.mult, [src, tmp_r] + list(gt), [dst_f32])


def phase_diff(P, PS, pT, pT_res, pv, pv_res, oT, oT_res, C, dq_g, dk_g, dlam, dsub, laminit):
    with P.phase():
        blk = P.tile("blk", [128, 128], F32)
        onesf = P.tile("onesf", [128, 128], F32)
        onesb = P.tile("onesb", [128, 128], BF16)
        rot = P.tile("rot", [128, 128], F32)
        cos = P.tile("cos", [128, NT], F32)
        sin = P.tile("sin", [128, NT], F32)
        for t, nm in ((blk, "blk64"), (onesf, "ones"), (rot, "rot"), (cos, "cos"), (sin, "sin")):
            P.dma("sp", t[:, :], C[nm], writes=[t], owner=t)
        P.copy("dve", onesb[:, :], onesf[:, :], [onesf], [onesb])
        gq = P.tile("gq", [128, 1], F32)
        gk = P.tile("gk", [128, 1], F32)
        sub = P.tile("sub", [128, 1], F32)
        P.dma("sp", gq[:, :], dq_g, writes=[gq], owner=gq)
        P.dma("sp", gk[:, :], dk_g, writes=[gk], owner=gk)
        P.dma("sp", sub[:, :], dsub, writes=[sub], owner=sub)
        P.ts("dve", gq[:, :], gq[:, :], 0.125, None, ALU.mult, None, [gq], [gq])
        li = P.tile("li", [128, 2], F32)
        P.dma("sp", li[:, :], laminit, writes=[li], owner=li)
        P.ts("dve", sub[:, :], sub[:, :], li[:, 1:2], None, ALU.mult, None, [sub, li], [sub])
        lv = P.tile("lv", [128, 256], F32)
        lt = P.tile("lt", [128, 4], F32)
        P.dma("sp", lv[:, :], dlam.broadcast_to([128, 256]), writes=[lv], owner=lv)
        P.tt("dve", lv[:, 0:64], lv[:, 0:64], lv[:, 64:128], ALU.mult, [lv], [lv])
        P.tt("dve", lv[:, 128:192], lv[:, 128:192], lv[:, 192:256], ALU.mult, [lv], [lv])
        P.op("dve", lambda e: e.reduce_sum(out=lt[:, 0:1], in_=lv[:, 0:64], axis=AX.X), reads=[lv], writes=[lt])
        P.op("dve", lambda e: e.reduce_sum(out=lt[:, 1:2], in_=lv[:, 128:192], axis=AX.X), reads=[lv], writes=[lt])
        P.act(lt[:, 0:2], lt[:, 0:2], AF.Exp, [lt], [lt])
        P.tt("dve", lt[:, 2:3], lt[:, 1:2], lt[:, 0:1], ALU.subtract, [lt], [lt])
        P.ts("dve", lt[:, 3:4], lt[:, 2:3], li[:, 0:1], None, ALU.add, None, [lt, li], [lt])
        lamneg = lt[:, 3:4]

        xq = P.tile("xq", [128, NT], F32)
        xk = P.tile("xk", [128, NT], F32)
        z = P.tile("z", [128, NT], F32)
        qb = P.tile("qb", [128, NT], BF16)
        kb = P.tile("kb", [128, NT], BF16)
        V = P.tile("V", [128, 34, 128], BF16)
        tsq = P.tile("tsq", [128, 512], F32)
        tr = P.tile("tr", [128, 512], F32)
        t1 = P.tile("t1", [128, 512], F32)
        t2 = P.tile("t2", [128, 512], F32)
        pt = [P.tile("pt%d" % i, [128, 2, 512], BF16) for i in range(3)]
        r0 = P.tile("r0", [128, 512], F32)
        r1 = P.tile("r1", [128, 512], F32)
        o0 = P.tile("o0", [128, 512], F32)
        o1 = P.tile("o1", [128, 512], F32)
        obt = [P.tile("obt%d" % i, [128, 512], BF16) for i in range(2)]
        nps = 0
        npt = 0
        nob = 0
        for hd in range(4):
            P.dma("sp", xq[:, :], pT[FMI["Cq"], hd], reads=[pT_res[FMI["Cq"]][hd]], writes=[xq], owner=xq)
            P.dma("pool", xk[:, :], pT[FMI["Ck"], hd], reads=[pT_res[FMI["Ck"]][hd]], writes=[xk], owner=xk)
            P.dma("sp", z[:, :], pT[FMI["Cz"], hd], reads=[pT_res[FMI["Cz"]][hd]], writes=[z], owner=z)
            P.dma("pool", V[:, :, :], pv[TMI["Cv"], :, hd * 128:(hd + 1) * 128].rearrange("(c p) d -> p c d", p=128),
                  reads=[pv_res[TMI["Cv"]]], writes=[V], owner=V)
            for (src, g, dst) in ((xq, gq, qb), (xk, gk, kb)):
                for (t0, nt) in TT:
                    rmsnorm_fm(P, PS, nps, src, src, blk, g[:, 0:1], 64, tsq, tr, t0, nt, gt=[g])
                    nps += 1
                    ps = PS[nps % 8]
                    nps += 1
                    P.mm(ps[:, 0:nt], rot[:, :], src[:, t0:t0 + nt], True, True, [rot, src], [ps])
                    P.tt("pool", t1[:, 0:nt], src[:, t0:t0 + nt], cos[:, t0:t0 + nt], ALU.mult, [src, cos], [t1])
                    P.tt("dve", t2[:, 0:nt], ps[:, 0:nt], sin[:, t0:t0 + nt], ALU.mult, [ps, sin], [t2])
                    P.tt("dve", dst[:, t0:t0 + nt], t1[:, 0:nt], t2[:, 0:nt], ALU.add, [t1, t2], [dst])
            for (q0, nq) in TT:
                chunks = [0, 1] if q0 == 0 else list(range(34))
                O0, O1, S0, S1 = PS[4], PS[5], PS[6], PS[7]
                def scores(ci_):
                    kc_ = chunks[ci_]
                    a_ = (ci_ % 2) * 2
                    for m in range(2):
                        P.mm(PS[a_ + m][:, 0:nq], kb[64 * m:64 * m + 64, kc_ * 128:(kc_ + 1) * 128], qb[64 * m:64 * m + 64, q0:q0 + nq],
                             True, True, [kb, qb], [PS[a_ + m]])
                scores(0)
                for ci, kc in enumerate(chunks):
                    a = (ci % 2) * 2
                    p_ = pt[npt % 3]
                    npt += 1
                    if ci + 1 < len(chunks):
                        scores(ci + 1)
                    for m in range(2):
                        P.act(p_[:, m, 0:nq], PS[a + m][:, 0:nq], AF.Exp, [PS[a + m]], [p_])
                    st = ci == 0
                    sp_ = ci == len(chunks) - 1
                    P.mm(O0[:, 0:nq], V[:, kc, :], p_[:, 0, 0:nq], st, sp_, [V, p_], [O0])
                    P.mm(O1[:, 0:nq], V[:, kc, :], p_[:, 1, 0:nq], st, sp_, [V, p_], [O1])
                    P.mm(S0[:, 0:nq], onesb[:, :], p_[:, 0, 0:nq], st, sp_, [onesb, p_], [S0])
                    P.mm(S1[:, 0:nq], onesb[:, :], p_[:, 1, 0:nq], st, sp_, [onesb, p_], [S1])
                P.op("dve", lambda e: e.reciprocal(out=r0[:, 0:nq], in_=S0[:, 0:nq]), reads=[S0], writes=[r0])
                P.op("dve", lambda e: e.reciprocal(out=r1[:, 0:nq], in_=S1[:, 0:nq]), reads=[S1], writes=[r1])
                P.tt("dve", o0[:, 0:nq], O0[:, 0:nq], r0[:, 0:nq], ALU.mult, [O0, r0], [o0])
                P.stt("dve", o1[:, 0:nq], O1[:, 0:nq], lamneg, r1[:, 0:nq], ALU.mult, ALU.mult, [O1, r1, lt], [o1])
                P.tt("dve", o0[:, 0:nq], o0[:, 0:nq], o1[:, 0:nq], ALU.add, [o0, o1], [o0])
                rmsnorm_fm(P, PS, 0, o0, o0, onesf, sub[:, 0:1], 128, tsq, tr, 0, nq, gt=[sub])
                ob = obt[nob % 2]
                nob += 1
                P.tt("dve", ob[:, 0:nq], o0[:, 0:nq], z[:, q0:q0 + nq], ALU.mult, [o0, z], [ob])
                P.dma("sp", oT[8 + hd][:, q0:q0 + nq], ob[:, 0:nq], reads=[ob], writes=[oT_res[8 + hd]], owner=ob, partial=True)


def na_tables():
    rows = 64
    def r0(r):
        return min(max(r - 4, 0), rows - 8)
    def c0(c):
        return min(max(c - 8, 0), GRID_W - 16)
    cfgs = {}
    pairs = []
    idx_dr, idx_dc, masks = [], [], []
    for pr in range(32):
        lo = min(r0(2 * pr), r0(2 * pr + 1))
        hi = max(r0(2 * pr), r0(2 * pr + 1)) + 8
        chunks = list(range(lo // 2, (hi + 1) // 2))
        assert len(chunks) <= 5
        DR = np.zeros((128, 640), np.int64)
        DC = np.zeros((128, 640), np.int64)
        M = np.full((128, 640), NEG, np.float32)
        for j, cj in enumerate(chunks):
            for krl in range(2):
                for qrl in range(2):
                    kr = 2 * cj + krl
                    qr = 2 * pr + qrl
                    rv = r0(qr) <= kr < r0(qr) + 8
                    for kc in range(64):
                        for qc in range(64):
                            cv = c0(qc) <= kc < c0(qc) + 16
                            p = krl * 64 + kc
                            q = j * 128 + qrl * 64 + qc
                            if rv and cv:
                                DR[p, q] = kr - qr + 7
                                DC[p, q] = kc - qc + 15
                                M[p, q] = 0.0
        key = (tuple(c - pr for c in chunks), DR.tobytes(), M.tobytes())
        if key not in cfgs:
            cfgs[key] = len(idx_dr)
            idx_dr.append(DR)
            idx_dc.append(DC)
            masks.append(M)
        pairs.append((cfgs[key], chunks))
    return pairs, np.stack(idx_dr), np.stack(idx_dc), np.stack(masks)


def phase_na(P, PS, pT, pT_res, pv, pv_res, oT, oT_res, C, nq_g, nk_g, nab, pairs, ncfg):
    with P.phase():
        onesf = P.tile("onesf", [128, 128], F32)
        onesb = P.tile("onesb", [128, 128], BF16)
        P.dma("sp", onesf[:, :], C["ones"], writes=[onesf], owner=onesf)
        P.copy("dve", onesb[:, :], onesf[:, :], [onesf], [onesb])
        gq = P.tile("gq", [128, 1], F32)
        gk = P.tile("gk", [128, 1], F32)
        P.dma("sp", gq[:, :], nq_g, writes=[gq], owner=gq)
        P.dma("sp", gk[:, :], nk_g, writes=[gk], owner=gk)
        P.ts("dve", gq[:, :], gq[:, :], 128.0 ** -0.5, None, ALU.mult, None, [gq], [gq])
        msk = P.tile("msk", [128, ncfg, 640], F32)
        for c in range(ncfg):
            P.dma("sp", msk[:, c, :], C["namask"][c], writes=[msk], owner=msk)
        bias = P.tile("bias", [128, ncfg, 640], F32)
        xq = P.tile("xq", [128, NT], F32)
        xk = P.tile("xk", [128, NT], F32)
        z = P.tile("z", [128, NT], F32)
        qb = P.tile("qb", [128, NT], BF16)
        kb = P.tile("kb", [128, NT], BF16)
        V = P.tile("V", [128, 34, 128], BF16)
        tsq = P.tile("tsq", [128, 512], F32)
        tr = P.tile("tr", [128, 512], F32)
        sb = [P.tile("sb%d" % i, [128, 640], F32) for i in range(2)]
        pt = [P.tile("pt%d" % i, [128, 896], BF16) for i in range(2)]
        rr = [P.tile("rr%d" % i, [128, 256], F32) for i in range(2)]
        oo = [P.tile("oo%d" % i, [128, 256], F32) for i in range(2)]
        obuf = [P.tile("obuf%d" % i, [128, NT], BF16) for i in range(2)]
        nps = 0
        it = 0
        for hd in range(4):
            P.dma("sp", xq[:, :], pT[FMI["Bq"], hd], reads=[pT_res[FMI["Bq"]][hd]], writes=[xq], owner=xq)
            P.dma("pool", xk[:, :], pT[FMI["Bk"], hd], reads=[pT_res[FMI["Bk"]][hd]], writes=[xk], owner=xk)
            P.dma("sp", z[:, :], pT[FMI["Bz"], hd], reads=[pT_res[FMI["Bz"]][hd]], writes=[z], owner=z)
            P.dma("pool", V[:, :, :], pv[TMI["Bv"], :, hd * 128:(hd + 1) * 128].rearrange("(c p) d -> p c d", p=128),
                  reads=[pv_res[TMI["Bv"]]], writes=[V], owner=V)
            for c in range(ncfg):
                P.dma("pool", bias[:, c, :], nab[hd, c], writes=[bias], owner=bias)
            P.tt("dve", bias[:, :, :], bias[:, :, :], msk[:, :, :], ALU.add, [bias, msk], [bias])
            for (src, g, dst) in ((xq, gq, qb), (xk, gk, kb)):
                for (t0, nt) in TT:
                    rmsnorm_fm(P, PS, nps, src, src, onesf, g[:, 0:1], 128, tsq, tr, t0, nt, gt=[g])
                    nps += 1
                    P.copy("pool", dst[:, t0:t0 + nt], src[:, t0:t0 + nt], [src], [dst])
            ob = obuf[hd % 2]
            A, B_, Cb = PS[0], PS[1], PS[2]
            p_ = pt[it % 2]; r_ = rr[it % 2]; o_ = oo[it % 2]
            it += 1
            for j in range(2):
                P.mm(A[:, j * 256:(j + 1) * 256], kb[:, j * 128:(j + 1) * 128], qb[:, 0:256], True, True, [kb, qb], [A])
            P.act(p_[:, 0:512], A[:, 0:512], AF.Exp, [A], [p_])
            for j in range(2):
                P.mm(Cb[:, 0:256], V[:, j, :], p_[:, j * 256:(j + 1) * 256], j == 0, j == 1, [V, p_], [Cb])
            for j in range(2):
                P.mm(Cb[:, 256:512], onesb[:, :], p_[:, j * 256:(j + 1) * 256], j == 0, j == 1, [onesb, p_], [Cb])
            P.op("dve", lambda e: e.reciprocal(out=r_[:, 0:256], in_=Cb[:, 256:512]), reads=[Cb], writes=[r_])
            P.tt("dve", o_[:, 0:256], Cb[:, 0:256], r_[:, 0:256], ALU.mult, [Cb, r_], [o_])
            P.tt("dve", ob[:, 0:256], o_[:, 0:256], z[:, 0:256], ALU.mult, [o_, z], [ob])
            def gen_pair(pr, cfg, chunks, bi):
                s3 = bi * 3
                A, B_, Cb = PS[s3], PS[s3 + 1], PS[s3 + 2]
                p_ = pt[bi]; r_ = rr[bi]; o_ = oo[bi]; s_ = sb[bi]
                q0 = NCTX + pr * 128
                nl = len(chunks)
                for j, cj in enumerate(chunks):
                    k0 = NCTX + cj * 128
                    dstp, off = (A, j * 128) if j < 4 else (B_, 0)
                    P.mm(dstp[:, off:off + 128], kb[:, k0:k0 + 128], qb[:, q0:q0 + 128], True, True, [kb, qb], [dstp])
                for j in range(2):
                    P.mm(B_[:, 128 + j * 128:256 + j * 128], kb[:, j * 128:(j + 1) * 128], qb[:, q0:q0 + 128], True, True, [kb, qb], [B_])
                yield
                na = min(nl, 4) * 128
                P.tt("dve", s_[:, 0:na], A[:, 0:na], bias[:, cfg, 0:na], ALU.add, [A, bias], [s_])
                if nl == 5:
                    P.tt("dve", s_[:, 512:640], B_[:, 0:128], bias[:, cfg, 512:640], ALU.add, [B_, bias], [s_])
                yield
                P.act(p_[:, 0:nl * 128], s_[:, 0:nl * 128], AF.Exp, [s_], [p_])
                P.act(p_[:, 640:896], B_[:, 128:384], AF.Exp, [B_], [p_])
                yield
                srcs = [(2 + cj, j * 128) for j, cj in enumerate(chunks)] + [(0, 640), (1, 768)]
                for n_, (vc, off) in enumerate(srcs):
                    P.mm(Cb[:, 0:128], V[:, vc, :], p_[:, off:off + 128], n_ == 0, n_ == len(srcs) - 1, [V, p_], [Cb])
                for n_, (vc, off) in enumerate(srcs):
                    P.mm(Cb[:, 128:256], onesb[:, :], p_[:, off:off + 128], n_ == 0, n_ == len(srcs) - 1, [onesb, p_], [Cb])
                yield
                P.op("dve", lambda e: e.reciprocal(out=r_[:, 0:128], in_=Cb[:, 128:256]), reads=[Cb], writes=[r_])
                P.tt("dve", o_[:, 0:128], Cb[:, 0:128], r_[:, 0:128], ALU.mult, [Cb, r_], [o_])
                yield
                P.tt("pool", ob[:, q0:q0 + 128], o_[:, 0:128], z[:, q0:q0 + 128], ALU.mult, [o_, z], [ob])
                yield

            for pr0 in range(0, len(pairs), 2):
                gens_ = [gen_pair(pr, pairs[pr][0], pairs[pr][1], k_) for k_, pr in enumerate(range(pr0, min(pr0 + 2, len(pairs))))]
                alive_ = gens_
                while alive_:
                    nx_ = []
                    for g_ in alive_:
                        try:
                            next(g_)
                            nx_.append(g_)
                        except StopIteration:
                            pass
                    alive_ = nx_
            P.dma("sp", oT[4 + hd], ob[:, :], reads=[ob], writes=[oT_res[4 + hd]], owner=ob)


import os
STAGE = int(os.environ.get('GDN_STAGE', '99'))
LEVELS = int(os.environ.get('GDN_LEVELS', '6'))
DIRS = int(os.environ.get('GDN_DIRS', '2'))


def gdn_consts():
    c = {}
    t = np.arange(128)
    c["U_f"] = (t[:, None] <= t[None, :]).astype(np.float32)
    c["U_b"] = (t[:, None] >= t[None, :]).astype(np.float32)
    c["mT_f"] = np.where(t[:, None] <= t[None, :], 0.0, NEG).astype(np.float32)
    c["mT_b"] = np.where(t[:, None] >= t[None, :], 0.0, NEG).astype(np.float32)
    c["st_f"] = (t[:, None] < t[None, :]).astype(np.float32)
    c["st_b"] = (t[:, None] > t[None, :]).astype(np.float32)

    def bm(b):
        return ((t[:, None] // b) == (t[None, :] // b)).astype(np.float32)
    c["bm16"] = bm(16)
    c["e32"] = bm(32) - bm(16)
    c["e64"] = bm(64) - bm(32)
    c["e128"] = bm(128) - bm(64)
    return c


def _lockstep(gens):
    alive = list(gens)
    while alive:
        nxt_ = []
        for g_ in alive:
            try:
                next(g_)
                nxt_.append(g_)
            except StopIteration:
                pass
        alive = nxt_


def phase_gdn(P, PS, pT, pT_res, psm, psm_res, oT, oT_res, C, g_conv, g_alog, g_dtb, g_norm, heads=range(4), nsteps=34):
    nps = [0]

    def bank():
        nps[0] += 1
        return PS[nps[0] % 8]

    with P.phase():
        cst = {}
        for nm in ("ident", "ones", "U_f", "U_b", "mT_f", "mT_b", "st_f", "st_b", "bm16", "e32", "e64", "e128"):
            cst[nm] = P.tile("c_" + nm, [128, 128], F32)
            P.dma("sp", cst[nm][:, :], C[nm], writes=[cst[nm]], owner=cst[nm])
        ident, ones = cst["ident"], cst["ones"]
        negones = P.tile("negones", [128, 128], F32)
        P.ts("dve", negones[:, :], ones[:, :], -1.0, None, ALU.mult, None, [ones], [negones])
        cw = P.tile("cw", [128, 48], F32)
        P.dma("sp", cw[:, :], g_conv, writes=[cw], owner=cw)
        gn = P.tile("gn", [128, 1], F32)
        P.dma("sp", gn[:, :], g_norm, writes=[gn], owner=gn)
        al = P.tile("al", [128, 8], F32)
        dtb = P.tile("dtb", [128, 8], F32)
        P.dma("sp", al[:, :], g_alog.broadcast_to([128, 8]), writes=[al], owner=al)
        P.dma("sp", dtb[:, :], g_dtb.broadcast_to([128, 8]), writes=[dtb], owner=dtb)
        P.act(al[:, :], al[:, :], AF.Exp, [al], [al])
        P.ts("dve", al[:, :], al[:, :], -1.0, None, ALU.mult, None, [al], [al])
        G = P.tile("G", [128, 34, 16], F32)
        P.dma("sp", G[:, :, :], psm.rearrange("(c p) k -> p c k", p=128), reads=[psm_res], writes=[G], owner=G)
        for j in range(8):
            P.ts("dve", G[:, :, j], G[:, :, j], dtb[:, j:j + 1], None, ALU.add, None, [G, dtb], [G])
        P.act(G[:, :, 0:8], G[:, :, 0:8], AF.Exp, [G], [G])
        P.act(G[:, :, 0:8], G[:, :, 0:8], AF.Ln, [G], [G], bias=1.0)
        for j in range(8):
            P.ts("dve", G[:, :, j], G[:, :, j], al[:, j:j + 1], None, ALU.mult, None, [G, al], [G])
        P.act(G[:, :, 8:16], G[:, :, 8:16], AF.Sigmoid, [G], [G])

        xr = P.tile("xr", [128, NT], F32)
        kn = P.tile("kn", [128, NT], F32)
        vn = P.tile("vn", [128, NT], F32)
        qnb = P.tile("qnb", [128, NT], BF16)
        knb = P.tile("knb", [128, NT], BF16)
        ktok = P.tile("ktok", [128, 34, 128], F32)
        vtok = P.tile("vtok", [128, 34, 128], F32)
        oacc = P.tile("oacc", [128, NT], F32)
        tsq = P.tile("tsq", [128, 512], F32)
        tr = P.tile("tr", [128, 512], F32)
        ob = P.tile("ob", [128, NT], BF16)
        segs = [(0, NCTX), (NCTX, NT)]
        NSET = 4
        W = []
        for i in range(NSET):
            w = {}
            for nm in ("gU", "Gt", "egb", "Gs", "Mt", "MtT", "Q", "QT", "Q2", "Q2T", "Pm", "PmT", "Y", "YT", "dg", "Xb", "kg", "u"):
                w[nm] = P.tile("%s%d" % (nm, i), [128, 128], F32)
            for nm in ("AT", "qg", "kd", "wT", "vnew"):
                w[nm] = P.tile("%s%d" % (nm, i), [128, 128], BF16)
            w["egc"] = P.tile("egc%d" % i, [128, 1], F32)
            W.append(w)
        St = [{"S": P.tile("S%d" % d, [128, 128], F32), "Sb": P.tile("Sb%d" % d, [128, 128], BF16)} for d in range(2)]

        def gen_ag(hd, d, n, w):
            sfx = "_f" if d == 0 else "_b"
            last = 127 if d == 0 else 0
            t0 = n * 128
            gcol = G[:, n, d * 4 + hd:d * 4 + hd + 1]
            bcol = G[:, n, 8 + d * 4 + hd:8 + d * 4 + hd + 1]
            P.ts("dve", w["gU"][:, :], cst["U" + sfx][:, :], gcol, None, ALU.mult, None, [cst["U" + sfx], G], [w["gU"]])
            yield
            pD = bank()
            P.mm(pD[:, 0:128], ones[:, :], w["gU"][:, :], True, False, [ones, w["gU"]], [pD])
            P.mm(pD[:, 0:128], w["gU"][:, :], negones[:, :], False, False, [w["gU"], negones], [pD])
            P.mm(pD[:, 0:128], ident[:, :], cst["mT" + sfx][:, :], False, True, [ident, cst["mT" + sfx]], [pD])
            P.mm(pD[:, 128:256], ones[:, :], w["gU"][:, :], True, True, [ones, w["gU"]], [pD])
            P.mm(pD[:, 256:258], w["gU"][:, :], ones[:, 0:2], True, True, [w["gU"], ones], [pD])
            yield
            P.act(w["Gt"][:, :], pD[:, 0:128], AF.Exp, [pD], [w["Gt"]])
            P.act(w["egb"][:, :], pD[:, 128:256], AF.Exp, [pD], [w["egb"]])
            P.act(w["egc"][:, :], pD[:, 256:257], AF.Exp, [pD], [w["egc"]])
            yield
            pK = bank()
            P.mm(pK[:, 0:128], knb[:, t0:t0 + 128], knb[:, t0:t0 + 128], True, True, [knb], [pK])
            P.mm(pK[:, 128:256], knb[:, t0:t0 + 128], qnb[:, t0:t0 + 128], True, True, [knb, qnb], [pK])
            P.tt("pool", w["Gs"][:, :], w["Gt"][:, :], cst["st" + sfx][:, :], ALU.mult, [w["Gt"], cst["st" + sfx]], [w["Gs"]])
            yield
            P.stt("dve", w["Mt"][:, :], pK[:, 0:128], bcol, w["Gs"][:, :], ALU.mult, ALU.mult, [pK, G, w["Gs"]], [w["Mt"]])
            P.tt("dve", w["AT"][:, :], pK[:, 128:256], w["Gt"][:, :], ALU.mult, [pK, w["Gt"]], [w["AT"]])
            yield
            pT_ = bank()
            P.mm(pT_[:, 0:128], w["Mt"][:, :], ident[:, :], True, True, [w["Mt"], ident], [pT_])
            yield
            P.copy("act", w["MtT"][:, :], pT_[:, 0:128], [pT_], [w["MtT"]])
            Q, QT = w["Q"], w["QT"]
            P.tt("pool", Q[:, :], w["Mt"][:, :], cst["bm16"][:, :], ALU.mult, [w["Mt"], cst["bm16"]], [Q])
            yield
            P.tt("dve", QT[:, :], w["MtT"][:, :], cst["bm16"][:, :], ALU.mult, [w["MtT"], cst["bm16"]], [QT])
            X, XT = w["Pm"], w["PmT"]
            P.tt("pool", X[:, :], ident[:, :], Q[:, :], ALU.subtract, [ident, Q], [X])
            P.tt("dve", XT[:, :], ident[:, :], QT[:, :], ALU.subtract, [ident, QT], [XT])
            yield
            nxt = [(w["Q"], w["QT"]), (w["Q2"], w["Q2T"])]
            for s in range(1, 4):
                pq = bank()
                P.mm(pq[:, 0:128], QT[:, :], Q[:, :], True, True, [QT, Q], [pq])
                P.mm(pq[:, 128:256], Q[:, :], QT[:, :], True, True, [QT, Q], [pq])
                yield
                Qn, QTn = nxt[s % 2]
                P.copy("act", Qn[:, :], pq[:, 0:128], [pq], [Qn])
                P.copy("act", QTn[:, :], pq[:, 128:256], [pq], [QTn])
                Q, QT = Qn, QTn
                yield
                pp = bank()
                P.mm(pp[:, 0:128], QT[:, :], X[:, :], True, True, [QT, X], [pp])
                P.mm(pp[:, 128:256], Q[:, :], XT[:, :], True, True, [Q, XT], [pp])
                yield
                P.tt("dve", X[:, :], X[:, :], pp[:, 0:128], ALU.add, [X, pp], [X])
                P.tt("dve", XT[:, :], XT[:, :], pp[:, 128:256], ALU.add, [XT, pp], [XT])
                yield
            E, ET = w["Q"], w["QT"]
            for li, em in enumerate(("e32", "e64", "e128")):
                lastl = li == 2
                P.tt("pool", E[:, :], w["Mt"][:, :], cst[em][:, :], ALU.mult, [w["Mt"], cst[em]], [E])
                P.tt("pool", ET[:, :], w["MtT"][:, :], cst[em][:, :], ALU.mult, [w["MtT"], cst[em]], [ET])
                yield
                py = bank()
                P.mm(py[:, 0:128], ET[:, :], X[:, :], True, True, [ET, X], [py])
                if not lastl:
                    P.mm(py[:, 128:256], X[:, :], ET[:, :], True, True, [X, ET], [py])
                yield
                P.copy("act", w["Y"][:, :], py[:, 0:128], [py], [w["Y"]])
                if not lastl:
                    P.copy("act", w["YT"][:, :], py[:, 128:256], [py], [w["YT"]])
                yield
                pz = bank()
                P.mm(pz[:, 0:128], XT[:, :], w["Y"][:, :], True, True, [XT, w["Y"]], [pz])
                if not lastl:
                    P.mm(pz[:, 128:256], w["Y"][:, :], XT[:, :], True, True, [w["Y"], XT], [pz])
                yield
                P.tt("dve", X[:, :], X[:, :], pz[:, 0:128], ALU.subtract, [X, pz], [X])
                if not lastl:
                    P.tt("dve", XT[:, :], XT[:, :], pz[:, 128:256], ALU.subtract, [XT, pz], [XT])
                yield
            P.ts("dve", w["dg"][:, :], ident[:, :], bcol, None, ALU.mult, None, [ident, G], [w["dg"]])
            yield
            pB = bank()
            P.mm(pB[:, 0:128], ones[:, :], w["dg"][:, :], True, True, [ones, w["dg"]], [pB])
            yield
            P.tt("dve", w["Xb"][:, :], w["Pm"][:, :], pB[:, 0:128], ALU.mult, [w["Pm"], pB], [w["Xb"]])
            P.act(w["kg"][:, :], ktok[:, n, :], AF.Identity, [ktok, w["egc"]], [w["kg"]], scale=w["egc"][:, 0:1])
            yield
            pU = bank()
            P.mm(pU[:, 0:128], w["Xb"][:, :], vtok[:, n, :], True, True, [w["Xb"], vtok], [pU])
            P.mm(pU[:, 128:256], w["kg"][:, :], w["Xb"][:, :], True, True, [w["kg"], w["Xb"]], [pU])
            yield
            P.copy("act", w["u"][:, :], pU[:, 0:128], [pU], [w["u"]])
            P.copy("act", w["wT"][:, :], pU[:, 128:256], [pU], [w["wT"]])
            P.tt("dve", w["qg"][:, :], qnb[:, t0:t0 + 128], w["egb"][:, :], ALU.mult, [qnb, w["egb"]], [w["qg"]])
            P.act(w["kd"][:, :], ktok[:, n, :], AF.Identity, [ktok, w["Gt"]], [w["kd"]], scale=w["Gt"][:, last:last + 1])
            yield

        def gen_h(hd, d, n, w, first_write):
            last = 127 if d == 0 else 0
            t0 = n * 128
            S, Sb = St[d]["S"], St[d]["Sb"]
            pW = bank()
            P.mm(pW[:, 0:128], w["wT"][:, :], Sb[:, :], True, True, [w["wT"], Sb], [pW])
            yield
            P.tt("dve", w["vnew"][:, :], w["u"][:, :], pW[:, 0:128], ALU.subtract, [w["u"], pW], [w["vnew"]])
            yield
            P.mm(pW[:, 128:256], Sb[:, :], w["qg"][:, :], True, False, [Sb, w["qg"]], [pW])
            P.mm(pW[:, 128:256], w["vnew"][:, :], w["AT"][:, :], False, True, [w["vnew"], w["AT"]], [pW])
            P.mm(pW[:, 256:384], w["kd"][:, :], w["vnew"][:, :], True, True, [w["kd"], w["vnew"]], [pW])
            yield
            if n not in first_write:
                first_write[n] = True
                P.copy("dve", oacc[:, t0:t0 + 128], pW[:, 128:256], [pW], [oacc])
            else:
                P.tt("dve", oacc[:, t0:t0 + 128], oacc[:, t0:t0 + 128], pW[:, 128:256], ALU.add, [oacc, pW], [oacc])
            P.stt("dve", S[:, :], S[:, :], w["egb"][:, last:last + 1], pW[:, 256:384], ALU.mult, ALU.add, [S, w["egb"], pW], [S])
            yield
            P.copy("act", Sb[:, :], S[:, :], [S], [Sb])
            yield

        for hd in heads:
            for ci, (comp, dst) in enumerate((("Aq", vn), ("Ak", kn), ("Av", vn))):
                P.dma("sp" if ci % 2 == 0 else "pool", xr[:, :], pT[FMI[comp], hd], reads=[pT_res[FMI[comp]][hd]], writes=[xr], owner=xr)
                cb = (ci * 4 + hd) * 4
                P.act(dst[:, :], xr[:, :], AF.Identity, [xr, cw], [dst], scale=cw[:, cb + 2:cb + 3])
                for (s, e) in segs:
                    P.stt("dve", dst[:, s + 2:e], xr[:, s:e - 2], cw[:, cb + 0:cb + 1], dst[:, s + 2:e], ALU.mult, ALU.add, [xr, cw, dst], [dst])
                    P.stt("dve", dst[:, s + 1:e], xr[:, s:e - 1], cw[:, cb + 1:cb + 2], dst[:, s + 1:e], ALU.mult, ALU.add, [xr, cw, dst], [dst])
                    P.stt("dve", dst[:, s:e - 1], xr[:, s + 1:e], cw[:, cb + 3:cb + 4], dst[:, s:e - 1], ALU.mult, ALU.add, [xr, cw, dst], [dst])
                P.act(dst[:, :], dst[:, :], AF.Silu, [dst], [dst])
                if comp != "Av":
                    for (t0, nt) in TT:
                        rmsnorm_fm(P, PS, nps[0], dst, dst, ones, (128.0 ** -0.5) if comp == "Aq" else 1.0, 1, tsq, tr, t0, nt)
                        nps[0] += 1
                if comp == "Aq":
                    P.copy("pool", qnb[:, :], vn[:, :], [vn], [qnb])
                if comp == "Ak":
                    P.copy("pool", knb[:, :], kn[:, :], [kn], [knb])
            P.dma("pool", xr[:, :], pT[FMI["Az"], hd], reads=[pT_res[FMI["Az"]][hd]], writes=[xr], owner=xr)
            for (src, dstt) in ((kn, ktok), (vn, vtok)):
                for g4 in range(0, 34, 4):
                    ps = bank()
                    nn = min(4, 34 - g4)
                    for k in range(nn):
                        n = g4 + k
                        P.mm(ps[:, k * 128:(k + 1) * 128], src[:, n * 128:(n + 1) * 128], ident[:, :], True, True, [src, ident], [ps])
                    P.copy("act", dstt[:, g4:g4 + nn, :], ps[:, 0:nn * 128].rearrange("p (a b) -> p a b", b=128), [ps], [dstt])
            for d in range(2):
                P.op("pool", lambda e: e.memset(St[d]["S"][:, :], 0.0), writes=[St[d]["S"]])
                P.op("pool", lambda e: e.memset(St[d]["Sb"][:, :], 0.0), writes=[St[d]["Sb"]])
            order = {0: [0, 1] + list(range(2, 34)), 1: [1, 0] + list(range(33, 1, -1))}
            first_write = {}
            for step0 in range(0, nsteps, 2):
                steps = [st_ for st_ in (step0, step0 + 1) if st_ < nsteps]
                _lockstep([gen_ag(hd, d, order[d][st_], W[k * 2 + d]) for k, st_ in enumerate(steps) for d in range(2)])
                for k, st_ in enumerate(steps):
                    _lockstep([gen_h(hd, d, order[d][st_], W[k * 2 + d], first_write) for d in range(2)])
            for (t0, nt) in TT:
                rmsnorm_fm(P, PS, nps[0], oacc, oacc, ones, gn[:, 0:1], 128, tsq, tr, t0, nt, gt=[gn])
                nps[0] += 1
            P.tt("dve", ob[:, :], oacc[:, :], xr[:, :], ALU.mult, [oacc, xr], [ob])
            P.dma("sp", oT[hd], ob[:, :], reads=[ob], writes=[oT_res[hd]], owner=ob)


import math

_IN_SPLITS = (1024,) * 4 + (8,) * 4 + (1024,) * 10
_HC = {}


def host_consts():
    if _HC:
        return _HC
    c = {}
    c["ident"] = np.eye(128, dtype=np.float32)
    c["ones"] = np.ones((128, 128), np.float32)
    blk = np.zeros((128, 128), np.float32)
    blk[0:64, 0:64] = 1.0
    blk[64:128, 64:128] = 1.0
    c["blk64"] = blk
    rot = np.zeros((128, 128), np.float32)
    for d in range(128):
        f = d % 32
        if f < 16:
            rot[d + 16, d] = -1.0
        else:
            rot[d - 16, d] = 1.0
    c["rot"] = rot
    n_freq = 16
    inv = (np.float32(10000.0) ** (-np.arange(n_freq, dtype=np.float32) / np.float32(n_freq))).astype(np.float32)
    t = np.arange(NLAT, dtype=np.int32)
    pos = np.stack([t // 64, t % 64], axis=-1).astype(np.float32)
    ang = (pos[:, :, None] * inv[None, None, :]).astype(np.float32)
    cs, sn = np.cos(ang).astype(np.float32), np.sin(ang).astype(np.float32)
    cos = np.ones((128, NT), np.float32)
    sin = np.zeros((128, NT), np.float32)
    for d in range(128):
        f = d % 64
        a = f // 32
        fi = f % 16
        cos[d, NCTX:] = cs[:, a, fi]
        sin[d, NCTX:] = sn[:, a, fi]
    c["cos"] = cos
    c["sin"] = sin
    pairs, idr, idc, masks = na_tables()
    c["namask"] = masks
    c["_pairs"] = pairs
    c["_idr"] = idr
    c["_idc"] = idc
    c.update(gdn_consts())
    _HC.update(c)
    return _HC


CONST_NAMES = ["ident", "ones", "blk64", "rot", "cos", "sin", "namask", "U_f", "U_b", "mT_f", "mT_b", "st_f", "st_b", "bm16", "e32", "e64", "e128"]


def core_inputs(inp, l, hh, xfull, mod_lat, mod_ctx, light=False):
    hc = host_consts()
    d = {}
    if not light:
        d["xin"] = np.ascontiguousarray(xfull)
        d["normg"] = np.ascontiguousarray(inp["norm_g"][l].reshape(32, 128).T)
        mv = np.stack([mod_lat[0:4096], mod_lat[4096:8192], mod_lat[8192:], mod_ctx[0:4096], mod_ctx[4096:8192], mod_ctx[8192:]])
        d["modv"] = np.ascontiguousarray(mv.reshape(6, 32, 128).transpose(2, 0, 1).reshape(128, 192))
    w = inp["w_in"][l]
    offs = np.cumsum([0] + list(_IN_SPLITS))
    cols = []
    for i in range(4):
        cols += list(range(offs[i] + hh * 512, offs[i] + hh * 512 + 512))
    for i in range(4, 8):
        cols += list(range(offs[i] + hh * 4, offs[i] + hh * 4 + 4))
    for i in range(8, 18):
        cols += list(range(offs[i] + hh * 512, offs[i] + hh * 512 + 512))
    d["w_in"] = np.ascontiguousarray(w[:, cols])
    hs = slice(hh * 4, hh * 4 + 4)
    cs = slice(hh * 512, hh * 512 + 512)
    d["lru_conv"] = np.ascontiguousarray(inp["lru_conv"][l][:, cs].reshape(4, 4, 128).transpose(2, 1, 0).reshape(128, 16))
    d["lru_conv_b"] = np.ascontiguousarray(inp["lru_conv_b"][l][cs].reshape(4, 128).T)
    d["lru_wg"] = np.ascontiguousarray(inp["lru_w_gate"][l][:, :, hs])
    d["lru_bg"] = np.ascontiguousarray(inp["lru_b_gate"][l][:, :, cs].reshape(2, 2, 4, 128).transpose(3, 0, 1, 2).reshape(128, 16))
    d["lru_lam"] = np.ascontiguousarray(inp["lru_lambda"][l][:, cs].reshape(2, 4, 128).transpose(2, 0, 1).reshape(128, 8))
    d["dq_g"] = np.ascontiguousarray(np.tile(inp["diff_q_norm"][l], 2).reshape(128, 1))
    d["dk_g"] = np.ascontiguousarray(np.tile(inp["diff_k_norm"][l], 2).reshape(128, 1))
    d["dlam"] = np.ascontiguousarray(inp["diff_lambda"][l].reshape(1, 256))
    d["dsub"] = np.ascontiguousarray(inp["diff_subln"][l].reshape(128, 1))
    d["nq_g"] = np.ascontiguousarray(inp["na_q_norm"][l].reshape(128, 1))
    d["nk_g"] = np.ascontiguousarray(inp["na_k_norm"][l].reshape(128, 1))
    rpb = inp["na_rpb"][l][hs]
    d["nab"] = np.ascontiguousarray(rpb[:, hc["_idr"], hc["_idc"]])
    gc = inp["gdn_conv"][l]
    gcc = np.stack([gc[:, ci * 1024 + hh * 512: ci * 1024 + hh * 512 + 512] for ci in range(3)])
    d["g_conv"] = np.ascontiguousarray(gcc.reshape(3, 4, 4, 128).transpose(3, 0, 2, 1).reshape(128, 48))
    d["g_alog"] = np.ascontiguousarray(inp["gdn_a_log"][l][:, hs].reshape(1, 8))
    d["g_dtb"] = np.ascontiguousarray(inp["gdn_dt_bias"][l][:, hs].reshape(1, 8))
    d["g_norm"] = np.ascontiguousarray(inp["gdn_norm_g"][l].reshape(128, 1))
    lam_init = 0.8 - 0.6 * math.exp(-0.3 * l)
    d["laminit"] = np.tile(np.array([[-lam_init, 1.0 - lam_init]], np.float32), (128, 1))
    if not light:
        for k in CONST_NAMES:
            d["c_" + k] = hc[k]
    return d


def build_la(phases=("norm", "inproj", "gdn", "na", "diff", "lru"), dump=False, comps=None):
    hc = host_consts()
    ncfg = hc["namask"].shape[0]
    P = Prog()
    PS = alloc_psum(P)
    EI = "ExternalInput"
    xin = P.dram("xin", [NT, 4096], F32, kind=EI)
    normg = P.dram("normg", [128, 32], F32, kind=EI)
    modv = P.dram("modv", [128, 192], F32, kind=EI)
    w_in = P.dram("w_in", [4096, NCOLS], F32, kind=EI)
    C = {}
    for k in CONST_NAMES:
        C[k] = P.dram("c_" + k, list(hc[k].shape), F32, kind=EI)
    lru_conv = P.dram("lru_conv", [128, 16], F32, kind=EI)
    lru_conv_b = P.dram("lru_conv_b", [128, 4], F32, kind=EI)
    lru_wg = P.dram("lru_wg", [2, 2, 4, 128, 128], F32, kind=EI)
    lru_bg = P.dram("lru_bg", [128, 16], F32, kind=EI)
    lru_lam = P.dram("lru_lam", [128, 8], F32, kind=EI)
    dq_g = P.dram("dq_g", [128, 1], F32, kind=EI)
    dk_g = P.dram("dk_g", [128, 1], F32, kind=EI)
    dlam = P.dram("dlam", [1, 256], F32, kind=EI)
    dsub = P.dram("dsub", [128, 1], F32, kind=EI)
    nq_g = P.dram("nq_g", [128, 1], F32, kind=EI)
    nk_g = P.dram("nk_g", [128, 1], F32, kind=EI)
    nab = P.dram("nab", [4, ncfg, 128, 640], F32, kind=EI)
    g_conv = P.dram("g_conv", [128, 48], F32, kind=EI)
    g_alog = P.dram("g_alog", [1, 8], F32, kind=EI)
    g_dtb = P.dram("g_dtb", [1, 8], F32, kind=EI)
    g_norm = P.dram("g_norm", [128, 1], F32, kind=EI)
    kd = "ExternalOutput" if dump else "Internal"
    hT = P.dram("hT", [32, 128, NT], BF16, kind=kd)
    hT_res = [P.res("hT%d" % i) for i in range(34)]
    pT = P.dram("pT", [12, 4, 128, NT], F32, kind=kd)
    pT_res = [[P.res("pT%d_%d" % (i, j)) for j in range(4)] for i in range(12)]
    pv = P.dram("pv", [2, NT, 512], BF16, kind=kd)
    pv_res = [P.res("pv%d" % i) for i in range(2)]
    psm = P.dram("psm", [NT, 16], F32, kind=kd)
    psm_res = P.res("psm")
    oT = P.dram("oT", [16, 128, NT], BF16, kind="ExternalOutput")
    oT_res = [P.res("oT%d" % i) for i in range(16)]
    laminit = P.dram("laminit", [128, 2], F32, kind=EI)
    if "norm" in phases:
        phase_norm(P, PS, xin, normg, modv, C["ident"], hT, hT_res)
    if "inproj" in phases:
        phase_inproj(P, PS, w_in, hT, hT_res, pT, pT_res, pv, pv_res, psm, psm_res, comps=comps)
    if "lru" in phases:
        phase_lru(P, PS, pT, pT_res, oT, oT_res, lru_conv, lru_conv_b, lru_wg, lru_bg, lru_lam)
    if "diff" in phases:
        phase_diff(P, PS, pT, pT_res, pv, pv_res, oT, oT_res, C, dq_g, dk_g, dlam, dsub, laminit)
    if "na" in phases:
        phase_na(P, PS, pT, pT_res, pv, pv_res, oT, oT_res, C, nq_g, nk_g, nab, hc["_pairs"], ncfg)
    if "gdn" in phases:
        phase_gdn(P, PS, pT, pT_res, psm, psm_res, oT, oT_res, C, g_conv, g_alog, g_dtb, g_norm)
    print("LA instructions", P.n_ins, "waits", P.n_wait, "dsems", P.ndsem)
    return P.finish()


import os

NL = 4
NCORE = 4


def phase_mod(P, PS, cs, w_mod, bmT, modv_d, gtrow_d):
    with P.phase():
        sT = P.tile("sT", [128, 32, 2], F32)
        sg = P.tile("sg", [128, 32, 2], F32)
        bm = P.tile("bm", [128, NL * 96], F32)
        P.dma("sp", sT[:, :, :], cs.rearrange("p (c r) -> p c r", r=2), writes=[sT], owner=sT)
        P.dma("sp", bm[:, :], bmT, writes=[bm], owner=bm)
        P.act(sg[:, :, :], sT[:, :, :], AF.Sigmoid, [sT], [sg])
        P.tt("dve", sT[:, :, :], sT[:, :, :], sg[:, :, :], ALU.mult, [sT, sg], [sT])
        wt = [P.tile("wm%d" % i, [128, 32, 128], F32) for i in range(3)]
        mT = [P.tile("mT%d" % i, [128, 96, 2], F32) for i in range(2)]
        mv = [P.tile("mvv%d" % i, [128, 192], F32) for i in range(2)]
        n = 0
        for l in range(NL):
            m = mT[l % 2]
            v = mv[l % 2]
            wv = w_mod[l].rearrange("(c p) n -> p c n", p=128)
            for j in range(96):
                w = wt[n % 3]
                ps = PS[n % 8]
                q = "sp" if n % 2 == 0 else "pool"
                n += 1
                P.dma(q, w[:, :, :], wv[:, :, j * 128:(j + 1) * 128], writes=[w], owner=w)
                for c in range(32):
                    P.mm(ps[:, 0:2], w[:, c, :], sT[:, c, :], c == 0, c == 31, [w, sT], [ps])
                P.ts("dve", m[:, j, :], ps[:, 0:2], bm[:, l * 96 + j:l * 96 + j + 1], None, ALU.add, None, [ps, bm], [m])
            P.copy("dve", v[:, 0:96], m[:, :, 0], [m], [v])
            P.copy("dve", v[:, 96:192], m[:, :, 1], [m], [v])
            P.dma("sp", modv_d[l], v[:, :], reads=[v], owner=v)
            for r in range(2):
                for g in range(4):
                    P.dma("sp", gtrow_d[l, r, g * 1024:(g + 1) * 1024].rearrange("(c p) -> p c", p=128),
                          v[:, r * 96 + 64 + g * 8: r * 96 + 64 + (g + 1) * 8], reads=[v], owner=v, allow_slow_non_contiguous=True)


def build_fused():
    hc = host_consts()
    ncfg = hc["namask"].shape[0]
    P = Prog()
    PS = alloc_psum(P)
    EI = "ExternalInput"
    xin = P.dram("xin", [NT, 4096], F32, kind=EI)
    cs = P.dram("cs", [128, 64], F32, kind=EI)
    w_mod = P.dram("w_mod", [NL, 4096, 12288], F32, kind=EI)
    bmT = P.dram("bmT", [128, NL * 96], F32, kind=EI)
    normg = P.dram("normg", [NL, 128, 32], F32, kind=EI)
    w_in = P.dram("w_in", [NL, 4096, 2 * NCOLS], F32, kind=EI)
    w_out = P.dram("w_out", [NL, 4096, 4096], F32, kind=EI)
    C = {k: P.dram("c_" + k, list(hc[k].shape), F32, kind=EI) for k in CONST_NAMES}
    shp = dict(lru_conv=[128, 16], lru_conv_b=[128, 4], lru_wg=[2, 2, 4, 128, 128], lru_bg=[128, 16], lru_lam=[128, 8],
               dq_g=[128, 1], dk_g=[128, 1], dlam=[1, 256], dsub=[128, 1], nq_g=[128, 1], nk_g=[128, 1],
               nab=[4, ncfg, 128, 640], g_conv=[128, 48], g_alog=[1, 8], g_dtb=[1, 8], g_norm=[128, 1], laminit=[128, 2])
    prm = {k: P.dram(k, [NL, 2] + v, F32, kind=EI) for k, v in shp.items()}
    xout = P.dram("xout", [NLAT, 4096], F32, kind="ExternalOutput")
    modv_d = P.dram("modv_d", [NL, 128, 192], F32)
    gtrow_d = P.dram("gtrow_d", [NL, 2, 4096], F32)
    hT = P.dram("hT", [32, 128, NT], BF16)
    hT_res = [P.res("hT%d" % i) for i in range(34)]
    pT = P.dram("pT", [12, 4, 128, NT], F32)
    pT_res = [[P.res("pT%d_%d" % (i, j)) for j in range(4)] for i in range(12)]
    pv = P.dram("pv", [2, NT, 512], BF16)
    pv_res = [P.res("pv%d" % i) for i in range(2)]
    psm = P.dram("psm", [NT, 16], F32)
    psm_res = P.res("psm")
    oT = P.dram("oT", [32, 128, NT], BF16)
    oT_res = [P.res("oT%d" % i) for i in range(32)]
    xb = [P.dram("xb%d" % i, [NT, 4096], F32) for i in range(2)]
    phase_mod(P, PS, cs, w_mod, bmT, modv_d, gtrow_d)
    for l in range(NL):
        xcur = xin if l == 0 else xb[l % 2]
        xnext = xb[(l + 1) % 2]
        phase_norm(P, PS, xcur, normg[l], modv_d[l], C["ident"], hT, hT_res)
        for hh in range(2):
            g = lambda k: prm[k][l, hh]
            o_ = oT[hh * 16:(hh + 1) * 16]
            r_ = oT_res[hh * 16:(hh + 1) * 16]
            phase_inproj(P, PS, w_in[l][:, hh * NCOLS:(hh + 1) * NCOLS], hT, hT_res, pT, pT_res, pv, pv_res, psm, psm_res)
            phase_lru(P, PS, pT, pT_res, o_, r_, g("lru_conv"), g("lru_conv_b"), g("lru_wg"), g("lru_bg"), g("lru_lam"))
            phase_diff(P, PS, pT, pT_res, pv, pv_res, o_, r_, C, g("dq_g"), g("dk_g"), g("dlam"), g("dsub"), g("laminit"))
            phase_na(P, PS, pT, pT_res, pv, pv_res, o_, r_, C, g("nq_g"), g("nk_g"), g("nab"), hc["_pairs"], ncfg)
            phase_gdn(P, PS, pT, pT_res, psm, psm_res, o_, r_, C, g("g_conv"), g("g_alog"), g("g_dtb"), g("g_norm"))
        if l < NL - 1:
            phase_outproj(P, PS, oT, oT_res, w_out[l], xcur, gtrow_d[l, 0:1, :], gtrow_d[l, 1:2, :], xnext, 2)
        else:
            phase_outproj(P, PS, oT, oT_res, w_out[l], xcur, gtrow_d[l, 0:1, :], gtrow_d[l, 1:2, :], None, 2,
                          xo_map=lambda ti: None if ti < 2 else xout[(ti - 2) * 128:(ti - 1) * 128, :])
    print("FUSED instructions", P.n_ins, "waits", P.n_wait, "dsems", P.ndsem, {k: v for k, v in P.cnt.items() if k in P.eng}, flush=True)
    return P.finish()


def fused_inputs(inp, b):
    hc = host_consts()
    d = {}
    d["xin"] = np.ascontiguousarray(np.concatenate([inp["ctx"][b], inp["x"][b]], axis=0))
    cs2 = np.stack([inp["c"][b], inp["c_ctx"]], axis=-1)
    d["cs"] = np.ascontiguousarray(cs2.reshape(32, 128, 2).transpose(1, 0, 2).reshape(128, 64))
    d["w_mod"] = np.ascontiguousarray(inp["w_mod"])
    d["bmT"] = np.ascontiguousarray(inp["b_mod"].reshape(NL, 96, 128).transpose(2, 0, 1).reshape(128, NL * 96))
    d["normg"] = np.ascontiguousarray(inp["norm_g"].reshape(NL, 32, 128).transpose(0, 2, 1))
    per = [[core_inputs(inp, l, hh, None, None, None, light=True) for hh in range(2)] for l in range(NL)]
    d["w_in"] = np.stack([np.concatenate([per[l][0]["w_in"], per[l][1]["w_in"]], axis=1) for l in range(NL)])
    rows = []
    for hh in range(2):
        for grp in range(4):
            for hl in range(4):
                r0 = grp * 1024 + (hh * 4 + hl) * 128
                rows += list(range(r0, r0 + 128))
    d["w_out"] = np.ascontiguousarray(inp["w_out"][:, rows, :])
    for k in ("lru_conv", "lru_conv_b", "lru_wg", "lru_bg", "lru_lam", "dq_g", "dk_g", "dlam", "dsub", "nq_g", "nk_g", "nab",
              "g_conv", "g_alog", "g_dtb", "g_norm", "laminit"):
        d[k] = np.ascontiguousarray(np.stack([np.stack([per[l][hh][k] for hh in range(2)]) for l in range(NL)]))
    for k in CONST_NAMES:
        d["c_" + k] = hc[k]
    return d


def kernel(**inputs):
    inp = {k: np.asarray(v) for k, v in inputs.items()}
    nc = build_fused()
    in_maps = [fused_inputs(inp, b) for b in range(NCORE)]
    res = run_bass_kernel_spmd(nc, in_maps, core_ids=list(range(NCORE)))
    return np.stack([np.asarray(res.results[b]["xout"]) for b in range(NCORE)]).astype(np.float32)
```

```python
import contextlib
import numpy as np
import concourse.bass as bass
import concourse.mybir as mybir
from concourse.bass_utils import run_bass_kernel_spmd

F32 = mybir.dt.float32
BF16 = mybir.dt.bfloat16
AF = mybir.ActivationFunctionType
ALU = mybir.AluOpType
AX = mybir.AxisListType

SELF_SYNC = ("dve", "act", "pool")


class Res:
    __slots__ = ("name", "w", "r", "dsem", "excl")

    def __init__(self, name):
        self.name = name
        self.excl = False
        self.w = {}
        self.r = {}
        self.dsem = None


class Tile:
    __slots__ = ("t", "res", "shape")

    def __init__(self, t, res, shape):
        self.t = t
        self.res = res
        self.shape = shape

    def __getitem__(self, idx):
        return self.t[idx]


class Prog:
    def __init__(self):
        self.nc = bass.Bass("TRN2", target_bir_lowering=False)
        nc = self.nc
        self.eng = dict(pe=nc.tensor, dve=nc.vector, act=nc.scalar, pool=nc.gpsimd, sp=nc.sync)
        self.st = contextlib.ExitStack()
        self.pst = None
        self.sem = {}
        self.cnt = {}
        self.seen = {k: {} for k in self.eng}
        for k in self.eng:
            self.sem[k] = self.st.enter_context(nc.semaphore("s_" + k))
            self.cnt[k] = 0
        self.ndsem = 0
        self.free_dsem = {}
        self.phase_dsem = []
        self.n_ins = 0
        self.n_wait = 0

    def dram(self, name, shape, dtype, kind="Internal"):
        return self.nc.dram_tensor(name, list(shape), dtype, kind=kind).ap()

    def tile(self, name, shape, dtype):
        st = self.pst if self.pst is not None else self.st
        self.n_tiles = getattr(self, "n_tiles", 0) + 1
        name = "%s_%d" % (name, self.n_tiles)
        t = st.enter_context(self.nc.sbuf_tensor(name, list(shape), dtype))
        return Tile(t, Res(name), shape)

    def psum(self, name, shape, dtype=F32):
        t = self.st.enter_context(self.nc.psum_tensor(name, list(shape), dtype))
        r = Res(name)
        r.excl = True
        return Tile(t, r, shape)

    def res(self, name):
        return Res(name)

    @contextlib.contextmanager
    def phase(self):
        assert self.pst is None
        self.pst = contextlib.ExitStack()
        self.phase_dsem = []
        try:
            yield
        finally:
            self.barrier()
            self.pst.close()
            self.pst = None
            for q_, k_ in self.phase_dsem:
                self.free_dsem.setdefault(q_, []).append(k_)
            self.phase_dsem = []

    def barrier(self):
        for e in self.eng:
            deps = {}
            for k, c in self.cnt.items():
                if c > 0 and k != e:
                    deps[k] = c
            self._wait(e, deps)

    @staticmethod
    def _r(x):
        return x.res if isinstance(x, Tile) else x

    def _deps(self, reads, writes, skip_waw_key=None, partial=False):
        deps = {}

        def add(k, c):
            if deps.get(k, 0) < c:
                deps[k] = c
        for r in reads:
            r = self._r(r)
            for k, c in r.w.items():
                add(k, c)
            if r.excl:
                for k, c in r.r.items():
                    add(k, c)
        for w in writes:
            w = self._r(w)
            if not partial:
                for k, c in w.w.items():
                    if skip_waw_key is not None and k == skip_waw_key and not w.r:
                        continue
                    add(k, c)
            for k, c in w.r.items():
                add(k, c)
        return deps

    def _wait(self, e, deps):
        seen = self.seen[e]
        for k, c in deps.items():
            if k == e and e not in SELF_SYNC:
                continue
            if seen.get(k, 0) >= c:
                continue
            self.eng[e].wait_ge(self.sem[k], c)
            self.n_wait += 1
            seen[k] = c

    def _mark(self, tok, reads, writes, partial=False):
        k, c = tok
        for r in reads:
            r = self._r(r)
            if r.r.get(k, 0) < c:
                r.r[k] = c
        for w in writes:
            w = self._r(w)
            if partial:
                w.w[k] = c
            else:
                w.w = {k: c}
                w.r = {}

    def op(self, e, fn, reads=(), writes=()):
        self._wait(e, self._deps(reads, writes))
        ins = fn(self.eng[e])
        self.cnt[e] += 1
        ins.then_inc(self.sem[e], 1)
        self._mark((e, self.cnt[e]), reads, writes)
        self.n_ins += 1
        return ins

    def dma(self, q, out, in_, reads=(), writes=(), owner=None, partial=False, **kw):
        owner = self._r(owner)
        if owner.dsem is None:
            owner.dsem = {}
        if q not in owner.dsem:
            fl = self.free_dsem.setdefault(q, [])
            if fl:
                key = fl.pop()
            else:
                key = "d%d" % self.ndsem
                self.ndsem += 1
                self.sem[key] = self.st.enter_context(self.nc.semaphore("s_" + key))
                self.cnt[key] = 0
            owner.dsem[q] = key
            if self.pst is not None:
                self.phase_dsem.append((q, key))
        key = owner.dsem[q]
        self._wait(q, self._deps(reads, writes, skip_waw_key=key, partial=partial))
        ins = self.eng[q].dma_start(out=out, in_=in_, **kw)
        self.cnt[key] += 16
        ins.then_inc(self.sem[key], 16)
        self._mark((key, self.cnt[key]), reads, writes, partial=partial)
        self.n_ins += 1
        return ins

    def mm(self, out, lhsT, rhs, start, stop, R, W):
        return self.op("pe", lambda e: e.matmul(out, lhsT=lhsT, rhs=rhs, start=start, stop=stop), reads=R, writes=W)

    def act(self, out, in_, func, R, W, **kw):
        return self.op("act", lambda e: e.activation(out=out, in_=in_, func=func, **kw), reads=R, writes=W)

    def tt(self, eng, out, in0, in1, op, R, W):
        return self.op(eng, lambda e: e.tensor_tensor(out=out, in0=in0, in1=in1, op=op), reads=R, writes=W)

    def ts(self, eng, out, in0, s1, s2, op0, op1, R, W):
        if s2 is None:
            return self.op(eng, lambda e: e.tensor_scalar(out=out, in0=in0, scalar1=s1, scalar2=None, op0=op0), reads=R, writes=W)
        return self.op(eng, lambda e: e.tensor_scalar(out=out, in0=in0, scalar1=s1, scalar2=s2, op0=op0, op1=op1), reads=R, writes=W)

    def stt(self, eng, out, in0, scalar, in1, op0, op1, R, W):
        return self.op(eng, lambda e: e.scalar_tensor_tensor(out=out, in0=in0, scalar=scalar, in1=in1, op0=op0, op1=op1), reads=R, writes=W)

    def copy(self, eng, out, in_, R, W):
        if eng == "act":
            return self.op("act", lambda e: e.activation(out=out, in_=in_, func=AF.Identity), reads=R, writes=W)
        return self.op(eng, lambda e: e.tensor_copy(out=out, in_=in_), reads=R, writes=W)

    def finish(self):
        deps = {k: c for k, c in self.cnt.items() if c > 0 and k != "sp"}
        self._wait("sp", deps)
        self.st.close()
        return self.nc


import ml_dtypes

NT = 4352
NCTX = 256
NLAT = 4096
EPS = 1e-6
TT = [(0, 256)] + [(256 + 512 * i, 512) for i in range(8)]
COMPS = ["Aq", "Ak", "Av", "Az", "As", "Bq", "Bk", "Bv", "Bz", "Cq", "Ck", "Cv", "Cz", "Dx", "Dz"]
CW = {c: (16 if c == "As" else 512) for c in COMPS}
COFF = {}
_o = 0
for _c in COMPS:
    COFF[_c] = _o
    _o += CW[_c]
NCOLS = _o
FM = ["Aq", "Ak", "Av", "Az", "Bq", "Bk", "Bz", "Cq", "Ck", "Cz", "Dx", "Dz"]
FMI = {c: i for i, c in enumerate(FM)}
TM = ["Bv", "Cv"]
TMI = {c: i for i, c in enumerate(TM)}


def alloc_psum(P):
    return [P.psum("psb%d" % i, [128, 512]) for i in range(8)]


def phase_norm(P, PS, xin, normg, modv, ident_d, hT, hT_res):
    with P.phase():
        ident = P.tile("ident", [128, 128], F32)
        P.dma("sp", ident[:, :], ident_d, writes=[ident], owner=ident)
        ng = P.tile("ng", [128, 32], F32)
        mv = P.tile("mv", [128, 6 * 32], F32)
        P.dma("sp", ng[:, :], normg, writes=[ng], owner=ng)
        P.dma("sp", mv[:, :], modv, writes=[mv], owner=mv)
        AB = P.tile("AB", [128, 4 * 32], F32)
        for j, (sci, shi) in enumerate(((4, 3), (1, 0))):
            P.ts("dve", AB[:, (2 * j) * 32:(2 * j + 1) * 32], mv[:, sci * 32:(sci + 1) * 32], 1.0, None, ALU.add, None, [mv], [AB])
            P.tt("dve", AB[:, (2 * j) * 32:(2 * j + 1) * 32], AB[:, (2 * j) * 32:(2 * j + 1) * 32], ng[:, :], ALU.mult, [AB, ng], [AB])
            P.copy("dve", AB[:, (2 * j + 1) * 32:(2 * j + 2) * 32], mv[:, shi * 32:(shi + 1) * 32], [mv], [AB])
        xt = [P.tile("xt%d" % i, [128, 4096], F32) for i in range(2)]
        sq = P.tile("sq", [128, 4096], BF16)
        ss = [P.tile("ss%d" % i, [128, 2], F32) for i in range(2)]
        dg = [P.tile("dg%d" % i, [128, 128], F32) for i in range(2)]
        ht = [P.tile("ht%d" % i, [128, 32, 128], BF16) for i in range(2)]
        for i in range(34):
            x = xt[i % 2]
            s = ss[i % 2]
            d = dg[i % 2]
            h = ht[i % 2]
            P.dma("sp" if i % 2 == 0 else "pool", x[:, :], xin[i * 128:(i + 1) * 128, :], writes=[x], owner=x)
            P.act(sq[:, :], x[:, :], AF.Square, [x], [sq])
            P.op("dve", lambda e: e.reduce_sum(out=s[:, 0:1], in_=sq[:, :], axis=AX.X), reads=[sq], writes=[s])
            P.act(s[:, 1:2], s[:, 0:1], AF.Sqrt, [s], [s], scale=1.0 / 4096, bias=EPS)
            P.op("dve", lambda e: e.reciprocal(out=s[:, 1:2], in_=s[:, 1:2]), reads=[s], writes=[s])
            P.ts("dve", d[:, :], ident[:, :], s[:, 1:2], None, ALU.mult, None, [ident, s], [d])
            j = 0 if i < 2 else 1
            for cg in range(8):
                ps = PS[cg % 8]
                for cc in range(4):
                    c = cg * 4 + cc
                    P.mm(ps[:, cc * 128:(cc + 1) * 128], x[:, c * 128:(c + 1) * 128], d[:, :], True, True, [x, d], [ps])
                for cc in range(4):
                    c = cg * 4 + cc
                    A = AB[:, (2 * j) * 32 + c:(2 * j) * 32 + c + 1]
                    Bv = AB[:, (2 * j + 1) * 32 + c:(2 * j + 1) * 32 + c + 1]
                    if cg % 2 == 0:
                        P.act(h[:, c, :], ps[:, cc * 128:(cc + 1) * 128], AF.Identity, [ps, AB], [h], scale=A, bias=Bv)
                    else:
                        P.ts("dve", h[:, c, :], ps[:, cc * 128:(cc + 1) * 128], A, Bv, ALU.mult, ALU.add, [ps, AB], [h])
            P.dma("pool" if i % 2 == 0 else "sp", hT[:, :, i * 128:(i + 1) * 128].rearrange("c p t -> p c t"), h[:, :, :],
                  reads=[h], writes=[hT_res[i]], owner=h)


def phase_inproj(P, PS, w_in, hT, hT_res, pT, pT_res, pv, pv_res, psm, psm_res, comps=None):
    wv = w_in.rearrange("(c p) n -> p c n", p=128)
    with P.phase():
        stg = [P.tile("stg%d" % i, [128, 8, 512], F32) for i in range(2)]
        wb = [P.tile("wb%d" % i, [128, 32, 512], BF16) for i in range(2)]
        hts = [P.tile("hts%d" % i, [128, 32, 512], BF16) for i in range(2)]
        ot = [P.tile("ot%d" % i, [128, 512], F32) for i in range(4)]
        ob = [P.tile("ob%d" % i, [128, 512], BF16) for i in range(2)]
        nst = 0
        nht = 0
        no = 0
        nob = 0
        nps = 0
        for ci, comp in enumerate(comps or COMPS):
            w = wb[ci % 2]
            cw = CW[comp]
            for g in range(4):
                s = stg[nst % 2]
                nst += 1
                P.dma("sp", s[:, :, 0:cw], wv[:, g * 8:(g + 1) * 8, COFF[comp]:COFF[comp] + cw], writes=[s], owner=s)
                P.copy("pool" if g % 2 == 0 else "dve", w[:, g * 8:(g + 1) * 8, 0:cw], s[:, :, 0:cw], [s], [w])
            for ti, (t0, nt) in enumerate(TT):
                h = hts[nht % 2]
                nht += 1
                P.dma("pool", h[:, :, 0:nt], hT[:, :, t0:t0 + nt].rearrange("c p t -> p c t"),
                      reads=[hT_res[t0 // 128 + k] for k in range(nt // 128)], writes=[h], owner=h)
                if comp in FMI:
                    for hd in range(4):
                        ps = PS[nps % 8]
                        nps += 1
                        for c in range(32):
                            P.mm(ps[:, 0:nt], w[:, c, hd * 128:(hd + 1) * 128], h[:, c, 0:nt], c == 0, c == 31, [w, h], [ps])
                        o = ot[no % 4]
                        no += 1
                        if comp.endswith("z"):
                            P.act(o[:, 0:nt], ps[:, 0:nt], AF.Silu, [ps], [o])
                        elif no % 2 == 0:
                            P.copy("act", o[:, 0:nt], ps[:, 0:nt], [ps], [o])
                        else:
                            P.copy("dve", o[:, 0:nt], ps[:, 0:nt], [ps], [o])
                        P.dma("sp", pT[FMI[comp], hd, :, t0:t0 + nt], o[:, 0:nt], reads=[o], writes=[pT_res[FMI[comp]][hd]],
                              owner=o, partial=True)
                else:
                    for sidx in range(nt // 128):
                        ps = PS[nps % 8]
                        nps += 1
                        for c in range(32):
                            P.mm(ps[:, 0:cw], h[:, c, sidx * 128:(sidx + 1) * 128], w[:, c, 0:cw], c == 0, c == 31, [w, h], [ps])
                        r0 = t0 + sidx * 128
                        if comp == "As":
                            o = ot[no % 4]
                            no += 1
                            P.copy("dve", o[:, 0:16], ps[:, 0:16], [ps], [o])
                            P.dma("sp", psm[r0:r0 + 128, :], o[:, 0:16], reads=[o], writes=[psm_res], owner=o, partial=True)
                        else:
                            o = ob[nob % 2]
                            nob += 1
                            P.copy("act" if nob % 2 else "dve", o[:, :], ps[:, :], [ps], [o])
                            P.dma("sp", pv[TMI[comp], r0:r0 + 128, :], o[:, :], reads=[o], writes=[pv_res[TMI[comp]]], owner=o, partial=True)


def phase_lru(P, PS, pT, pT_res, oT, oT_res, lru_conv, lru_conv_b, lru_wg, lru_bg, lru_lam):
    with P.phase():
        cw = P.tile("cw", [128, 16], F32)
        cb = P.tile("cb", [128, 4], F32)
        bg = P.tile("bg", [128, 16], F32)
        lam = P.tile("lam", [128, 8], F32)
        c1 = P.tile("c1", [128, 8], F32)
        P.dma("sp", cw[:, :], lru_conv, writes=[cw], owner=cw)
        P.dma("sp", cb[:, :], lru_conv_b, writes=[cb], owner=cb)
        P.dma("sp", bg[:, :], lru_bg, writes=[bg], owner=bg)
        P.dma("sp", lam[:, :], lru_lam, writes=[lam], owner=lam)
        P.act(c1[:, :], lam[:, :], AF.Exp, [lam], [c1], scale=-1.0)
        P.act(c1[:, :], c1[:, :], AF.Ln, [c1], [c1], bias=1.0)
        P.ts("dve", c1[:, :], c1[:, :], -8.0, None, ALU.mult, None, [c1], [c1])
        wgs = P.tile("wgs", [128, 128], F32)
        wgb = [P.tile("wgb%d" % i, [128, 128], BF16) for i in range(4)]
        x = P.tile("x", [128, NT], F32)
        z = P.tile("z", [128, NT], F32)
        u = P.tile("u", [128, NT], F32)
        ub = P.tile("ub", [128, NT], BF16)
        r = P.tile("r", [128, NT], F32)
        ig = P.tile("ig", [128, NT], F32)
        hs = P.tile("hs", [128, NT], F32)
        hsum = P.tile("hsum", [128, NT], F32)
        ob = P.tile("ob", [128, NT], BF16)
        segs = [(0, NCTX), (NCTX, NT)]
        nps = 0
        for hd in range(4):
            P.dma("sp", x[:, :], pT[FMI["Dx"], hd], reads=[pT_res[FMI["Dx"]][hd]], writes=[x], owner=x)
            P.dma("pool", z[:, :], pT[FMI["Dz"], hd], reads=[pT_res[FMI["Dz"]][hd]], writes=[z], owner=z)
            P.act(u[:, :], x[:, :], AF.Identity, [x, cw, cb], [u], scale=cw[:, hd * 4 + 2:hd * 4 + 3], bias=cb[:, hd:hd + 1])
            for (s, e) in segs:
                P.stt("dve", u[:, s + 2:e], x[:, s:e - 2], cw[:, hd * 4 + 0:hd * 4 + 1], u[:, s + 2:e], ALU.mult, ALU.add, [x, cw, u], [u])
                P.stt("dve", u[:, s + 1:e], x[:, s:e - 1], cw[:, hd * 4 + 1:hd * 4 + 2], u[:, s + 1:e], ALU.mult, ALU.add, [x, cw, u], [u])
                P.stt("dve", u[:, s:e - 1], x[:, s + 1:e], cw[:, hd * 4 + 3:hd * 4 + 4], u[:, s:e - 1], ALU.mult, ALU.add, [x, cw, u], [u])
            P.copy("pool", ub[:, :], u[:, :], [u], [ub])
            for d in range(2):
                for g in range(2):
                    P.dma("sp", wgs[:, :], lru_wg[d, g, hd], writes=[wgs], owner=wgs)
                    P.copy("dve", wgb[d * 2 + g][:, :], wgs[:, :], [wgs], [wgb[d * 2 + g]])
            for d in range(2):
                col = d * 4 + hd
                for g in range(2):
                    dst = r if g == 0 else ig
                    bcol = (d * 2 + g) * 4 + hd
                    for (t0, nt) in TT:
                        ps = PS[nps % 8]
                        nps += 1
                        P.mm(ps[:, 0:nt], wgb[d * 2 + g][:, :], ub[:, t0:t0 + nt], True, True, [wgb[d * 2 + g], ub], [ps])
                        P.act(dst[:, t0:t0 + nt], ps[:, 0:nt], AF.Sigmoid, [ps, bg], [dst], bias=bg[:, bcol:bcol + 1])
                P.act(r[:, :], r[:, :], AF.Exp, [r, c1], [r], scale=c1[:, col:col + 1])
                P.tt("dve", ig[:, :], ig[:, :], u[:, :], ALU.mult, [ig, u], [ig])
                P.tt("pool", hs[:, :], r[:, :], r[:, :], ALU.mult, [r], [hs])
                P.ts("dve", hs[:, :], hs[:, :], 1.0, None, ALU.min, None, [hs], [hs])
                P.act(hs[:, :], hs[:, :], AF.Sqrt, [hs], [hs], scale=-1.0, bias=1.0)
                P.tt("dve", ig[:, :], ig[:, :], hs[:, :], ALU.mult, [ig, hs], [ig])
                if d == 0:
                    P.op("dve", lambda e: e.tensor_tensor_scan(out=hsum[:, :], data0=r[:, :], data1=ig[:, :], initial=0.0, op0=ALU.mult, op1=ALU.add),
                         reads=[r, ig], writes=[hsum])
                else:
                    P.op("dve", lambda e: e.tensor_tensor_scan(out=hs[:, 0:NCTX][:, ::-1], data0=r[:, 0:NCTX][:, ::-1], data1=ig[:, 0:NCTX][:, ::-1],
                                                               initial=0.0, op0=ALU.mult, op1=ALU.add), reads=[r, ig], writes=[hs])
                    P.op("dve", lambda e: e.tensor_tensor_scan(out=hs[:, NCTX:NT][:, ::-1], data0=r[:, NCTX:NT][:, ::-1], data1=ig[:, NCTX:NT][:, ::-1],
                                                               initial=hs[:, 0:1], op0=ALU.mult, op1=ALU.add), reads=[r, ig, hs], writes=[hs])
                    P.tt("dve", hsum[:, :], hsum[:, :], hs[:, :], ALU.add, [hsum, hs], [hsum])
            P.tt("dve", ob[:, :], hsum[:, :], z[:, :], ALU.mult, [hsum, z], [ob])
            P.dma("sp", oT[12 + hd], ob[:, :], reads=[ob], writes=[oT_res[12 + hd]], owner=ob)


def phase_outproj(P, PS, oTf, oTf_res, w_out, xh, gtl, gtc, xo, nctx_tiles, xo_map=None):
    ntok = xh.shape[0]
    ntiles = ntok // 128
    wv = w_out.rearrange("(c p) n -> p c n", p=128)
    oreads = list(oTf_res) if isinstance(oTf_res, (list, tuple)) else [oTf_res]
    with P.phase():
        gt = [P.tile("gt%d" % i, [128, 4096], F32) for i in range(2)]
        P.dma("sp", gt[0][:, :], gtc.broadcast_to([128, 4096]), writes=[gt[0]], owner=gt[0])
        P.dma("sp", gt[1][:, :], gtl.broadcast_to([128, 4096]), writes=[gt[1]], owner=gt[1])
        stg = [P.tile("stg%d" % i, [128, 8, 512], F32) for i in range(2)]
        wb = [P.tile("wb%d" % i, [128, 32, 512], BF16) for i in range(2)]
        ot = [P.tile("ot%d" % i, [128, 32, 128], BF16) for i in range(3)]
        xt = [P.tile("xt%d" % i, [128, 512], F32) for i in range(3)]
        yt = [P.tile("yt%d" % i, [128, 512], F32) for i in range(3)]
        nst = 0
        n = 0
        for cb in range(8):
            w = wb[cb % 2]
            for g in range(4):
                s = stg[nst % 2]
                nst += 1
                P.dma("sp", s[:, :, :], wv[:, g * 8:(g + 1) * 8, cb * 512:(cb + 1) * 512], writes=[s], owner=s)
                P.copy("pool" if g % 2 == 0 else "act", w[:, g * 8:(g + 1) * 8, :], s[:, :, :], [s], [w])
            for ti in range(ntiles):
                dst = xo_map(ti) if xo_map is not None else xo[ti * 128:(ti + 1) * 128, :]
                if dst is None:
                    continue
                o = ot[n % 3]
                x = xt[n % 3]
                y = yt[n % 3]
                ps = PS[n % 8]
                n += 1
                P.dma("pool", o[:, :, :], oTf[:, :, ti * 128:(ti + 1) * 128].rearrange("c p t -> p c t"), reads=oreads, writes=[o], owner=o)
                P.dma("pool", x[:, :], xh[ti * 128:(ti + 1) * 128, cb * 512:(cb + 1) * 512], writes=[x], owner=x)
                for c in range(32):
                    P.mm(ps[:, :], o[:, c, :], w[:, c, :], c == 0, c == 31, [o, w], [ps])
                g_ = gt[0] if ti < nctx_tiles else gt[1]
                P.tt("dve", y[:, :], ps[:, :], g_[:, cb * 512:(cb + 1) * 512], ALU.mult, [ps, g_], [y])
                P.tt("dve", y[:, :], y[:, :], x[:, :], ALU.add, [y, x], [y])
                P.dma("sp", dst[:, cb * 512:(cb + 1) * 512], y[:, :], reads=[y], owner=y)


NEG = -30000.0
GRID_W = 64


def rmsnorm_fm(P, PS, nps, src, dst_f32, ones_t, gcol, ndim, tmp_sq, tmp_r, t0, nt, gt=()):
    ps = PS[nps % 8]
    P.act(tmp_sq[:, 0:nt], src[:, t0:t0 + nt], AF.Square, [src], [tmp_sq])
    P.mm(ps[:, 0:nt], ones_t[:, :], tmp_sq[:, 0:nt], True, True, [ones_t, tmp_sq], [ps])
    P.act(tmp_r[:, 0:nt], ps[:, 0:nt], AF.Sqrt, [ps], [tmp_r], scale=1.0 / ndim, bias=EPS)
    P.op("dve", lambda e: e.reciprocal(out=tmp_r[:, 0:nt], in_=tmp_r[:, 0:nt]), reads=[tmp_r], writes=[tmp_r])
    P.stt("dve", dst_f32[:, t0:t0 + nt], src[:, t0:t0 + nt], gcol, tmp_r[:, 0:nt], ALU.mult, ALU.mult, [src, tmp_r] + list(gt), [dst_f32])


def phase_diff(P, PS, pT, pT_res, pv, pv_res, oT, oT_res, C, dq_g, dk_g, dlam, dsub, laminit):
    with P.phase():
        blk = P.tile("blk", [128, 128], F32)
        onesf = P.tile("onesf", [128, 128], F32)
        onesb = P.tile("onesb", [128, 128], BF16)
        rot = P.tile("rot", [128, 128], F32)
        cos = P.tile("cos", [128, NT], F32)
        sin = P.tile("sin", [128, NT], F32)
        for t, nm in ((blk, "blk64"), (onesf, "ones"), (rot, "rot"), (cos, "cos"), (sin, "sin")):
            P.dma("sp", t[:, :], C[nm], writes=[t], owner=t)
        P.copy("dve", onesb[:, :], onesf[:, :], [onesf], [onesb])
        gq = P.tile("gq", [128, 1], F32)
        gk = P.tile("gk", [128, 1], F32)
        sub = P.tile("sub", [128, 1], F32)
        P.dma("sp", gq[:, :], dq_g, writes=[gq], owner=gq)
        P.dma("sp", gk[:, :], dk_g, writes=[gk], owner=gk)
        P.dma("sp", sub[:, :], dsub, writes=[sub], owner=sub)
        P.ts("dve", gq[:, :], gq[:, :], 0.125, None, ALU.mult, None, [gq], [gq])
        li = P.tile("li", [128, 2], F32)
        P.dma("sp", li[:, :], laminit, writes=[li], owner=li)
        P.ts("dve", sub[:, :], sub[:, :], li[:, 1:2], None, ALU.mult, None, [sub, li], [sub])
        lv = P.tile("lv", [128, 256], F32)
        lt = P.tile("lt", [128, 4], F32)
        P.dma("sp", lv[:, :], dlam.broadcast_to([128, 256]), writes=[lv], owner=lv)
        P.tt("dve", lv[:, 0:64], lv[:, 0:64], lv[:, 64:128], ALU.mult, [lv], [lv])
        P.tt("dve", lv[:, 128:192], lv[:, 128:192], lv[:, 192:256], ALU.mult, [lv], [lv])
        P.op("dve", lambda e: e.reduce_sum(out=lt[:, 0:1], in_=lv[:, 0:64], axis=AX.X), reads=[lv], writes=[lt])
        P.op("dve", lambda e: e.reduce_sum(out=lt[:, 1:2], in_=lv[:, 128:192], axis=AX.X), reads=[lv], writes=[lt])
        P.act(lt[:, 0:2], lt[:, 0:2], AF.Exp, [lt], [lt])
        P.tt("dve", lt[:, 2:3], lt[:, 1:2], lt[:, 0:1], ALU.subtract, [lt], [lt])
        P.ts("dve", lt[:, 3:4], lt[:, 2:3], li[:, 0:1], None, ALU.add, None, [lt, li], [lt])
        lamneg = lt[:, 3:4]

        xq = P.tile("xq", [128, NT], F32)
        xk = P.tile("xk", [128, NT], F32)
        z = P.tile("z", [128, NT], F32)
        qb = P.tile("qb", [128, NT], BF16)
        kb = P.tile("kb", [128, NT], BF16)
        V = P.tile("V", [128, 34, 128], BF16)
        tsq = P.tile("tsq", [128, 512], F32)
        tr = P.tile("tr", [128, 512], F32)
        t1 = P.tile("t1", [128, 512], F32)
        t2 = P.tile("t2", [128, 512], F32)
        pt = [P.tile("pt%d" % i, [128, 2, 512], BF16) for i in range(3)]
        r0 = P.tile("r0", [128, 512], F32)
        r1 = P.tile("r1", [128, 512], F32)
        o0 = P.tile("o0", [128, 512], F32)
        o1 = P.tile("o1", [128, 512], F32)
        obt = [P.tile("obt%d" % i, [128, 512], BF16) for i in range(2)]
        nps = 0
        npt = 0
        nob = 0
        for hd in range(4):
            P.dma("sp", xq[:, :], pT[FMI["Cq"], hd], reads=[pT_res[FMI["Cq"]][hd]], writes=[xq], owner=xq)
            P.dma("pool", xk[:, :], pT[FMI["Ck"], hd], reads=[pT_res[FMI["Ck"]][hd]], writes=[xk], owner=xk)
            P.dma("sp", z[:, :], pT[FMI["Cz"], hd], reads=[pT_res[FMI["Cz"]][hd]], writes=[z], owner=z)
            P.dma("pool", V[:, :, :], pv[TMI["Cv"], :, hd * 128:(hd + 1) * 128].rearrange("(c p) d -> p c d", p=128),
                  reads=[pv_res[TMI["Cv"]]], writes=[V], owner=V)
            for (src, g, dst) in ((xq, gq, qb), (xk, gk, kb)):
                for (t0, nt) in TT:
                    rmsnorm_fm(P, PS, nps, src, src, blk, g[:, 0:1], 64, tsq, tr, t0, nt, gt=[g])
                    nps += 1
                    ps = PS[nps % 8]
                    nps += 1
                    P.mm(ps[:, 0:nt], rot[:, :], src[:, t0:t0 + nt], True, True, [rot, src], [ps])
                    P.tt("pool", t1[:, 0:nt], src[:, t0:t0 + nt], cos[:, t0:t0 + nt], ALU.mult, [src, cos], [t1])
                    P.tt("dve", t2[:, 0:nt], ps[:, 0:nt], sin[:, t0:t0 + nt], ALU.mult, [ps, sin], [t2])
                    P.tt("dve", dst[:, t0:t0 + nt], t1[:, 0:nt], t2[:, 0:nt], ALU.add, [t1, t2], [dst])
            for (q0, nq) in TT:
                chunks = [0, 1] if q0 == 0 else list(range(34))
                O0, O1, S0, S1 = PS[4], PS[5], PS[6], PS[7]
                def scores(ci_):
                    kc_ = chunks[ci_]
                    a_ = (ci_ % 2) * 2
                    for m in range(2):
                        P.mm(PS[a_ + m][:, 0:nq], kb[64 * m:64 * m + 64, kc_ * 128:(kc_ + 1) * 128], qb[64 * m:64 * m + 64, q0:q0 + nq],
                             True, True, [kb, qb], [PS[a_ + m]])
                scores(0)
                for ci, kc in enumerate(chunks):
                    a = (ci % 2) * 2
                    p_ = pt[npt % 3]
                    npt += 1
                    if ci + 1 < len(chunks):
                        scores(ci + 1)
                    for m in range(2):
                        P.act(p_[:, m, 0:nq], PS[a + m][:, 0:nq], AF.Exp, [PS[a + m]], [p_])
                    st = ci == 0
                    sp_ = ci == len(chunks) - 1
                    P.mm(O0[:, 0:nq], V[:, kc, :], p_[:, 0, 0:nq], st, sp_, [V, p_], [O0])
                    P.mm(O1[:, 0:nq], V[:, kc, :], p_[:, 1, 0:nq], st, sp_, [V, p_], [O1])
                    P.mm(S0[:, 0:nq], onesb[:, :], p_[:, 0, 0:nq], st, sp_, [onesb, p_], [S0])
                    P.mm(S1[:, 0:nq], onesb[:, :], p_[:, 1, 0:nq], st, sp_, [onesb, p_], [S1])
                P.op("dve", lambda e: e.reciprocal(out=r0[:, 0:nq], in_=S0[:, 0:nq]), reads=[S0], writes=[r0])
                P.op("dve", lambda e: e.reciprocal(out=r1[:, 0:nq], in_=S1[:, 0:nq]), reads=[S1], writes=[r1])
                P.tt("dve", o0[:, 0:nq], O0[:, 0:nq], r0[:, 0:nq], ALU.mult, [O0, r0], [o0])
                P.stt("dve", o1[:, 0:nq], O1[:, 0:nq], lamneg, r1[:, 0:nq], ALU.mult, ALU.mult, [O1, r1, lt], [o1])
                P.tt("dve", o0[:, 0:nq], o0[:, 0:nq], o1[:, 0:nq], ALU.add, [o0, o1], [o0])
                rmsnorm_fm(P, PS, 0, o0, o0, onesf, sub[:, 0:1], 128, tsq, tr, 0, nq, gt=[sub])
                ob = obt[nob % 2]
                nob += 1
                P.tt("dve", ob[:, 0:nq], o0[:, 0:nq], z[:, q0:q0 + nq], ALU.mult, [o0, z], [ob])
                P.dma("sp", oT[8 + hd][:, q0:q0 + nq], ob[:, 0:nq], reads=[ob], writes=[oT_res[8 + hd]], owner=ob, partial=True)


def na_tables():
    rows = 64
    def r0(r):
        return min(max(r - 4, 0), rows - 8)
    def c0(c):
        return min(max(c - 8, 0), GRID_W - 16)
    cfgs = {}
    pairs = []
    idx_dr, idx_dc, masks = [], [], []
    for pr in range(32):
        lo = min(r0(2 * pr), r0(2 * pr + 1))
        hi = max(r0(2 * pr), r0(2 * pr + 1)) + 8
        chunks = list(range(lo // 2, (hi + 1) // 2))
        assert len(chunks) <= 5
        DR = np.zeros((128, 640), np.int64)
        DC = np.zeros((128, 640), np.int64)
        M = np.full((128, 640), NEG, np.float32)
        for j, cj in enumerate(chunks):
            for krl in range(2):
                for qrl in range(2):
                    kr = 2 * cj + krl
                    qr = 2 * pr + qrl
                    rv = r0(qr) <= kr < r0(qr) + 8
                    for kc in range(64):
                        for qc in range(64):
                            cv = c0(qc) <= kc < c0(qc) + 16
                            p = krl * 64 + kc
                            q = j * 128 + qrl * 64 + qc
                            if rv and cv:
                                DR[p, q] = kr - qr + 7
                                DC[p, q] = kc - qc + 15
                                M[p, q] = 0.0
        key = (tuple(c - pr for c in chunks), DR.tobytes(), M.tobytes())
        if key not in cfgs:
            cfgs[key] = len(idx_dr)
            idx_dr.append(DR)
            idx_dc.append(DC)
            masks.append(M)
        pairs.append((cfgs[key], chunks))
    return pairs, np.stack(idx_dr), np.stack(idx_dc), np.stack(masks)


def phase_na(P, PS, pT, pT_res, pv, pv_res, oT, oT_res, C, nq_g, nk_g, nab, pairs, ncfg):
    with P.phase():
        onesf = P.tile("onesf", [128, 128], F32)
        onesb = P.tile("onesb", [128, 128], BF16)
        P.dma("sp", onesf[:, :], C["ones"], writes=[onesf], owner=onesf)
        P.copy("dve", onesb[:, :], onesf[:, :], [onesf], [onesb])
        gq = P.tile("gq", [128, 1], F32)
        gk = P.tile("gk", [128, 1], F32)
        P.dma("sp", gq[:, :], nq_g, writes=[gq], owner=gq)
        P.dma("sp", gk[:, :], nk_g, writes=[gk], owner=gk)
        P.ts("dve", gq[:, :], gq[:, :], 128.0 ** -0.5, None, ALU.mult, None, [gq], [gq])
        msk = P.tile("msk", [128, ncfg, 640], F32)
        for c in range(ncfg):
            P.dma("sp", msk[:, c, :], C["namask"][c], writes=[msk], owner=msk)
        bias = P.tile("bias", [128, ncfg, 640], F32)
        xq = P.tile("xq", [128, NT], F32)
        xk = P.tile("xk", [128, NT], F32)
        z = P.tile("z", [128, NT], F32)
        qb = P.tile("qb", [128, NT], BF16)
        kb = P.tile("kb", [128, NT], BF16)
        V = P.tile("V", [128, 34, 128], BF16)
        tsq = P.tile("tsq", [128, 512], F32)
        tr = P.tile("tr", [128, 512], F32)
        sb = [P.tile("sb%d" % i, [128, 640], F32) for i in range(2)]
        pt = [P.tile("pt%d" % i, [128, 896], BF16) for i in range(2)]
        rr = [P.tile("rr%d" % i, [128, 256], F32) for i in range(2)]
        oo = [P.tile("oo%d" % i, [128, 256], F32) for i in range(2)]
        obuf = [P.tile("obuf%d" % i, [128, NT], BF16) for i in range(2)]
        nps = 0
        it = 0
        for hd in range(4):
            P.dma("sp", xq[:, :], pT[FMI["Bq"], hd], reads=[pT_res[FMI["Bq"]][hd]], writes=[xq], owner=xq)
            P.dma("pool", xk[:, :], pT[FMI["Bk"], hd], reads=[pT_res[FMI["Bk"]][hd]], writes=[xk], owner=xk)
            P.dma("sp", z[:, :], pT[FMI["Bz"], hd], reads=[pT_res[FMI["Bz"]][hd]], writes=[z], owner=z)
            P.dma("pool", V[:, :, :], pv[TMI["Bv"], :, hd * 128:(hd + 1) * 128].rearrange("(c p) d -> p c d", p=128),
                  reads=[pv_res[TMI["Bv"]]], writes=[V], owner=V)
            for c in range(ncfg):
                P.dma("pool", bias[:, c, :], nab[hd, c], writes=[bias], owner=bias)
            P.tt("dve", bias[:, :, :], bias[:, :, :], msk[:, :, :], ALU.add, [bias, msk], [bias])
            for (src, g, dst) in ((xq, gq, qb), (xk, gk, kb)):
                for (t0, nt) in TT:
                    rmsnorm_fm(P, PS, nps, src, src, onesf, g[:, 0:1], 128, tsq, tr, t0, nt, gt=[g])
                    nps += 1
                    P.copy("pool", dst[:, t0:t0 + nt], src[:, t0:t0 + nt], [src], [dst])
            ob = obuf[hd % 2]
            A, B_, Cb = PS[0], PS[1], PS[2]
            p_ = pt[it % 2]; r_ = rr[it % 2]; o_ = oo[it % 2]
            it += 1
            for j in range(2):
                P.mm(A[:, j * 256:(j + 1) * 256], kb[:, j * 128:(j + 1) * 128], qb[:, 0:256], True, True, [kb, qb], [A])
            P.act(p_[:, 0:512], A[:, 0:512], AF.Exp, [A], [p_])
            for j in range(2):
                P.mm(Cb[:, 0:256], V[:, j, :], p_[:, j * 256:(j + 1) * 256], j == 0, j == 1, [V, p_], [Cb])
            for j in range(2):
                P.mm(Cb[:, 256:512], onesb[:, :], p_[:, j * 256:(j + 1) * 256], j == 0, j == 1, [onesb, p_], [Cb])
            P.op("dve", lambda e: e.reciprocal(out=r_[:, 0:256], in_=Cb[:, 256:512]), reads=[Cb], writes=[r_])
            P.tt("dve", o_[:, 0:256], Cb[:, 0:256], r_[:, 0:256], ALU.mult, [Cb, r_], [o_])
            P.tt("dve", ob[:, 0:256], o_[:, 0:256], z[:, 0:256], ALU.mult, [o_, z], [ob])
            def gen_pair(pr, cfg, chunks, bi):
                s3 = bi * 3
                A, B_, Cb = PS[s3], PS[s3 + 1], PS[s3 + 2]
                p_ = pt[bi]; r_ = rr[bi]; o_ = oo[bi]; s_ = sb[bi]
                q0 = NCTX + pr * 128
                nl = len(chunks)
                for j, cj in enumerate(chunks):
                    k0 = NCTX + cj * 128
                    dstp, off = (A, j * 128) if j < 4 else (B_, 0)
                    P.mm(dstp[:, off:off + 128], kb[:, k0:k0 + 128], qb[:, q0:q0 + 128], True, True, [kb, qb], [dstp])
                for j in range(2):
                    P.mm(B_[:, 128 + j * 128:256 + j * 128], kb[:, j * 128:(j + 1) * 128], qb[:, q0:q0 + 128], True, True, [kb, qb], [B_])
                yield
                na = min(nl, 4) * 128
                P.tt("dve", s_[:, 0:na], A[:, 0:na], bias[:, cfg, 0:na], ALU.add, [A, bias], [s_])
                if nl == 5:
                    P.tt("dve", s_[:, 512:640], B_[:, 0:128], bias[:, cfg, 512:640], ALU.add, [B_, bias], [s_])
                yield
                P.act(p_[:, 0:nl * 128], s_[:, 0:nl * 128], AF.Exp, [s_], [p_])
                P.act(p_[:, 640:896], B_[:, 128:384], AF.Exp, [B_], [p_])
                yield
                srcs = [(2 + cj, j * 128) for j, cj in enumerate(chunks)] + [(0, 640), (1, 768)]
                for n_, (vc, off) in enumerate(srcs):
                    P.mm(Cb[:, 0:128], V[:, vc, :], p_[:, off:off + 128], n_ == 0, n_ == len(srcs) - 1, [V, p_], [Cb])
                for n_, (vc, off) in enumerate(srcs):
                    P.mm(Cb[:, 128:256], onesb[:, :], p_[:, off:off + 128], n_ == 0, n_ == len(srcs) - 1, [onesb, p_], [Cb])
                yield
                P.op("dve", lambda e: e.reciprocal(out=r_[:, 0:128], in_=Cb[:, 128:256]), reads=[Cb], writes=[r_])
                P.tt("dve", o_[:, 0:128], Cb[:, 0:128], r_[:, 0:128], ALU.mult, [Cb, r_], [o_])
                yield
                P.tt("pool", ob[:, q0:q0 + 128], o_[:, 0:128], z[:, q0:q0 + 128], ALU.mult, [o_, z], [ob])
                yield

            for pr0 in range(0, len(pairs), 2):
                gens_ = [gen_pair(pr, pairs[pr][0], pairs[pr][1], k_) for k_, pr in enumerate(range(pr0, min(pr0 + 2, len(pairs))))]
                alive_ = gens_
                while alive_:
                    nx_ = []
                    for g_ in alive_:
                        try:
                            next(g_)
                            nx_.append(g_)
                        except StopIteration:
                            pass
                    alive_ = nx_
            P.dma("sp", oT[4 + hd], ob[:, :], reads=[ob], writes=[oT_res[4 + hd]], owner=ob)


import os
STAGE = int(os.environ.get('GDN_STAGE', '99'))
LEVELS = int(os.environ.get('GDN_LEVELS', '6'))
DIRS = int(os.environ.get('GDN_DIRS', '2'))


def gdn_consts():
    c = {}
    t = np.arange(128)
    c["U_f"] = (t[:, None] <= t[None, :]).astype(np.float32)
    c["U_b"] = (t[:, None] >= t[None, :]).astype(np.float32)
    c["mT_f"] = np.where(t[:, None] <= t[None, :], 0.0, NEG).astype(np.float32)
    c["mT_b"] = np.where(t[:, None] >= t[None, :], 0.0, NEG).astype(np.float32)
    c["st_f"] = (t[:, None] < t[None, :]).astype(np.float32)
    c["st_b"] = (t[:, None] > t[None, :]).astype(np.float32)

    def bm(b):
        return ((t[:, None] // b) == (t[None, :] // b)).astype(np.float32)
    c["bm16"] = bm(16)
    c["e32"] = bm(32) - bm(16)
    c["e64"] = bm(64) - bm(32)
    c["e128"] = bm(128) - bm(64)
    return c


def _lockstep(gens):
    alive = list(gens)
    while alive:
        nxt_ = []
        for g_ in alive:
            try:
                next(g_)
                nxt_.append(g_)
            except StopIteration:
                pass
        alive = nxt_


def phase_gdn(P, PS, pT, pT_res, psm, psm_res, oT, oT_res, C, g_conv, g_alog, g_dtb, g_norm, heads=range(4), nsteps=34):
    nps = [0]

    def bank():
        nps[0] += 1
        return PS[nps[0] % 8]

    with P.phase():
        cst = {}
        for nm in ("ident", "ones", "U_f", "U_b", "mT_f", "mT_b", "st_f", "st_b", "bm16", "e32", "e64", "e128"):
            cst[nm] = P.tile("c_" + nm, [128, 128], F32)
            P.dma("sp", cst[nm][:, :], C[nm], writes=[cst[nm]], owner=cst[nm])
        ident, ones = cst["ident"], cst["ones"]
        negones = P.tile("negones", [128, 128], F32)
        P.ts("dve", negones[:, :], ones[:, :], -1.0, None, ALU.mult, None, [ones], [negones])
        cw = P.tile("cw", [128, 48], F32)
        P.dma("sp", cw[:, :], g_conv, writes=[cw], owner=cw)
        gn = P.tile("gn", [128, 1], F32)
        P.dma("sp", gn[:, :], g_norm, writes=[gn], owner=gn)
        al = P.tile("al", [128, 8], F32)
        dtb = P.tile("dtb", [128, 8], F32)
        P.dma("sp", al[:, :], g_alog.broadcast_to([128, 8]), writes=[al], owner=al)
        P.dma("sp", dtb[:, :], g_dtb.broadcast_to([128, 8]), writes=[dtb], owner=dtb)
        P.act(al[:, :], al[:, :], AF.Exp, [al], [al])
        P.ts("dve", al[:, :], al[:, :], -1.0, None, ALU.mult, None, [al], [al])
        G = P.tile("G", [128, 34, 16], F32)
        P.dma("sp", G[:, :, :], psm.rearrange("(c p) k -> p c k", p=128), reads=[psm_res], writes=[G], owner=G)
        for j in range(8):
            P.ts("dve", G[:, :, j], G[:, :, j], dtb[:, j:j + 1], None, ALU.add, None, [G, dtb], [G])
        P.act(G[:, :, 0:8], G[:, :, 0:8], AF.Exp, [G], [G])
        P.act(G[:, :, 0:8], G[:, :, 0:8], AF.Ln, [G], [G], bias=1.0)
        for j in range(8):
            P.ts("dve", G[:, :, j], G[:, :, j], al[:, j:j + 1], None, ALU.mult, None, [G, al], [G])
        P.act(G[:, :, 8:16], G[:, :, 8:16], AF.Sigmoid, [G], [G])

        xr = P.tile("xr", [128, NT], F32)
        kn = P.tile("kn", [128, NT], F32)
        vn = P.tile("vn", [128, NT], F32)
        qnb = P.tile("qnb", [128, NT], BF16)
        knb = P.tile("knb", [128, NT], BF16)
        ktok = P.tile("ktok", [128, 34, 128], F32)
        vtok = P.tile("vtok", [128, 34, 128], F32)
        oacc = P.tile("oacc", [128, NT], F32)
        tsq = P.tile("tsq", [128, 512], F32)
        tr = P.tile("tr", [128, 512], F32)
        ob = P.tile("ob", [128, NT], BF16)
        segs = [(0, NCTX), (NCTX, NT)]
        NSET = 4
        W = []
        for i in range(NSET):
            w = {}
            for nm in ("gU", "Gt", "egb", "Gs", "Mt", "MtT", "Q", "QT", "Q2", "Q2T", "Pm", "PmT", "Y", "YT", "dg", "Xb", "kg", "u"):
                w[nm] = P.tile("%s%d" % (nm, i), [128, 128], F32)
            for nm in ("AT", "qg", "kd", "wT", "vnew"):
                w[nm] = P.tile("%s%d" % (nm, i), [128, 128], BF16)
            w["egc"] = P.tile("egc%d" % i, [128, 1], F32)
            W.append(w)
        St = [{"S": P.tile("S%d" % d, [128, 128], F32), "Sb": P.tile("Sb%d" % d, [128, 128], BF16)} for d in range(2)]

        def gen_ag(hd, d, n, w):
            sfx = "_f" if d == 0 else "_b"
            last = 127 if d == 0 else 0
            t0 = n * 128
            gcol = G[:, n, d * 4 + hd:d * 4 + hd + 1]
            bcol = G[:, n, 8 + d * 4 + hd:8 + d * 4 + hd + 1]
            P.ts("dve", w["gU"][:, :], cst["U" + sfx][:, :], gcol, None, ALU.mult, None, [cst["U" + sfx], G], [w["gU"]])
            yield
            pD = bank()
            P.mm(pD[:, 0:128], ones[:, :], w["gU"][:, :], True, False, [ones, w["gU"]], [pD])
            P.mm(pD[:, 0:128], w["gU"][:, :], negones[:, :], False, False, [w["gU"], negones], [pD])
            P.mm(pD[:, 0:128], ident[:, :], cst["mT" + sfx][:, :], False, True, [ident, cst["mT" + sfx]], [pD])
            P.mm(pD[:, 128:256], ones[:, :], w["gU"][:, :], True, True, [ones, w["gU"]], [pD])
            P.mm(pD[:, 256:258], w["gU"][:, :], ones[:, 0:2], True, True, [w["gU"], ones], [pD])
            yield
            P.act(w["Gt"][:, :], pD[:, 0:128], AF.Exp, [pD], [w["Gt"]])
            P.act(w["egb"][:, :], pD[:, 128:256], AF.Exp, [pD], [w["egb"]])
            P.act(w["egc"][:, :], pD[:, 256:257], AF.Exp, [pD], [w["egc"]])
            yield
            pK = bank()
            P.mm(pK[:, 0:128], knb[:, t0:t0 + 128], knb[:, t0:t0 + 128], True, True, [knb], [pK])
            P.mm(pK[:, 128:256], knb[:, t0:t0 + 128], qnb[:, t0:t0 + 128], True, True, [knb, qnb], [pK])
            P.tt("pool", w["Gs"][:, :], w["Gt"][:, :], cst["st" + sfx][:, :], ALU.mult, [w["Gt"], cst["st" + sfx]], [w["Gs"]])
            yield
            P.stt("dve", w["Mt"][:, :], pK[:, 0:128], bcol, w["Gs"][:, :], ALU.mult, ALU.mult, [pK, G, w["Gs"]], [w["Mt"]])
            P.tt("dve", w["AT"][:, :], pK[:, 128:256], w["Gt"][:, :], ALU.mult, [pK, w["Gt"]], [w["AT"]])
            yield
            pT_ = bank()
            P.mm(pT_[:, 0:128], w["Mt"][:, :], ident[:, :], True, True, [w["Mt"], ident], [pT_])
            yield
            P.copy("act", w["MtT"][:, :], pT_[:, 0:128], [pT_], [w["MtT"]])
            Q, QT = w["Q"], w["QT"]
            P.tt("pool", Q[:, :], w["Mt"][:, :], cst["bm16"][:, :], ALU.mult, [w["Mt"], cst["bm16"]], [Q])
            yield
            P.tt("dve", QT[:, :], w["MtT"][:, :], cst["bm16"][:, :], ALU.mult, [w["MtT"], cst["bm16"]], [QT])
            X, XT = w["Pm"], w["PmT"]
            P.tt("pool", X[:, :], ident[:, :], Q[:, :], ALU.subtract, [ident, Q], [X])
            P.tt("dve", XT[:, :], ident[:, :], QT[:, :], ALU.subtract, [ident, QT], [XT])
            yield
            nxt = [(w["Q"], w["QT"]), (w["Q2"], w["Q2T"])]
            for s in range(1, 4):
                pq = bank()
                P.mm(pq[:, 0:128], QT[:, :], Q[:, :], True, True, [QT, Q], [pq])
                P.mm(pq[:, 128:256], Q[:, :], QT[:, :], True, True, [QT, Q], [pq])
                yield
                Qn, QTn = nxt[s % 2]
                P.copy("act", Qn[:, :], pq[:, 0:128], [pq], [Qn])
                P.copy("act", QTn[:, :], pq[:, 128:256], [pq], [QTn])
                Q, QT = Qn, QTn
                yield
                pp = bank()
                P.mm(pp[:, 0:128], QT[:, :], X[:, :], True, True, [QT, X], [pp])
                P.mm(pp[:, 128:256], Q[:, :], XT[:, :], True, True, [Q, XT], [pp])
                yield
                P.tt("dve", X[:, :], X[:, :], pp[:, 0:128], ALU.add, [X, pp], [X])
                P.tt("dve", XT[:, :], XT[:, :], pp[:, 128:256], ALU.add, [XT, pp], [XT])
                yield
            E, ET = w["Q"], w["QT"]
            for li, em in enumerate(("e32", "e64", "e128")):
                lastl = li == 2
                P.tt("pool", E[:, :], w["Mt"][:, :], cst[em][:, :], ALU.mult, [w["Mt"], cst[em]], [E])
                P.tt("pool", ET[:, :], w["MtT"][:, :], cst[em][:, :], ALU.mult, [w["MtT"], cst[em]], [ET])
                yield
                py = bank()
                P.mm(py[:, 0:128], ET[:, :], X[:, :], True, True, [ET, X], [py])
                if not lastl:
                    P.mm(py[:, 128:256], X[:, :], ET[:, :], True, True, [X, ET], [py])
                yield
                P.copy("act", w["Y"][:, :], py[:, 0:128], [py], [w["Y"]])
                if not lastl:
                    P.copy("act", w["YT"][:, :], py[:, 128:256], [py], [w["YT"]])
                yield
                pz = bank()
                P.mm(pz[:, 0:128], XT[:, :], w["Y"][:, :], True, True, [XT, w["Y"]], [pz])
                if not lastl:
                    P.mm(pz[:, 128:256], w["Y"][:, :], XT[:, :], True, True, [w["Y"], XT], [pz])
                yield
                P.tt("dve", X[:, :], X[:, :], pz[:, 0:128], ALU.subtract, [X, pz], [X])
                if not lastl:
                    P.tt("dve", XT[:, :], XT[:, :], pz[:, 128:256], ALU.subtract, [XT, pz], [XT])
                yield
            P.ts("dve", w["dg"][:, :], ident[:, :], bcol, None, ALU.mult, None, [ident, G], [w["dg"]])
            yield
            pB = bank()
            P.mm(pB[:, 0:128], ones[:, :], w["dg"][:, :], True, True, [ones, w["dg"]], [pB])
            yield
            P.tt("dve", w["Xb"][:, :], w["Pm"][:, :], pB[:, 0:128], ALU.mult, [w["Pm"], pB], [w["Xb"]])
            P.act(w["kg"][:, :], ktok[:, n, :], AF.Identity, [ktok, w["egc"]], [w["kg"]], scale=w["egc"][:, 0:1])
            yield
            pU = bank()
            P.mm(pU[:, 0:128], w["Xb"][:, :], vtok[:, n, :], True, True, [w["Xb"], vtok], [pU])
            P.mm(pU[:, 128:256], w["kg"][:, :], w["Xb"][:, :], True, True, [w["kg"], w["Xb"]], [pU])
            yield
            P.copy("act", w["u"][:, :], pU[:, 0:128], [pU], [w["u"]])
            P.copy("act", w["wT"][:, :], pU[:, 128:256], [pU], [w["wT"]])
            P.tt("dve", w["qg"][:, :], qnb[:, t0:t0 + 128], w["egb"][:, :], ALU.mult, [qnb, w["egb"]], [w["qg"]])
            P.act(w["kd"][:, :], ktok[:, n, :], AF.Identity, [ktok, w["Gt"]], [w["kd"]], scale=w["Gt"][:, last:last + 1])
            yield

        def gen_h(hd, d, n, w, first_write):
            last = 127 if d == 0 else 0
            t0 = n * 128
            S, Sb = St[d]["S"], St[d]["Sb"]
            pW = bank()
            P.mm(pW[:, 0:128], w["wT"][:, :], Sb[:, :], True, True, [w["wT"], Sb], [pW])
            yield
            P.tt("dve", w["vnew"][:, :], w["u"][:, :], pW[:, 0:128], ALU.subtract, [w["u"], pW], [w["vnew"]])
            yield
            P.mm(pW[:, 128:256], Sb[:, :], w["qg"][:, :], True, False, [Sb, w["qg"]], [pW])
            P.mm(pW[:, 128:256], w["vnew"][:, :], w["AT"][:, :], False, True, [w["vnew"], w["AT"]], [pW])
            P.mm(pW[:, 256:384], w["kd"][:, :], w["vnew"][:, :], True, True, [w["kd"], w["vnew"]], [pW])
            yield
            if n not in first_write:
                first_write[n] = True
                P.copy("dve", oacc[:, t0:t0 + 128], pW[:, 128:256], [pW], [oacc])
            else:
                P.tt("dve", oacc[:, t0:t0 + 128], oacc[:, t0:t0 + 128], pW[:, 128:256], ALU.add, [oacc, pW], [oacc])
            P.stt("dve", S[:, :], S[:, :], w["egb"][:, last:last + 1], pW[:, 256:384], ALU.mult, ALU.add, [S, w["egb"], pW], [S])
            yield
            P.copy("act", Sb[:, :], S[:, :], [S], [Sb])
            yield

        for hd in heads:
            for ci, (comp, dst) in enumerate((("Aq", vn), ("Ak", kn), ("Av", vn))):
                P.dma("sp" if ci % 2 == 0 else "pool", xr[:, :], pT[FMI[comp], hd], reads=[pT_res[FMI[comp]][hd]], writes=[xr], owner=xr)
                cb = (ci * 4 + hd) * 4
                P.act(dst[:, :], xr[:, :], AF.Identity, [xr, cw], [dst], scale=cw[:, cb + 2:cb + 3])
                for (s, e) in segs:
                    P.stt("dve", dst[:, s + 2:e], xr[:, s:e - 2], cw[:, cb + 0:cb + 1], dst[:, s + 2:e], ALU.mult, ALU.add, [xr, cw, dst], [dst])
                    P.stt("dve", dst[:, s + 1:e], xr[:, s:e - 1], cw[:, cb + 1:cb + 2], dst[:, s + 1:e], ALU.mult, ALU.add, [xr, cw, dst], [dst])
                    P.stt("dve", dst[:, s:e - 1], xr[:, s + 1:e], cw[:, cb + 3:cb + 4], dst[:, s:e - 1], ALU.mult, ALU.add, [xr, cw, dst], [dst])
                P.act(dst[:, :], dst[:, :], AF.Silu, [dst], [dst])
                if comp != "Av":
                    for (t0, nt) in TT:
                        rmsnorm_fm(P, PS, nps[0], dst, dst, ones, (128.0 ** -0.5) if comp == "Aq" else 1.0, 1, tsq, tr, t0, nt)
                        nps[0] += 1
                if comp == "Aq":
                    P.copy("pool", qnb[:, :], vn[:, :], [vn], [qnb])
                if comp == "Ak":
                    P.copy("pool", knb[:, :], kn[:, :], [kn], [knb])
            P.dma("pool", xr[:, :], pT[FMI["Az"], hd], reads=[pT_res[FMI["Az"]][hd]], writes=[xr], owner=xr)
            for (src, dstt) in ((kn, ktok), (vn, vtok)):
                for g4 in range(0, 34, 4):
                    ps = bank()
                    nn = min(4, 34 - g4)
                    for k in range(nn):
                        n = g4 + k
                        P.mm(ps[:, k * 128:(k + 1) * 128], src[:, n * 128:(n + 1) * 128], ident[:, :], True, True, [src, ident], [ps])
                    P.copy("act", dstt[:, g4:g4 + nn, :], ps[:, 0:nn * 128].rearrange("p (a b) -> p a b", b=128), [ps], [dstt])
            for d in range(2):
                P.op("pool", lambda e: e.memset(St[d]["S"][:, :], 0.0), writes=[St[d]["S"]])
                P.op("pool", lambda e: e.memset(St[d]["Sb"][:, :], 0.0), writes=[St[d]["Sb"]])
            order = {0: [0, 1] + list(range(2, 34)), 1: [1, 0] + list(range(33, 1, -1))}
            first_write = {}
            for step0 in range(0, nsteps, 2):
                steps = [st_ for st_ in (step0, step0 + 1) if st_ < nsteps]
                _lockstep([gen_ag(hd, d, order[d][st_], W[k * 2 + d]) for k, st_ in enumerate(steps) for d in range(2)])
                for k, st_ in enumerate(steps):
                    _lockstep([gen_h(hd, d, order[d][st_], W[k * 2 + d], first_write) for d in range(2)])
            for (t0, nt) in TT:
                rmsnorm_fm(P, PS, nps[0], oacc, oacc, ones, gn[:, 0:1], 128, tsq, tr, t0, nt, gt=[gn])
                nps[0] += 1
            P.tt("dve", ob[:, :], oacc[:, :], xr[:, :], ALU.mult, [oacc, xr], [ob])
            P.dma("sp", oT[hd], ob[:, :], reads=[ob], writes=[oT_res[hd]], owner=ob)


import math

_IN_SPLITS = (1024,) * 4 + (8,) * 4 + (1024,) * 10
_HC = {}


def host_consts():
    if _HC:
        return _HC
    c = {}
    c["ident"] = np.eye(128, dtype=np.float32)
    c["ones"] = np.ones((128, 128), np.float32)
    blk = np.zeros((128, 128), np.float32)
    blk[0:64, 0:64] = 1.0
    blk[64:128, 64:128] = 1.0
    c["blk64"] = blk
    rot = np.zeros((128, 128), np.float32)
    for d in range(128):
        f = d % 32
        if f < 16:
            rot[d + 16, d] = -1.0
        else:
            rot[d - 16, d] = 1.0
    c["rot"] = rot
    n_freq = 16
    inv = (np.float32(10000.0) ** (-np.arange(n_freq, dtype=np.float32) / np.float32(n_freq))).astype(np.float32)
    t = np.arange(NLAT, dtype=np.int32)
    pos = np.stack([t // 64, t % 64], axis=-1).astype(np.float32)
    ang = (pos[:, :, None] * inv[None, None, :]).astype(np.float32)
    cs, sn = np.cos(ang).astype(np.float32), np.sin(ang).astype(np.float32)
    cos = np.ones((128, NT), np.float32)
    sin = np.zeros((128, NT), np.float32)
    for d in range(128):
        f = d % 64
        a = f // 32
        fi = f % 16
        cos[d, NCTX:] = cs[:, a, fi]
        sin[d, NCTX:] = sn[:, a, fi]
    c["cos"] = cos
    c["sin"] = sin
    pairs, idr, idc, masks = na_tables()
    c["namask"] = masks
    c["_pairs"] = pairs
    c["_idr"] = idr
    c["_idc"] = idc
    c.update(gdn_consts())
    _HC.update(c)
    return _HC


CONST_NAMES = ["ident", "ones", "blk64", "rot", "cos", "sin", "namask", "U_f", "U_b", "mT_f", "mT_b", "st_f", "st_b", "bm16", "e32", "e64", "e128"]


def core_inputs(inp, l, hh, xfull, mod_lat, mod_ctx, light=False):
    hc = host_consts()
    d = {}
    if not light:
        d["xin"] = np.ascontiguousarray(xfull)
        d["normg"] = np.ascontiguousarray(inp["norm_g"][l].reshape(32, 128).T)
        mv = np.stack([mod_lat[0:4096], mod_lat[4096:8192], mod_lat[8192:], mod_ctx[0:4096], mod_ctx[4096:8192], mod_ctx[8192:]])
        d["modv"] = np.ascontiguousarray(mv.reshape(6, 32, 128).transpose(2, 0, 1).reshape(128, 192))
    w = inp["w_in"][l]
    offs = np.cumsum([0] + list(_IN_SPLITS))
    cols = []
    for i in range(4):
        cols += list(range(offs[i] + hh * 512, offs[i] + hh * 512 + 512))
    for i in range(4, 8):
        cols += list(range(offs[i] + hh * 4, offs[i] + hh * 4 + 4))
    for i in range(8, 18):
        cols += list(range(offs[i] + hh * 512, offs[i] + hh * 512 + 512))
    d["w_in"] = np.ascontiguousarray(w[:, cols])
    hs = slice(hh * 4, hh * 4 + 4)
    cs = slice(hh * 512, hh * 512 + 512)
    d["lru_conv"] = np.ascontiguousarray(inp["lru_conv"][l][:, cs].reshape(4, 4, 128).transpose(2, 1, 0).reshape(128, 16))
    d["lru_conv_b"] = np.ascontiguousarray(inp["lru_conv_b"][l][cs].reshape(4, 128).T)
    d["lru_wg"] = np.ascontiguousarray(inp["lru_w_gate"][l][:, :, hs])
    d["lru_bg"] = np.ascontiguousarray(inp["lru_b_gate"][l][:, :, cs].reshape(2, 2, 4, 128).transpose(3, 0, 1, 2).reshape(128, 16))
    d["lru_lam"] = np.ascontiguousarray(inp["lru_lambda"][l][:, cs].reshape(2, 4, 128).transpose(2, 0, 1).reshape(128, 8))
    d["dq_g"] = np.ascontiguousarray(np.tile(inp["diff_q_norm"][l], 2).reshape(128, 1))
    d["dk_g"] = np.ascontiguousarray(np.tile(inp["diff_k_norm"][l], 2).reshape(128, 1))
    d["dlam"] = np.ascontiguousarray(inp["diff_lambda"][l].reshape(1, 256))
    d["dsub"] = np.ascontiguousarray(inp["diff_subln"][l].reshape(128, 1))
    d["nq_g"] = np.ascontiguousarray(inp["na_q_norm"][l].reshape(128, 1))
    d["nk_g"] = np.ascontiguousarray(inp["na_k_norm"][l].reshape(128, 1))
    rpb = inp["na_rpb"][l][hs]
    d["nab"] = np.ascontiguousarray(rpb[:, hc["_idr"], hc["_idc"]])
    gc = inp["gdn_conv"][l]
    gcc = np.stack([gc[:, ci * 1024 + hh * 512: ci * 1024 + hh * 512 + 512] for ci in range(3)])
    d["g_conv"] = np.ascontiguousarray(gcc.reshape(3, 4, 4, 128).transpose(3, 0, 2, 1).reshape(128, 48))
    d["g_alog"] = np.ascontiguousarray(inp["gdn_a_log"][l][:, hs].reshape(1, 8))
    d["g_dtb"] = np.ascontiguousarray(inp["gdn_dt_bias"][l][:, hs].reshape(1, 8))
    d["g_norm"] = np.ascontiguousarray(inp["gdn_norm_g"][l].reshape(128, 1))
    lam_init = 0.8 - 0.6 * math.exp(-0.3 * l)
    d["laminit"] = np.tile(np.array([[-lam_init, 1.0 - lam_init]], np.float32), (128, 1))
    if not light:
        for k in CONST_NAMES:
            d["c_" + k] = hc[k]
    return d


def build_la(phases=("norm", "inproj", "gdn", "na", "diff", "lru"), dump=False, comps=None):
    hc = host_consts()
    ncfg = hc["namask"].shape[0]
    P = Prog()
    PS = alloc_psum(P)
    EI = "ExternalInput"
    xin = P.dram("xin", [NT, 4096], F32, kind=EI)
    normg = P.dram("normg", [128, 32], F32, kind=EI)
    modv = P.dram("modv", [128, 192], F32, kind=EI)
    w_in = P.dram("w_in", [4096, NCOLS], F32, kind=EI)
    C = {}
    for k in CONST_NAMES:
        C[k] = P.dram("c_" + k, list(hc[k].shape), F32, kind=EI)
    lru_conv = P.dram("lru_conv", [128, 16], F32, kind=EI)
    lru_conv_b = P.dram("lru_conv_b", [128, 4], F32, kind=EI)
    lru_wg = P.dram("lru_wg", [2, 2, 4, 128, 128], F32, kind=EI)
    lru_bg = P.dram("lru_bg", [128, 16], F32, kind=EI)
    lru_lam = P.dram("lru_lam", [128, 8], F32, kind=EI)
    dq_g = P.dram("dq_g", [128, 1], F32, kind=EI)
    dk_g = P.dram("dk_g", [128, 1], F32, kind=EI)
    dlam = P.dram("dlam", [1, 256], F32, kind=EI)
    dsub = P.dram("dsub", [128, 1], F32, kind=EI)
    nq_g = P.dram("nq_g", [128, 1], F32, kind=EI)
    nk_g = P.dram("nk_g", [128, 1], F32, kind=EI)
    nab = P.dram("nab", [4, ncfg, 128, 640], F32, kind=EI)
    g_conv = P.dram("g_conv", [128, 48], F32, kind=EI)
    g_alog = P.dram("g_alog", [1, 8], F32, kind=EI)
    g_dtb = P.dram("g_dtb", [1, 8], F32, kind=EI)
    g_norm = P.dram("g_norm", [128, 1], F32, kind=EI)
    kd = "ExternalOutput" if dump else "Internal"
    hT = P.dram("hT", [32, 128, NT], BF16, kind=kd)
    hT_res = [P.res("hT%d" % i) for i in range(34)]
    pT = P.dram("pT", [12, 4, 128, NT], F32, kind=kd)
    pT_res = [[P.res("pT%d_%d" % (i, j)) for j in range(4)] for i in range(12)]
    pv = P.dram("pv", [2, NT, 512], BF16, kind=kd)
    pv_res = [P.res("pv%d" % i) for i in range(2)]
    psm = P.dram("psm", [NT, 16], F32, kind=kd)
    psm_res = P.res("psm")
    oT = P.dram("oT", [16, 128, NT], BF16, kind="ExternalOutput")
    oT_res = [P.res("oT%d" % i) for i in range(16)]
    laminit = P.dram("laminit", [128, 2], F32, kind=EI)
    if "norm" in phases:
        phase_norm(P, PS, xin, normg, modv, C["ident"], hT, hT_res)
    if "inproj" in phases:
        phase_inproj(P, PS, w_in, hT, hT_res, pT, pT_res, pv, pv_res, psm, psm_res, comps=comps)
    if "lru" in phases:
        phase_lru(P, PS, pT, pT_res, oT, oT_res, lru_conv, lru_conv_b, lru_wg, lru_bg, lru_lam)
    if "diff" in phases:
        phase_diff(P, PS, pT, pT_res, pv, pv_res, oT, oT_res, C, dq_g, dk_g, dlam, dsub, laminit)
    if "na" in phases:
        phase_na(P, PS, pT, pT_res, pv, pv_res, oT, oT_res, C, nq_g, nk_g, nab, hc["_pairs"], ncfg)
    if "gdn" in phases:
        phase_gdn(P, PS, pT, pT_res, psm, psm_res, oT, oT_res, C, g_conv, g_alog, g_dtb, g_norm)
    print("LA instructions", P.n_ins, "waits", P.n_wait, "dsems", P.ndsem)
    return P.finish()


import os

NL = 4
MCOL = 1536
NHALF = 2176


def build_mod():
    P = Prog()
    csT = P.dram("csT", [4096, 5], F32, kind="ExternalInput")
    wm = P.dram("wm", [NL, 4096, MCOL], F32, kind="ExternalInput")
    bm = P.dram("bm", [NL, MCOL], F32, kind="ExternalInput")
    out = P.dram("mod", [NL, 5, MCOL], F32, kind="ExternalOutput")
    sT = P.tile("sT", [128, 32 * 5], F32)
    sg = P.tile("sg", [128, 32 * 5], F32)
    P.dma("sp", sT[:, :], csT.rearrange("(p c) r -> p (c r)", c=32), writes=[sT], owner=sT)
    P.act(sg[:, :], sT[:, :], AF.Sigmoid, [sT], [sg])
    P.tt("dve", sT[:, :], sT[:, :], sg[:, :], ALU.mult, [sT, sg], [sT])
    wt = [P.tile("w%d" % i, [128, 8 * MCOL], F32) for i in range(2)]
    ps = [P.psum("ps%d" % i, [128, 512]) for i in range(3)]
    bt = P.tile("bt", [5, NL * MCOL], F32)
    ot = [P.tile("ot%d" % i, [5, MCOL], F32) for i in range(2)]
    for l in range(NL):
        P.dma("sp", bt[0:5, l * MCOL:(l + 1) * MCOL], bm[l:l + 1, :].broadcast_to([5, MCOL]), writes=[bt], owner=bt)
    wv = wm.rearrange("l (p c) n -> l p c n", c=32)
    it = 0
    for l in range(NL):
        for g in range(4):
            w = wt[it % 2]
            q = "sp" if it % 2 == 0 else "pool"
            it += 1
            P.dma(q, w[:, :].rearrange("p (c n) -> p c n", c=8), wv[l, :, g * 8:(g + 1) * 8, :], writes=[w], owner=w)
            for cc in range(8):
                c = g * 8 + cc
                for j in range(3):
                    P.mm(ps[j][0:5, :], sT[:, c * 5:(c + 1) * 5], w[:, cc * MCOL + j * 512: cc * MCOL + (j + 1) * 512],
                         c == 0, c == 31, [sT, w], [ps[j]])
        o = ot[l % 2]
        for j in range(3):
            P.tt("dve", o[0:5, j * 512:(j + 1) * 512], ps[j][0:5, :], bt[0:5, l * MCOL + j * 512: l * MCOL + (j + 1) * 512], ALU.add,
                 [ps[j], bt], [o])
        P.dma("sp", out[l], o[0:5, :], reads=[o], owner=o)
    return P.finish()


def build_lb():
    P = Prog()
    PS = alloc_psum(P)
    EI = "ExternalInput"
    oTf = P.dram("oTf", [32, 128, NHALF], BF16, kind=EI)
    w_out = P.dram("w_out", [4096, 4096], F32, kind=EI)
    xh = P.dram("xh", [NHALF, 4096], F32, kind=EI)
    gtl = P.dram("gtl", [1, 4096], F32, kind=EI)
    gtc = P.dram("gtc", [1, 4096], F32, kind=EI)
    xo = P.dram("xo", [NHALF, 4096], F32, kind="ExternalOutput")
    phase_outproj(P, PS, oTf, P.res("oTf"), w_out, xh, gtl, gtc, xo, 1)
    print("LB instructions", P.n_ins, "waits", P.n_wait)
    return P.finish()


def _tok_idx(th):
    return np.concatenate([np.arange(th * 128, th * 128 + 128), NCTX + np.arange(th * 2048, th * 2048 + 2048)])


def kernel(**inputs):
    import ml_dtypes
    inp = {k: np.asarray(v) for k, v in inputs.items()}
    cores = list(range(8))
    cs = np.concatenate([inp["c"], inp["c_ctx"][None, :]], axis=0)
    csT = np.ascontiguousarray(cs.T)
    nc_mod = build_mod()
    in_maps = [{"csT": csT, "wm": np.ascontiguousarray(inp["w_mod"][:, :, i * MCOL:(i + 1) * MCOL]),
                "bm": np.ascontiguousarray(inp["b_mod"][:, i * MCOL:(i + 1) * MCOL])} for i in cores]
    res = run_bass_kernel_spmd(nc_mod, in_maps, core_ids=cores)
    mod = np.concatenate([r["mod"] for r in res.results], axis=2)
    x = [np.concatenate([inp["ctx"][b], inp["x"][b]], axis=0) for b in range(4)]
    nc_la = build_la()
    nc_lb = build_lb()
    for l in range(NL):
        in_maps = [core_inputs(inp, l, core % 2, x[core // 2], mod[l, core // 2], mod[l, 4]) for core in cores]
        res = run_bass_kernel_spmd(nc_la, in_maps, core_ids=cores)
        oT = [np.asarray(r["oT"]) for r in res.results]
        if os.environ.get("KDEBUG"):
            for ci_, o_ in enumerate(oT):
                of_ = o_.astype(np.float32)
                bad_ = [int((~np.isfinite(of_[s_])).sum()) for s_ in range(16)]
                print("KDEBUG layer", l, "core", ci_, "nonfinite per slot", bad_, "absmax", float(np.nanmax(np.abs(of_))), flush=True)
        del in_maps, res
        in_maps = []
        for core in cores:
            b, th = core // 2, core % 2
            idx = _tok_idx(th)
            oTf = np.empty((32, 128, NHALF), dtype=oT[0].dtype)
            for grp in range(4):
                for hh in range(2):
                    oTf[grp * 8 + hh * 4: grp * 8 + hh * 4 + 4] = oT[b * 2 + hh][grp * 4:grp * 4 + 4][:, :, idx]
            in_maps.append({"oTf": oTf, "w_out": np.ascontiguousarray(inp["w_out"][l]), "xh": np.ascontiguousarray(x[b][idx]),
                            "gtl": np.ascontiguousarray(mod[l, b, 8192:].reshape(1, 4096)),
                            "gtc": np.ascontiguousarray(mod[l, 4, 8192:].reshape(1, 4096))})
        res = run_bass_kernel_spmd(nc_lb, in_maps, core_ids=cores)
        xn = [np.empty((NT, 4096), np.float32) for _ in range(4)]
        for core in cores:
            b, th = core // 2, core % 2
            xn[b][_tok_idx(th)] = np.asarray(res.results[core]["xo"])
        x = xn
        del in_maps, res
    return np.stack([x[b][NCTX:] for b in range(4)]).astype(np.float32)
```
